# Optimizing a Trainium2 kernel written in Bass

```python
import jax, jax.numpy as jnp
from jax import lax
import numpy as np

D_MODEL = 1024
BATCH = 4
SEQ = 8192
DEPTH = 2

GRID_W = 64
CTX_LEN = 256
Q_BLOCK = 128
ROPE_THETA = 10000.0
EPS = 1e-6
N_MOD = 6
MLA_HEADS = 8
MLA_NOPE = 64
MLA_ROPE = 32
MLA_V = 64
MLA_Q_RANK = 384
MLA_KV_RANK = 256
POOL_WINDOWS = (2, 4, 8, 16)
POOL_WIDTH = 512
POOL_GROUP = POOL_WIDTH // len(POOL_WINDOWS)
MIX0_IN = MLA_Q_RANK + MLA_KV_RANK + MLA_ROPE + POOL_WIDTH
MIX0_OUT = MLA_HEADS * MLA_V + POOL_WIDTH
GQA_HEADS = 8
GQA_KV_HEADS = 2
GQA_HEAD_DIM = 128
GQA_GROUP = GQA_HEADS // GQA_KV_HEADS
GQA_Q_W = GQA_HEADS * GQA_HEAD_DIM
GQA_KV_W = GQA_KV_HEADS * GQA_HEAD_DIM
D_FF = 2816
CONV_W = 3
N_EVEN = (DEPTH + 1) // 2
N_ODD = DEPTH // 2

kernel_name = 'hybrid_mla_pool_gqa_convffn_dit'


def rms_norm(x, gain=None):
    xf = x.astype(jnp.float32)
    y = (xf * lax.rsqrt(jnp.mean(xf * xf, axis=-1, keepdims=True) + EPS)).astype(x.dtype)
    return y if gain is None else y * gain


def modulate(x, shift, scale):
    return rms_norm(x) * (1.0 + scale) + shift


def axial_rope_tables(n_tokens, rope_dim):
    rows = n_tokens // GRID_W
    row = jnp.repeat(jnp.arange(rows, dtype=jnp.float32), GRID_W)
    col = jnp.tile(jnp.arange(GRID_W, dtype=jnp.float32), rows)
    n_freq = rope_dim // 4
    freq = ROPE_THETA ** (-jnp.arange(n_freq, dtype=jnp.float32) / n_freq)
    ang = jnp.concatenate([row[:, None] * freq, col[:, None] * freq], axis=-1)
    return jnp.cos(ang), jnp.sin(ang)


def apply_rope(x, cos, sin):
    xr = x.reshape(x.shape[:-1] + (x.shape[-1] // 2, 2))
    x0, x1 = xr[..., 0], xr[..., 1]
    c = cos[None, :, None, :].astype(x.dtype)
    s = sin[None, :, None, :].astype(x.dtype)
    return jnp.stack([x0 * c - x1 * s, x0 * s + x1 * c], axis=-1).reshape(x.shape)


def attention(q, k, v):
    B, Nq, KV, G, Dh = q.shape
    blk = min(Q_BLOCK, Nq)
    qb = jnp.moveaxis(q.reshape(B, Nq // blk, blk, KV, G, Dh), 1, 0)
    scale = Dh ** -0.5

    def one_block(qi):
        s = jnp.einsum('bqhgd,bkhd->bhgqk', qi, k).astype(jnp.float32) * scale
        p = jax.nn.softmax(s, axis=-1).astype(v.dtype)
        return jnp.einsum('bhgqk,bkhd->bqhgd', p, v)

    o = lax.map(one_block, qb)
    return jnp.moveaxis(o, 0, 1).reshape(B, Nq, KV * G * v.shape[-1])


def multiscale_pool(p, w_pool, s_pool):
    B, T, C = p.shape
    cs = jnp.cumsum(p.astype(jnp.float32), axis=1)
    cs = jnp.concatenate([jnp.zeros((B, 1, C), jnp.float32), cs], axis=1)
    t = jnp.arange(T)
    outs = []
    for g, w in enumerate(POOL_WINDOWS):
        lo = jnp.clip(t - w // 2, 0, T)
        hi = jnp.clip(t - w // 2 + w, 0, T)
        csg = cs[:, :, g * POOL_GROUP:(g + 1) * POOL_GROUP]
        mean = (csg[:, hi] - csg[:, lo]) / (hi - lo).astype(jnp.float32)[None, :, None]
        d = mean.astype(p.dtype) - p[:, :, g * POOL_GROUP:(g + 1) * POOL_GROUP]
        outs.append(d @ w_pool[g])
    return jnp.concatenate(outs, axis=-1) * s_pool


def mla_q(cq, g_q, w_uq, cos, sin, rope):
    B, T, _ = cq.shape
    q = (rms_norm(cq, g_q) @ w_uq).reshape(B, T, MLA_HEADS, MLA_NOPE + MLA_ROPE)
    if rope:
        q = jnp.concatenate([q[..., :MLA_NOPE], apply_rope(q[..., MLA_NOPE:], cos, sin)], axis=-1)
    return q[:, :, :, None, :]


def mla_kv(ckv, kr, g_kv, w_uk, w_uv, cos, sin, rope):
    B, T, _ = ckv.shape
    ckv = rms_norm(ckv, g_kv)
    k_nope = (ckv @ w_uk).reshape(B, T, MLA_HEADS, MLA_NOPE)
    v = (ckv @ w_uv).reshape(B, T, MLA_HEADS, MLA_V)
    kr = kr[:, :, None, :]
    if rope:
        kr = apply_rope(kr, cos, sin)
    k = jnp.concatenate([k_nope, jnp.broadcast_to(kr, (B, T, MLA_HEADS, MLA_ROPE))], axis=-1)
    return k, v


def mla_pool_mixer(hc, hl, w_in, g_q, w_uq, g_kv, w_uk, w_uv, w_pool, s_pool, w_out, cos, sin, need_ctx):
    i_kv = MLA_Q_RANK
    i_kr = i_kv + MLA_KV_RANK
    i_p = i_kr + MLA_ROPE
    pl = hl @ w_in
    ql = mla_q(pl[..., :i_kv], g_q, w_uq, cos, sin, True)
    kl, vl = mla_kv(pl[..., i_kv:i_kr], pl[..., i_kr:i_p], g_kv, w_uk, w_uv, cos, sin, True)
    if need_ctx:
        pc = hc @ w_in
        qc = mla_q(pc[..., :i_kv], g_q, w_uq, cos, sin, False)
        kc, vc = mla_kv(pc[..., i_kv:i_kr], pc[..., i_kr:i_p], g_kv, w_uk, w_uv, cos, sin, False)
    else:
        pkv = hc @ w_in[:, i_kv:i_p]
        kc, vc = mla_kv(pkv[..., :MLA_KV_RANK], pkv[..., MLA_KV_RANK:], g_kv, w_uk, w_uv, cos, sin, False)
    al = attention(ql, jnp.concatenate([kc, kl], axis=1), jnp.concatenate([vc, vl], axis=1))
    ol = jnp.concatenate([al, multiscale_pool(pl[..., i_p:], w_pool, s_pool)], axis=-1) @ w_out
    oc = None
    if need_ctx:
        ac = attention(qc, kc, vc)
        oc = jnp.concatenate([ac, multiscale_pool(pc[..., i_p:], w_pool, s_pool)], axis=-1) @ w_out
    return oc, ol


def gqa_q(a, g_q, cos, sin, rope):
    B, T, _ = a.shape
    q = rms_norm(a.reshape(B, T, GQA_HEADS, GQA_HEAD_DIM), g_q)
    if rope:
        q = apply_rope(q, cos, sin)
    return q.reshape(B, T, GQA_KV_HEADS, GQA_GROUP, GQA_HEAD_DIM)


def gqa_kv(a, g_k, cos, sin, rope):
    B, T, _ = a.shape
    k = rms_norm(a[..., :GQA_KV_W].reshape(B, T, GQA_KV_HEADS, GQA_HEAD_DIM), g_k)
    v = a[..., GQA_KV_W:].reshape(B, T, GQA_KV_HEADS, GQA_HEAD_DIM)
    if rope:
        k = apply_rope(k, cos, sin)
    return k, v


def gqa_mixer(hc, hl, w_in, g_q, g_k, w_out, cos, sin, need_ctx):
    pl = hl @ w_in
    ql = gqa_q(pl[..., :GQA_Q_W], g_q, cos, sin, True)
    kl, vl = gqa_kv(pl[..., GQA_Q_W:], g_k, cos, sin, True)
    if need_ctx:
        pc = hc @ w_in
        qc = gqa_q(pc[..., :GQA_Q_W], g_q, cos, sin, False)
        kc, vc = gqa_kv(pc[..., GQA_Q_W:], g_k, cos, sin, False)
    else:
        kc, vc = gqa_kv(hc @ w_in[:, GQA_Q_W:], g_k, cos, sin, False)
    ol = attention(ql, jnp.concatenate([kc, kl], axis=1), jnp.concatenate([vc, vl], axis=1)) @ w_out
    oc = attention(qc, kc, vc) @ w_out if need_ctx else None
    return oc, ol


def conv_ffn(h, w_up, conv_w, conv_b, w_down):
    a = h @ w_up
    g, u = a[..., :D_FF], a[..., D_FF:]
    T = g.shape[1]
    half = CONV_W // 2
    gp = jnp.pad(g, ((0, 0), (half, half), (0, 0)))
    acc = conv_b + gp[:, 0:T] * conv_w[0]
    for j in range(1, CONV_W):
        acc = acc + gp[:, j:j + T] * conv_w[j]
    return (jax.nn.silu(acc) * u) @ w_down


def setup_inputs(seed: int = 0) -> dict:
    key = jax.random.key(seed)
    ks = iter(jax.random.split(key, 32))
    D = D_MODEL

    def normal(shape, scale=1.0):
        return jax.random.normal(next(ks), shape, jnp.float32) * scale

    def gain(shape):
        return 1.0 + 0.05 * normal(shape)

    return {
        'x': normal((BATCH, SEQ, D)),
        'c': normal((BATCH, D)),
        'ctx': normal((BATCH, CTX_LEN, D)),
        'c_ctx': normal((D,)),
        'w_mod': normal((DEPTH, D, N_MOD * D), D ** -0.5),
        'b_mod': normal((DEPTH, N_MOD * D), 0.02),
        'mix0_w_in': normal((N_EVEN, D, MIX0_IN), D ** -0.5),
        'mla_g_q': gain((N_EVEN, MLA_Q_RANK)),
        'mla_w_uq': normal((N_EVEN, MLA_Q_RANK, MLA_HEADS * (MLA_NOPE + MLA_ROPE)), MLA_Q_RANK ** -0.5),
        'mla_g_kv': gain((N_EVEN, MLA_KV_RANK)),
        'mla_w_uk': normal((N_EVEN, MLA_KV_RANK, MLA_HEADS * MLA_NOPE), MLA_KV_RANK ** -0.5),
        'mla_w_uv': normal((N_EVEN, MLA_KV_RANK, MLA_HEADS * MLA_V), MLA_KV_RANK ** -0.5),
        'pool_w': normal((N_EVEN, len(POOL_WINDOWS), POOL_GROUP, POOL_GROUP), POOL_GROUP ** -0.5),
        'pool_scale': gain((N_EVEN, POOL_WIDTH)),
        'mix0_w_out': normal((N_EVEN, MIX0_OUT, D), MIX0_OUT ** -0.5),
        'gqa_w_in': normal((N_ODD, D, GQA_Q_W + 2 * GQA_KV_W), D ** -0.5),
        'gqa_g_q': gain((N_ODD, GQA_HEAD_DIM)),
        'gqa_g_k': gain((N_ODD, GQA_HEAD_DIM)),
        'gqa_w_out': normal((N_ODD, GQA_Q_W, D), GQA_Q_W ** -0.5),
        'ffn_w_up': normal((DEPTH, D, 2 * D_FF), D ** -0.5),
        'ffn_conv_w': normal((DEPTH, CONV_W, D_FF), CONV_W ** -0.5),
        'ffn_conv_b': normal((DEPTH, D_FF), 0.02),
        'ffn_w_down': normal((DEPTH, D_FF, D), D_FF ** -0.5),
        'g_final': gain((D,)),
    }


def reference(x, c, ctx, c_ctx, w_mod, b_mod, mix0_w_in, mla_g_q, mla_w_uq, mla_g_kv, mla_w_uk, mla_w_uv,
              pool_w, pool_scale, mix0_w_out, gqa_w_in, gqa_g_q, gqa_g_k, gqa_w_out,
              ffn_w_up, ffn_conv_w, ffn_conv_b, ffn_w_down, g_final):
    B, N, D = x.shape
    cos_a, sin_a = axial_rope_tables(N, MLA_ROPE)
    cos_c, sin_c = axial_rope_tables(N, GQA_HEAD_DIM)
    xl, xc = x, ctx
    for i in range(DEPTH):
        last = i == DEPTH - 1
        j = i // 2
        ml = (jax.nn.silu(c) @ w_mod[i] + b_mod[i]).reshape(B, N_MOD, 1, D)
        mc = (jax.nn.silu(c_ctx) @ w_mod[i] + b_mod[i]).reshape(N_MOD, D)
        hl = modulate(xl, ml[:, 0], ml[:, 1])
        hc = modulate(xc, mc[0], mc[1])
        if i % 2 == 0:
            oc, ol = mla_pool_mixer(hc, hl, mix0_w_in[j], mla_g_q[j], mla_w_uq[j], mla_g_kv[j], mla_w_uk[j],
                                    mla_w_uv[j], pool_w[j], pool_scale[j], mix0_w_out[j], cos_a, sin_a, not last)
        else:
            oc, ol = gqa_mixer(hc, hl, gqa_w_in[j], gqa_g_q[j], gqa_g_k[j], gqa_w_out[j], cos_c, sin_c, not last)
        xl = xl + ml[:, 2] * ol
        xl = xl + ml[:, 5] * conv_ffn(modulate(xl, ml[:, 3], ml[:, 4]),
                                      ffn_w_up[i], ffn_conv_w[i], ffn_conv_b[i], ffn_w_down[i])
        if not last:
            xc = xc + mc[2] * oc
            xc = xc + mc[5] * conv_ffn(modulate(xc, mc[3], mc[4]),
                                       ffn_w_up[i], ffn_conv_w[i], ffn_conv_b[i], ffn_w_down[i])
    return rms_norm(xl, g_final)
```

```python
import contextlib
import numpy as np
import ml_dtypes
import concourse.bass as bass
import concourse.mybir as mybir
from concourse.bass_utils import run_bass_kernel_spmd

F32 = mybir.dt.float32
BF16 = mybir.dt.bfloat16
AF = mybir.ActivationFunctionType
ALU = mybir.AluOpType
AX = mybir.AxisListType
ALLK = "__all__"

D = 1024
S = 8192
CT = 256
NTOK = S + CT
NT = NTOK // 128
DFF = 2816
NFC = DFF // 128
EPS = 1e-6
HALF = S // 2
ET = HALF // 128 + 2
EN = ET * 128


class Buf:
    def __init__(self, t, name):
        self.t = t
        self.name = name
        self.st = {}

    def __getitem__(self, idx):
        return self.t[idx]


class Op:
    __slots__ = ("eng", "fn", "deps", "is_dma", "pos", "tok", "waits", "signal", "vc", "pre")

    def __init__(self, eng, fn, is_dma):
        self.eng = eng
        self.fn = fn
        self.deps = []
        self.is_dma = is_dma
        self.pos = -1
        self.tok = None
        self.waits = []
        self.signal = False
        self.vc = None
        self.pre = None


class Prog:
    ENGS = ("pe", "act", "dve", "pool", "sp")
    NDMA = 12

    def __init__(self, nc):
        self.nc = nc
        self.ops = []
        self.streams = {e: [] for e in self.ENGS}
        self.dma_count = {e: 0 for e in self.ENGS}
        self.dma_ops = {e: [] for e in self.ENGS}
        self.pending_dma = []

    def _deps(self, op, reads, writes):
        deps = set()

        def conflicts(buf, key):
            st = buf.st
            if key == ALLK:
                return list(st.values())
            out = []
            if key in st:
                out.append(st[key])
            if ALLK in st:
                out.append(st[ALLK])
            return out

        for (buf, key) in reads:
            for s in conflicts(buf, key):
                if s[0] is not None:
                    deps.add(s[0])
        for (buf, key) in writes:
            for s in conflicts(buf, key):
                if s[0] is not None:
                    deps.add(s[0])
                for r in s[1]:
                    deps.add(r)
        deps.discard(op)
        for (buf, key) in reads:
            s = buf.st.setdefault(key, [None, []])
            s[1].append(op)
        for (buf, key) in writes:
            if key == ALLK:
                buf.st.clear()
            buf.st[key] = [op, []]
        return deps

    def add(self, eng, fn, reads=(), writes=(), is_dma=False):
        op = Op(eng, fn, is_dma)
        deps = self._deps(op, list(reads), list(writes))
        if eng == "pe" and not is_dma:
            deps = {d for d in deps if not (d.eng == "pe" and not d.is_dma)}
        op.deps = sorted(deps, key=lambda o: o.pos)
        op.pos = len(self.ops)
        self.ops.append(op)
        self.streams[eng].append(op)
        if is_dma:
            i = self.dma_count[eng]
            self.dma_count[eng] += 1
            op.tok = (("dma", eng, i % self.NDMA), 16 * (i // self.NDMA + 1))
            if i >= self.NDMA:
                op.pre = self.dma_ops[eng][i - self.NDMA]
            self.dma_ops[eng].append(op)
            self.pending_dma.append(op)
        return op

    def barrier(self):
        lasts = [s[-1] for s in self.streams.values() if s]
        deps = lasts + self.pending_dma
        self.pending_dma = []
        for e in self.ENGS:
            op = self.add(e, lambda en: en.nop())
            op.deps = sorted(set(deps) - {op}, key=lambda o: o.pos)

    def dma(self, q, out, in_, reads=(), writes=()):
        return self.add(q, lambda e: e.dma_start(out=out, in_=in_), reads, writes, is_dma=True)

    def mm(self, out, lhsT, rhs, start, stop, reads=(), writes=()):
        return self.add("pe", lambda e: e.matmul(out, lhsT, rhs, start=start, stop=stop), reads, writes)

    def tr(self, out, in_, ident, reads=(), writes=()):
        return self.add("pe", lambda e: e.transpose(out, in_, ident), reads, writes)

    def act(self, out, in_, func, bias=None, scale=None, accum_out=None, reads=(), writes=()):
        kw = {}
        if bias is not None:
            kw["bias"] = bias
        if scale is not None:
            kw["scale"] = scale
        if accum_out is not None:
            kw["accum_out"] = accum_out
        return self.add("act", lambda e: e.activation(out, in_, func, **kw), reads, writes)

    def v(self, eng, name, *args, reads=(), writes=(), **kw):
        return self.add(eng, lambda e: getattr(e, name)(*args, **kw), reads, writes)

    def lower(self):
        known = {e: {} for e in self.ENGS}
        for op in self.ops:
            kn = known[op.eng]
            deps = list(op.deps)
            if op.pre is not None:
                deps.append(op.pre)
            for d in deps:
                key = d.tok[0] if d.is_dma else ("eng", d.eng)
                val = d.tok[1] if d.is_dma else d.pos
                if kn.get(key, -1) >= val:
                    continue
                op.waits.append(d)
                d.signal = True
                for k2, v2 in d.vc.items():
                    if kn.get(k2, -1) < v2:
                        kn[k2] = v2
                kn[key] = max(kn.get(key, -1), val)
            vc = dict(kn)
            if op.is_dma:
                vc[op.tok[0]] = max(vc.get(op.tok[0], -1), op.tok[1])
            else:
                vc[("eng", op.eng)] = op.pos
            op.vc = vc
        cnt = {e: 0 for e in self.ENGS}
        for op in self.ops:
            if op.is_dma:
                continue
            if op.signal:
                cnt[op.eng] += 1
                op.tok = (("eng", op.eng), cnt[op.eng])
        for op in self.ops:
            op.vc = None

    def emit(self, stack):
        nc = self.nc
        self.lower()
        sems = {}

        def getsem(key):
            if key not in sems:
                sems[key] = stack.enter_context(nc.semaphore("s_" + "_".join(str(x) for x in key)))
            return sems[key]

        for op in self.ops:
            if op.is_dma or op.signal:
                getsem(op.tok[0])
        block = stack.enter_context(nc.Block())

        def run_stream(ename, eobj):
            for op in self.streams[ename]:
                for d in op.waits:
                    eobj.wait_ge(getsem(d.tok[0]), d.tok[1])
                ins = op.fn(eobj)
                if op.is_dma:
                    ins.then_inc(getsem(op.tok[0]), 16)
                elif op.signal:
                    ins.then_inc(getsem(op.tok[0]), 1)

        @block.tensor
        def _(e):
            run_stream("pe", e)

        @block.scalar
        def _(e):
            run_stream("act", e)

        @block.vector
        def _(e):
            run_stream("dve", e)

        @block.gpsimd
        def _(e):
            run_stream("pool", e)

        @block.sync
        def _(e):
            run_stream("sp", e)


class KB:
    def __init__(self, mode, dbg=False):
        self.mode = mode
        self.dbg = dbg
        self.nc = bass.Bass("TRN2", target_bir_lowering=False)
        self.P = Prog(self.nc)
        self.din = {}
        self.dout = {}
        self.rr = 0

    def inp(self, name, shape, dt=F32):
        t = self.nc.dram_tensor(name, list(shape), dt, kind="ExternalInput").ap()
        self.din[name] = t
        return t

    def outp(self, name, shape, dt=F32):
        t = self.nc.dram_tensor(name, list(shape), dt, kind="ExternalOutput").ap()
        self.dout[name] = t
        return t

    def scratch(self, name, shape, dt=F32, external=None):
        if external == "in":
            return self.inp(name, shape, dt)
        if external == "out" or self.dbg:
            return self.outp(name, shape, dt)
        return self.nc.dram_tensor(name, list(shape), dt).ap()

    def sb(self, st, name, shape, dt):
        self.rr += 1
        name = f"{name}_{self.rr}"
        return Buf(st.enter_context(self.nc.sbuf_tensor(name, list(shape), dt)), name)


def build(mode="full", dbg=False, stop_after=None):
    kb = KB(mode, dbg)
    nc, P = kb.nc, kb.P
    do0 = mode in ("full", "l0")
    do1 = mode in ("full", "l1")
    A = ALLK

    ident_d = kb.inp("ident", [128, 128], BF16)
    cc_d = kb.inp("cc", [128, 8, 2])
    wmod_d = kb.inp("w_mod", [2, D, 6 * D])
    bmod_d = kb.inp("b_mod", [2, 6 * D])
    bmodT_d = kb.inp("b_modT", [2, 128, 48])
    wup_d = kb.inp("ffn_w_up", [2, D, 2 * DFF])
    wdn_d = kb.inp("ffn_w_down", [2, DFF, D])
    convw_d = kb.inp("conv_wT", [2, 128, 3, NFC])
    convb_d = kb.inp("conv_bT", [2, 128, NFC])
    if do0:
        x_d = kb.inp("x", [S, D])
        ctx_d = kb.inp("ctx", [CT, D])
        win0_d = kb.inp("mix0_w_in", [D, 1184])
        wuq_d = kb.inp("w_uq", [384, 768])
        wuqs_d = kb.inp("w_uq_sw", [384, 768])
        gq0_d = kb.inp("g_q0T", [128, 3])
        gkv0_d = kb.inp("g_kv0T", [128, 2])
        wuk_d = kb.inp("w_uk", [256, 512])
        wuv_d = kb.inp("w_uv", [256, 512])
        poolw_d = kb.inp("pool_w", [4, 128, 128])
        pools_d = kb.inp("pool_sT", [128, 4])
        wout0_d = kb.inp("mix0_w_out", [D, D])
        ropeA_tok_d = kb.inp("ropeA_tok", [NTOK, 32])
        ropeA_c_d = kb.inp("ropeA_c", [32, NTOK])
        ropeA_s_d = kb.inp("ropeA_s", [32, NTOK])
        invc_d = kb.inp("invcnt", [4, S])
        invcc_d = kb.inp("invcnt_ctx", [4, CT])
    if do1:
        win1_d = kb.inp("gqa_w_in", [D, 1536])
        gq1_d = kb.inp("gqa_g_q", [128])
        gk1_d = kb.inp("gqa_g_k", [128])
        wout1_d = kb.inp("gqa_w_out", [D, D])
        gfin_d = kb.inp("g_final", [D])
        ropeC_k_d = kb.inp("ropeC_k", [NTOK, 128])
        ropeC_q_d = kb.inp("ropeC_q", [EN, 128])
        m01_d = kb.inp("m01", [128, 2])
        mAB_d = kb.inp("mAB", [128, 2])
        out_d = kb.outp("out", [HALF, D])

    if do0:
        cqnT_d = kb.scratch("cqnT", [3, 128, NTOK], BF16)
        pT_d = kb.scratch("pT", [4, 128, S + 16])
        pTc_d = kb.scratch("pTc", [4, 128, CT + 16])
        aoT_d = kb.scratch("aoT", [8, 128, NTOK], BF16)
        xs1_d = kb.scratch("xs1", [NTOK, D])
    ext = "out" if mode == "l0" else ("in" if mode == "l1" else None)
    xs2_d = kb.scratch("xs2", [S + 256, D], external=ext)
    xcs2_d = kb.scratch("xcs2", [CT, D], external=ext)
    xmid_d = kb.scratch("xmid", [NTOK, D])
    if do1:
        xs3_d = kb.scratch("xs3", [EN, D])
        xE_d = kb.scratch("xE", [EN, D])
        aoT1_d = kb.scratch("aoT1", [8, 128, EN], BF16)

    with contextlib.ExitStack() as top:
        def finish_build():
            P.barrier()
            P.emit(top)
            return kb

        ident = kb.sb(top, "ident_sb", [128, 128], BF16)
        ones_f = kb.sb(top, "ones_f", [128, 128], F32)
        ones_b = kb.sb(top, "ones_b", [128, 128], BF16)
        cc = kb.sb(top, "cc_sb", [128, 8, 2], F32)
        sc2 = kb.sb(top, "sc2", [128, 8, 2], F32)
        modv = kb.sb(top, "modv", [128, 6, 8, 2], F32)
        G = {}
        psA = Buf(top.enter_context(nc.psum_tensor("psA", [128, 2048], F32)), "psA")
        psB = Buf(top.enter_context(nc.psum_tensor("psB", [128, 2048], F32)), "psB")

        def bank(i, n=1):
            assert (i % 4) + n <= 4
            b = psA if i < 4 else psB
            o = (i % 4) * 512
            return b[:, o:o + 512 * n], [(b, i + q) for q in range(n)]

        def bank_bf(i, n=1):
            ap, keys = bank(i, n)
            return ap.bitcast(BF16), keys

        P.dma("sp", ident[:], ident_d, writes=[(ident, A)])
        P.dma("sp", cc[:], cc_d, writes=[(cc, A)])
        P.v("pool", "memset", ones_f[:], 1.0, writes=[(ones_f, A)])
        P.v("pool", "memset", ones_b[:], 1.0, writes=[(ones_b, A)])
        P.act(sc2[:], cc[:], AF.Silu, reads=[(cc, A)], writes=[(sc2, A)])

        cast_rr = [0]

        def wload(stg_bufs, dst_ap, src_ap, n, dst_buf, dst_key, scale_ap=None, scale_reads=()):
            i = cast_rr[0]
            cast_rr[0] += 1
            stg = stg_bufs[i % len(stg_bufs)]
            P.dma("sp", stg[:, 0:n], src_ap, writes=[(stg, A)])
            if scale_ap is not None:
                P.v("dve", "tensor_scalar", dst_ap, stg[:, 0:n], scale_ap, None, ALU.mult,
                    reads=[(stg, A)] + list(scale_reads), writes=[(dst_buf, dst_key)])
            else:
                eng = "pool" if i % 2 == 0 else "dve"
                P.v(eng, "tensor_copy", dst_ap, stg[:, 0:n], reads=[(stg, A)], writes=[(dst_buf, dst_key)])

        def mod_phase(l, need_ctx_gates):
            with contextlib.ExitStack() as st:
                wm = [kb.sb(st, f"wm{i}", [128, 8, D], F32) for i in range(2)]
                bmT = kb.sb(st, "bmT", [128, 48], F32)
                bmb = [kb.sb(st, f"bmb{i}", [128, D], F32) for i in range(2)]
                scb = kb.sb(st, "scb", [128, 8, 2, 128], F32)
                P.dma("sp", bmT[:], bmodT_d[l], writes=[(bmT, A)])
                for k in range(8):
                    for w in range(2):
                        P.v("dve", "tensor_copy", scb[:, k, w, :], sc2[:, k, w:w + 1].to_broadcast([128, 128]),
                            reads=[(sc2, A)], writes=[(scb, (k, w))])
                for mi, m in enumerate((0, 1, 3, 4, 2, 5)):
                    wb = wm[mi % 2]
                    P.dma("sp", wb[:], wmod_d[l, :, m * D:(m + 1) * D].rearrange("(k p) n -> p k n", p=128),
                          writes=[(wb, A)])
                    if m in (0, 1, 3, 4):
                        pv, pk = bank(mi % 2)
                        pv3 = pv[:, 0:16].rearrange("p (j w) -> p j w", w=2)
                        for j in range(8):
                            for k in range(8):
                                P.mm(pv3[:, j, :], wb[:, k, j * 128:(j + 1) * 128], sc2[:, k, :], start=(k == 0),
                                     stop=(k == 7), reads=[(wb, A), (sc2, A)], writes=pk)
                        P.v("dve", "tensor_tensor", modv[:, m, :, :], pv3,
                            bmT[:, m * 8:(m + 1) * 8].unsqueeze(2).to_broadcast([128, 8, 2]), ALU.add,
                            reads=pk + [(bmT, A)], writes=[(modv, m)])
                        if m in (1, 4):
                            P.v("dve", "tensor_scalar_add", modv[:, m, :, :], modv[:, m, :, :], 1.0,
                                reads=[(modv, m)], writes=[(modv, m)])
                    else:
                        gi = 0 if m == 2 else 1
                        bb = bmb[gi]
                        P.dma("sp", bb[:], bmod_d[l, m * D:(m + 1) * D].partition_broadcast(128), writes=[(bb, A)])
                        for w in range(2 if need_ctx_gates else 1):
                            pv, pk = bank(2 + 2 * w if w == 0 else 4, 2)
                            for hf in range(2):
                                for k in range(8):
                                    P.mm(pv[:, hf * 512:(hf + 1) * 512], scb[:, k, w, :],
                                         wb[:, k, hf * 512:(hf + 1) * 512], start=(k == 0), stop=(k == 7),
                                         reads=[(scb, (k, w)), (wb, A)], writes=[pk[hf]])
                            P.v("dve", "tensor_tensor", G["gates"][:, w, gi, :], pv, bb[:], ALU.add,
                                reads=pk + [(bb, A)], writes=[(G["gates"], (w, gi))])
            P.barrier()

        def rms_prep(xt, tiles, mean, rstd, xn, junk, dim_scale=1.0 / 32.0):
            nt_ = len(tiles)
            for (j, p_) in tiles:
                P.act(junk[0:p_, :], xt[0:p_, j, :], AF.Square, scale=dim_scale, accum_out=mean[0:p_, j:j + 1],
                      reads=[(xt, j)], writes=[(junk, A), (mean, j)])
            jmax = max(j for j, _ in tiles) + 1
            P.act(rstd[:, 0:jmax], mean[:, 0:jmax], AF.Ln, bias=EPS, reads=[(mean, j) for j, _ in tiles],
                  writes=[(rstd, A)])
            P.act(rstd[:, 0:jmax], rstd[:, 0:jmax], AF.Exp, scale=-0.5, reads=[(rstd, A)], writes=[(rstd, A)])
            for (j, p_) in tiles:
                P.act(xn[0:p_, j, :], xt[0:p_, j, :], AF.Copy, scale=rstd[0:p_, j:j + 1],
                      reads=[(xt, j), (rstd, A)], writes=[(xn, j)])

        def trans_mod(xn, trans, evac, hT, which, mset, pbase, halves=(0, 1), act_main=False):
            m_shift, m_scale = (0, 1) if mset == 0 else (3, 4)
            for half in halves:
                pv, pk = bank_bf(pbase, 2)
                pv3 = pv.rearrange("p (k t) -> p k t", k=4)
                for (j, p0, np_, dc) in trans:
                    for kk in range(4):
                        k = half * 4 + kk
                        P.tr(pv3[:, kk, dc:dc + np_], xn[p0:p0 + np_, j, k * 128:(k + 1) * 128],
                             ident[p0:p0 + np_, p0:p0 + np_], reads=[(xn, j), (ident, A)], writes=pk)
                for kk in range(4):
                    k = half * 4 + kk
                    for (sc_, dc_, nc_) in evac:
                        if act_main and nc_ >= 64:
                            P.act(hT[:, k, dc_:dc_ + nc_], pv3[:, kk, sc_:sc_ + nc_], AF.Identity,
                                  bias=modv[:, m_shift, k, which:which + 1], scale=modv[:, m_scale, k, which:which + 1],
                                  reads=pk + [(modv, m_scale), (modv, m_shift)], writes=[(hT, A)])
                        else:
                            P.v("dve", "tensor_scalar", hT[:, k, dc_:dc_ + nc_], pv3[:, kk, sc_:sc_ + nc_],
                                modv[:, m_scale, k, which:which + 1], modv[:, m_shift, k, which:which + 1], ALU.mult,
                                ALU.add, reads=pk + [(modv, m_scale), (modv, m_shift)], writes=[(hT, A)])

        def ffn_phase(l, windows, final=False):
            for half in range(2):
                ffn_pass(l, windows, final, half)

        def ffn_pass(l, windows, final, half):
            NH = NFC // 2
            HW = NH * 128
            last = (half == 1)
            with contextlib.ExitStack() as st:
                stg = [kb.sb(st, f"fstg{i}", [128, HW], F32) for i in range(2)]
                wup = kb.sb(st, "wup", [128, 8, 2 * HW], BF16)
                wdn = kb.sb(st, "wdn", [128, NH, D], BF16)
                cw = kb.sb(st, "cw", [128, 3, NFC], F32)
                cb = kb.sb(st, "cb", [128, NFC], F32)
                P.dma("sp", cw[:], convw_d[l], writes=[(cw, A)])
                P.dma("sp", cb[:], convb_d[l], writes=[(cb, A)])
                for k in range(8):
                    for gu_ in range(2):
                        c0 = gu_ * DFF + half * HW
                        wload(stg, wup[:, k, gu_ * HW:(gu_ + 1) * HW], wup_d[l, k * 128:(k + 1) * 128, c0:c0 + HW], HW,
                              wup, (k, gu_))
                for c in range(NH):
                    r0 = (half * NH + c) * 128
                    wload(stg, wdn[:, c, :], wdn_d[l, r0:r0 + 128, :], D, wdn, c)
                WUP_R = [(wup, (k, g2)) for k in range(8) for g2 in range(2)]
                WDN_R = [(wdn, c) for c in range(NH)]
                if final:
                    gfb = kb.sb(st, "gfb", [128, D], F32)
                    P.dma("sp", gfb[:], gfin_d.partition_broadcast(128), writes=[(gfb, A)])
                    mAB = kb.sb(st, "mAB_sb", [128, 2], F32)
                    P.dma("sp", mAB[:], mAB_d, writes=[(mAB, A)])
                xw = kb.sb(st, "fxw", [128, 3, D], F32)
                mean = [kb.sb(st, f"fmean{i}", [128, 4], F32) for i in range(2)]
                rstd = [kb.sb(st, f"frstd{i}", [128, 4], F32) for i in range(2)]
                fm2 = kb.sb(st, "fm2", [128, 4], F32)
                fr2 = kb.sb(st, "fr2", [128, 4], F32)
                xn = kb.sb(st, "fxn", [128, 3, D], BF16)
                junk = kb.sb(st, "fjunk", [128, D], BF16)
                hT = [kb.sb(st, f"fhT{i}", [128, 8, 258], BF16) for i in range(2)]
                acc = [kb.sb(st, f"facc{i}", [128, 256], F32) for i in range(3)]
                sil = [kb.sb(st, f"fsil{i}", [128, 256], F32) for i in range(2)]
                gu = kb.sb(st, "fgu", [128, NH, 256], BF16)
                tmp = [kb.sb(st, f"ftmp{i}", [128, D], F32) for i in range(4)]
                xo = [kb.sb(st, f"fxo{i}", [128, D], F32) for i in range(4)]
                xr = [kb.sb(st, f"fxr{i}", [128, D], F32) for i in range(2)]
                P.v("pool", "memset", xw[:], 0.0, writes=[(xw, A)])
                for mb_ in mean + rstd:
                    P.v("pool", "memset", mb_[:], 1.0, writes=[(mb_, A)])

                def prep_a(wi):
                    w = windows[wi]
                    P.dma("sp", xw[:, 0:2, :], w["src"].rearrange("(j p) d -> p j d", p=128),
                          writes=[(xw, 0), (xw, 1)])
                    tiles = [(0, 128), (1, 128)]
                    if w["prev"] is not None:
                        P.dma("sp", xw[0:1, 2, :], w["prev"], writes=[(xw, 2)])
                        P.dma("sp", xw[32:33, 2, :], w["next"], writes=[(xw, 2)])
                        tiles.append((2, 33))
                    rms_prep(xw, tiles, mean[wi % 2], rstd[wi % 2], xn, junk)

                def prep(wi):
                    prep_a(wi)
                    prep_b(wi)

                def prep_b(wi, halves=(0, 1)):
                    w = windows[wi]
                    h_ = hT[wi % 2]
                    trans = [(0, 0, 128, 0), (1, 0, 128, 128)]
                    evac = [(0, 1, 256)]
                    halo = w["prev"] is not None
                    if halo:
                        trans.append((2, 0, 33, 256))
                        evac += [(256, 0, 1), (288, 257, 1)]
                    trans_mod(xn, trans, evac, h_, w["which"], 1, 0, halves=halves, act_main=True)
                    for (mk, col) in ((w["pmask"], 0), (w["nmask"], 257)):
                        for k in [kk_ + 4 * hv for hv in halves for kk_ in range(4)]:
                            if (not halo) or mk == 0:
                                P.v("dve", "memset", h_[:, k, col:col + 1], 0.0, writes=[(h_, A)])
                            elif mk in ("A", "B"):
                                mi = 0 if mk == "A" else 1
                                P.v("dve", "tensor_scalar", h_[:, k, col:col + 1], h_[:, k, col:col + 1],
                                    mAB[:, mi:mi + 1], None, ALU.mult, reads=[(h_, A), (mAB, A)], writes=[(h_, A)])

                def finish(wi):
                    w = windows[wi]
                    which = w["which"]
                    dst = w["mid"] if half == 0 else w["dst"]
                    for j in range(2):
                        r_ = xr[j]
                        t_, o_ = tmp[(2 * wi + j) % 4], xo[(2 * wi + j) % 4]
                        for hf in range(2):
                            pv, pk = bank(hf)
                            for c in range(NH):
                                P.mm(pv, gu[:, c, j * 128:(j + 1) * 128],
                                     wdn[:, c, hf * 512:(hf + 1) * 512], start=(c == 0), stop=(c == NH - 1),
                                     reads=[(gu, c)] + WDN_R, writes=pk)
                            P.v("dve", "tensor_tensor", t_[:, hf * 512:(hf + 1) * 512], pv,
                                G["gates"][:, which, 1, hf * 512:(hf + 1) * 512], ALU.mult,
                                reads=pk + [(G["gates"], (which, 1))], writes=[(t_, hf)])
                        P.v("pool", "tensor_tensor", o_[:], t_[:], r_[:], ALU.add,
                            reads=[(t_, 0), (t_, 1), (r_, A)], writes=[(o_, A)])
                        if not (final and last):
                            P.dma("pool", dst[j * 128:(j + 1) * 128, :], o_[:], reads=[(o_, A)])

                def finish_b(wi):
                    if not (final and last):
                        return
                    w = windows[wi]
                    dst = w["dst"]
                    for j in range(2):
                        t_, o_ = tmp[(2 * wi + j) % 4], xo[(2 * wi + j) % 4]
                        mj = (2 * wi + j) % 4
                        P.act(junk[:], o_[:], AF.Square, scale=1.0 / 32.0, accum_out=fm2[:, mj:mj + 1],
                              reads=[(o_, A)], writes=[(junk, A), (fm2, mj)])
                        P.act(fr2[:, mj:mj + 1], fm2[:, mj:mj + 1], AF.Ln, bias=EPS, reads=[(fm2, mj)], writes=[(fr2, mj)])
                        P.act(fr2[:, mj:mj + 1], fr2[:, mj:mj + 1], AF.Exp, scale=-0.5, reads=[(fr2, mj)],
                              writes=[(fr2, mj)])
                        P.v("dve", "scalar_tensor_tensor", t_[:], o_[:], fr2[:, mj:mj + 1], gfb[:], ALU.mult, ALU.mult,
                            reads=[(o_, A), (fr2, mj), (gfb, A)], writes=[(t_, 0), (t_, 1)])
                        P.dma("pool", dst[j * 128:(j + 1) * 128, :], t_[:], reads=[(t_, 0), (t_, 1)])

                GB = [2, 3, 6]
                UB = [4, 5, 7]

                def st_pe(wi, c):
                    h_ = hT[wi % 2]
                    pg, pgk = bank(GB[c % 3])
                    pu, puk = bank(UB[c % 3])
                    for k in range(8):
                        P.mm(pg[:, 0:258], wup[:, k, c * 128:(c + 1) * 128], h_[:, k, 0:258], start=(k == 0),
                             stop=(k == 7), reads=[(h_, A)] + WUP_R, writes=pgk)
                    for k in range(8):
                        P.mm(pu[:, 0:256], wup[:, k, HW + c * 128:HW + (c + 1) * 128], h_[:, k, 1:257],
                             start=(k == 0), stop=(k == 7), reads=[(h_, A)] + WUP_R, writes=puk)

                def st_id(c):
                    cg = half * NH + c
                    pg, pgk = bank(GB[c % 3])
                    a_ = acc[c % 3]
                    P.act(a_[:], pg[:, 1:257], AF.Identity, bias=cb[:, cg:cg + 1], scale=cw[:, 1, cg:cg + 1],
                          reads=pgk + [(cb, A), (cw, A)], writes=[(a_, A)])

                def st_conv(c, which_tap):
                    cg = half * NH + c
                    pg, pgk = bank(GB[c % 3])
                    a_ = acc[c % 3]
                    if which_tap == 0:
                        P.v("dve", "scalar_tensor_tensor", a_[:], pg[:, 0:256], cw[:, 0, cg:cg + 1], a_[:], ALU.mult,
                            ALU.add, reads=pgk + [(cw, A), (a_, A)], writes=[(a_, A)])
                    else:
                        P.v("dve", "scalar_tensor_tensor", a_[:], pg[:, 2:258], cw[:, 2, cg:cg + 1], a_[:], ALU.mult,
                            ALU.add, reads=pgk + [(cw, A), (a_, A)], writes=[(a_, A)])

                def st_silu(c):
                    a_, s_ = acc[c % 3], sil[c % 2]
                    P.act(s_[:], a_[:], AF.Silu, reads=[(a_, A)], writes=[(s_, A)])

                def st_mult(c):
                    s_ = sil[c % 2]
                    pu, puk = bank(UB[c % 3])
                    P.v("dve", "tensor_tensor", gu[:, c, :], s_[:], pu[:, 0:256], ALU.mult,
                        reads=[(s_, A)] + puk, writes=[(gu, c)])

                import os as _os3
                _PA = int(_os3.environ.get("FFN_PA", "3"))
                _PB = int(_os3.environ.get("FFN_PB", "6"))
                _PB2 = int(_os3.environ.get("FFN_PB2", "9"))
                prep(0)
                for wi, w in enumerate(windows):
                    base = w["res"] if half == 0 else w["mid"]
                    for j in range(2):
                        P.dma("sp", xr[j][:], base[j * 128:(j + 1) * 128, :], writes=[(xr[j], A)])
                    for c in range(NH + 1):
                        if c < NH:
                            st_pe(wi, c)
                            st_id(c)
                            st_conv(c, 0)
                        if c >= 1:
                            st_silu(c - 1)
                            st_mult(c - 1)
                        if c < NH:
                            st_conv(c, 1)
                        if c == 2 and wi >= 1:
                            finish_b(wi - 1)
                        if c == _PA and wi + 1 < len(windows):
                            prep_a(wi + 1)
                        if c == _PB and wi + 1 < len(windows):
                            prep_b(wi + 1, halves=(0,))
                        if c == _PB2 and wi + 1 < len(windows):
                            prep_b(wi + 1, halves=(1,))
                    finish(wi)
                finish_b(len(windows) - 1)
            P.barrier()

        if do0:
          with contextlib.ExitStack() as lay0:
            G["gates"] = kb.sb(lay0, "gates0", [128, 2, 2, D], F32)
            mod_phase(0, True)
            if stop_after == "mod":
                return finish_build()

            with contextlib.ExitStack() as l0:
                ckvnT = kb.sb(l0, "ckvnT", [128, 2, NTOK], BF16)
                KTb = [kb.sb(l0, f"KT{i}", [96, NTOK], BF16) for i in range(2)]

                def src_rows(t0, n):
                    if t0 < S:
                        return x_d[t0:t0 + n, :]
                    return ctx_d[t0 - S:t0 - S + n, :]

                groups = [(g * 512, 4, 0) for g in range(16)] + [(S, 2, 1)]

                with contextlib.ExitStack() as st:
                    zero_f = kb.sb(st, "zero_f", [128, 1024], F32)
                    P.v("pool", "memset", zero_f[:], 0.0, writes=[(zero_f, A)])
                    for g_ in range(4):
                        P.dma("sp", pT_d[g_, :, 0:8], zero_f[:, 0:8], reads=[(zero_f, A)])
                        P.dma("sp", pT_d[g_, :, S + 8:S + 16], zero_f[:, 0:8], reads=[(zero_f, A)])
                        P.dma("sp", pTc_d[g_, :, 0:8], zero_f[:, 0:8], reads=[(zero_f, A)])
                        P.dma("sp", pTc_d[g_, :, CT + 8:CT + 16], zero_f[:, 0:8], reads=[(zero_f, A)])
                    P.dma("sp", xs2_d[0:128, :], zero_f[:], reads=[(zero_f, A)])
                    P.dma("sp", xs2_d[128 + S:256 + S, :], zero_f[:], reads=[(zero_f, A)])
                    stg = [kb.sb(st, f"astg{i}", [128, 1184], F32) for i in range(2)]
                    win = kb.sb(st, "win0", [128, 8, 1184], BF16)
                    for k in range(8):
                        wload(stg, win[:, k, :], win0_d[k * 128:(k + 1) * 128, :], 1184, win, k)
                    WIN_R = [(win, k) for k in range(8)]
                    ropeA = kb.sb(st, "ropeA", [128, NT, 32], F32)
                    for t8 in range(0, NT, 8):
                        n8 = min(8, NT - t8)
                        P.dma("sp", ropeA[:, t8:t8 + n8, :],
                              ropeA_tok_d[t8 * 128:(t8 + n8) * 128, :].rearrange("(t p) c -> p t c", p=128),
                              writes=[(ropeA, A)])
                    xg = [kb.sb(st, f"axg{i}", [128, 4, D], F32) for i in range(2)]
                    mean = [kb.sb(st, f"amean{i}", [128, 4], F32) for i in range(2)]
                    rstd = [kb.sb(st, f"arstd{i}", [128, 4], F32) for i in range(2)]
                    xn = kb.sb(st, "axn", [128, 4, D], BF16)
                    junk = kb.sb(st, "ajunk", [128, D], BF16)
                    hT = [kb.sb(st, f"ahT{i}", [128, 8, 512], BF16) for i in range(2)]
                    m2 = kb.sb(st, "am2", [128, 4, 2], F32)
                    r2 = kb.sb(st, "ar2", [128, 4, 2], F32)
                    cq_tok = kb.sb(st, "acq", [128, 4, 384], BF16)
                    kv_tok = kb.sb(st, "akv", [128, 4, 288], BF16)
                    rtmp = kb.sb(st, "artmp", [128, 8, 16], F32)
                    cqT = [kb.sb(st, f"acqT{i}", [128, 3, 512], BF16) for i in range(2)]
                    pst = [kb.sb(st, f"apst{i}", [128, 4, 512], F32) for i in range(1)]

                    def a_prep(gi):
                        t0, n, which = groups[gi]
                        xb = xg[gi % 2]
                        P.dma("sp", xb[:, 0:n, :], src_rows(t0, n * 128).rearrange("(j p) d -> p j d", p=128),
                              writes=[(xb, j) for j in range(n)])
                        rms_prep(xb, [(j, 128) for j in range(n)], mean[gi % 2], rstd[gi % 2], xn, junk)
                        trans_mod(xn, [(j, 0, 128, j * 128) for j in range(n)], [(0, 0, n * 128)], hT[gi % 2], which, 0, 0)

                    import os as _os
                    _ng = int(_os.environ.get("A_NG", "99"))
                    _parts = int(_os.environ.get("A_PARTS", "15"))
                    a_prep(0)
                    for gi, (t0, n, which) in enumerate(groups):
                        if gi >= _ng:
                            break
                        h_ = hT[gi % 2]
                        ntok = n * 128
                        for j in range(n if (_parts & 1) else 0):
                            tt = t0 // 128 + j
                            pq, pqk = bank(2)
                            pk_, pkk = bank(3)
                            for k in range(8):
                                P.mm(pq[:, 0:384], h_[:, k, j * 128:(j + 1) * 128], win[:, k, 0:384], start=(k == 0),
                                     stop=(k == 7), reads=[(h_, A)] + WIN_R, writes=pqk)
                            for k in range(8):
                                P.mm(pk_[:, 0:288], h_[:, k, j * 128:(j + 1) * 128], win[:, k, 384:672],
                                     start=(k == 0), stop=(k == 7), reads=[(h_, A)] + WIN_R, writes=pkk)
                            if not (int(_os.environ.get("A_SUB", "3")) & 1):
                                continue
                            P.act(junk[:, 0:384], pq[:, 0:384], AF.Square, scale=384.0 ** -0.5,
                                  accum_out=m2[:, j, 0:1], reads=pqk, writes=[(junk, A), (m2, j)])
                            P.act(junk[:, 0:256], pk_[:, 0:256], AF.Square, scale=1.0 / 16.0,
                                  accum_out=m2[:, j, 1:2], reads=pkk, writes=[(junk, A), (m2, j)])
                            P.act(r2[:, j, :], m2[:, j, :], AF.Ln, bias=EPS, reads=[(m2, j)], writes=[(r2, j)])
                            P.act(r2[:, j, :], r2[:, j, :], AF.Exp, scale=-0.5, reads=[(r2, j)], writes=[(r2, j)])
                            P.act(cq_tok[:, j, :], pq[:, 0:384], AF.Copy, scale=r2[:, j, 0:1],
                                  reads=pqk + [(r2, j)], writes=[(cq_tok, j)])
                            P.act(kv_tok[:, j, 0:256], pk_[:, 0:256], AF.Copy, scale=r2[:, j, 1:2],
                                  reads=pkk + [(r2, j)], writes=[(kv_tok, j)])
                            if not (int(_os.environ.get("A_SUB", "3")) & 2):
                                continue
                            kr32 = rtmp[:, 4:6, :].rearrange("p a i -> p (a i)")
                            P.act(kr32, pk_[:, 256:288], AF.Copy, reads=pkk, writes=[(rtmp, 4)])
                            kr = kr32.rearrange("p (i two) -> p i two", two=2)
                            cs, sn = ropeA[:, tt, 0:16], ropeA[:, tt, 16:32]
                            R_ = [(rtmp, 4), (ropeA, A)]
                            P.v("dve", "tensor_tensor", rtmp[:, 0, :], kr[:, :, 0], cs, ALU.mult, reads=R_, writes=[(rtmp, 0)])
                            P.v("dve", "tensor_tensor", rtmp[:, 1, :], kr[:, :, 1], sn, ALU.mult, reads=R_, writes=[(rtmp, 1)])
                            P.v("dve", "tensor_tensor", rtmp[:, 2, :], kr[:, :, 0], sn, ALU.mult, reads=R_, writes=[(rtmp, 2)])
                            P.v("dve", "tensor_tensor", rtmp[:, 3, :], kr[:, :, 1], cs, ALU.mult, reads=R_, writes=[(rtmp, 3)])
                            ro = rtmp[:, 6:8, :].rearrange("p a i -> p (a i)")
                            ro2 = ro.rearrange("p (i two) -> p i two", two=2)
                            P.v("dve", "tensor_tensor", ro2[:, :, 0], rtmp[:, 0, :], rtmp[:, 1, :], ALU.subtract,
                                reads=[(rtmp, 0), (rtmp, 1)], writes=[(rtmp, 6)])
                            P.v("dve", "tensor_tensor", ro2[:, :, 1], rtmp[:, 2, :], rtmp[:, 3, :], ALU.add,
                                reads=[(rtmp, 2), (rtmp, 3), (rtmp, 6)], writes=[(rtmp, 6)])
                            P.v("dve", "tensor_copy", kv_tok[:, j, 256:288], ro, reads=[(rtmp, 6)], writes=[(kv_tok, j)])
                        pv, pk6 = bank_bf(6, 2)
                        pv3 = pv[:, 0:1536].rearrange("p (c t) -> p c t", c=3)
                        for j in range(n if (_parts & 2) else 0):
                            for c in range(3):
                                P.tr(pv3[:, c, j * 128:(j + 1) * 128], cq_tok[:, j, c * 128:(c + 1) * 128], ident[:],
                                     reads=[(cq_tok, j), (ident, A)], writes=pk6)
                        cqb = cqT[gi % 2]
                        for c in range(3 if (_parts & 2) else 0):
                            P.v("dve", "tensor_copy", cqb[:, c, 0:ntok], pv3[:, c, 0:ntok], reads=pk6, writes=[(cqb, A)])
                        if _parts & 2:
                            P.dma("pool", cqnT_d[:, :, t0:t0 + ntok].rearrange("c p t -> p c t"), cqb[:, :, 0:ntok],
                                  reads=[(cqb, A)])
                        for j in range(n if (_parts & 4) else 0):
                            for c in range(2):
                                P.tr(pv3[:, c, j * 128:(j + 1) * 128], kv_tok[:, j, c * 128:(c + 1) * 128], ident[:],
                                     reads=[(kv_tok, j), (ident, A)], writes=pk6)
                            P.tr(pv3[0:96, 2, j * 128:(j + 1) * 128], kv_tok[:, j, 192:288], ident[:],
                                 reads=[(kv_tok, j), (ident, A)], writes=pk6)
                        for c in range(2 if (_parts & 4) else 0):
                            P.v("dve", "tensor_copy", ckvnT[:, c, t0:t0 + ntok], pv3[:, c, 0:ntok], reads=pk6,
                                writes=[(ckvnT, gi)])
                        if _parts & 4:
                            for KT_ in KTb:
                                P.v("dve", "tensor_copy", KT_[64:96, t0:t0 + ntok], pv3[64:96, 2, 0:ntok], reads=pk6,
                                    writes=[(KT_, ("r", gi))])
                        pb = pst[0]
                        for gq in range(4 if (_parts & 8) else 0):
                            pp, ppk = bank(4 + gq % 2)
                            for k in range(8):
                                P.mm(pp[:, 0:ntok], win[:, k, 672 + gq * 128:672 + (gq + 1) * 128], h_[:, k, 0:ntok],
                                     start=(k == 0), stop=(k == 7), reads=[(h_, A)] + WIN_R, writes=ppk)
                            P.act(pb[:, gq, 0:ntok], pp[:, 0:ntok], AF.Copy, reads=ppk, writes=[(pb, gq)])
                        if not (_parts & 8):
                            pass
                        elif which == 0:
                            P.dma("pool", pT_d[:, :, 8 + t0:8 + t0 + ntok].rearrange("g p t -> p g t"), pb[:, :, 0:ntok],
                                  reads=[(pb, g_) for g_ in range(4)])
                        else:
                            P.dma("pool", pTc_d[:, :, 8:8 + ntok].rearrange("g p t -> p g t"), pb[:, :, 0:ntok],
                                  reads=[(pb, g_) for g_ in range(4)])
                        if gi + 1 < len(groups) and gi + 1 < _ng:
                            a_prep(gi + 1)
                P.barrier()
                if stop_after == "A":
                    return finish_build()

                with contextlib.ExitStack() as st:
                    stg = [kb.sb(st, f"bstg{i}", [128, 128], F32) for i in range(2)]
                    pwb = kb.sb(st, "poolw", [128, 4, 128], BF16)
                    for g_ in range(4):
                        wload(stg, pwb[:, g_, :], poolw_d[g_], 128, pwb, g_)
                    psc = kb.sb(st, "pools", [128, 4], F32)
                    P.dma("sp", psc[:], pools_d, writes=[(psc, A)])
                    pw = [kb.sb(st, f"bpw{i}", [128, 4, 528], F32) for i in range(2)]
                    inv = [kb.sb(st, f"binv{i}", [128, 4, 512], F32) for i in range(2)]
                    t1 = [kb.sb(st, f"bt1_{i}", [128, 528], F32) for i in range(4)]
                    t2 = [kb.sb(st, f"bt2_{i}", [128, 528], F32) for i in range(4)]
                    dT = [kb.sb(st, f"bdT{i}", [128, 4, 512], BF16) for i in range(2)]
                    po = [kb.sb(st, f"bpo{i}", [128, 4, 512], BF16) for i in range(2)]
                    blocks = [(g * 512, 512, 0) for g in range(16)] + [(S, 256, 1)]
                    for bi, (t0, ntok, which) in enumerate(blocks):
                        p_, iv = pw[bi % 2], inv[bi % 2]
                        if which == 0:
                            P.dma("sp", p_[:, :, 0:ntok + 16], pT_d[:, :, t0:t0 + ntok + 16].rearrange("g p t -> p g t"),
                                  writes=[(p_, A)])
                            for g_ in range(4):
                                P.dma("sp", iv[:, g_, 0:ntok], invc_d[g_, t0:t0 + ntok].partition_broadcast(128),
                                      writes=[(iv, g_)])
                        else:
                            P.dma("sp", p_[:, :, 0:ntok + 16], pTc_d[:, :, 0:ntok + 16].rearrange("g p t -> p g t"),
                                  writes=[(p_, A)])
                            for g_ in range(4):
                                P.dma("sp", iv[:, g_, 0:ntok], invcc_d[g_, 0:ntok].partition_broadcast(128),
                                      writes=[(iv, g_)])
                        d_, o_ = dT[bi % 2], po[bi % 2]
                        for g_ in range(4):
                            w_ = 2 << g_
                            eng = "dve" if g_ % 2 == 1 else "pool"
                            cur, curlen, curbuf = p_[:, g_, 0:ntok + 16], ntok + 16, None
                            a_, b_ = t1[g_], t2[g_]
                            step = 1
                            while step < w_:
                                nl = curlen - step
                                dst = a_
                                P.v(eng, "tensor_tensor", dst[:, 0:nl], cur[:, 0:nl], cur[:, step:step + nl], ALU.add,
                                    reads=[(p_, A)] + ([(curbuf, A)] if curbuf is not None else []), writes=[(dst, A)])
                                cur, curlen, curbuf = dst[:, 0:nl], nl, dst
                                a_, b_ = b_, a_
                                step *= 2
                            o0 = 8 - w_ // 2
                            dst = a_
                            P.v("dve", "tensor_tensor", dst[:, 0:ntok], cur[:, o0:o0 + ntok], iv[:, g_, 0:ntok], ALU.mult,
                                reads=[(curbuf, A), (iv, g_)], writes=[(dst, A)])
                            P.v("dve", "tensor_tensor", d_[:, g_, 0:ntok], dst[:, 0:ntok], p_[:, g_, 8:8 + ntok],
                                ALU.subtract, reads=[(dst, A), (p_, A)], writes=[(d_, g_)])
                            pp, ppk = bank(g_)
                            P.mm(pp[:, 0:ntok], pwb[:, g_, :], d_[:, g_, 0:ntok], start=True, stop=True,
                                 reads=[(pwb, g_), (d_, g_)], writes=ppk)
                            P.act(o_[:, g_, 0:ntok], pp[:, 0:ntok], AF.Copy, scale=psc[:, g_:g_ + 1],
                                  reads=ppk + [(psc, A)], writes=[(o_, g_)])
                        P.dma("pool", aoT_d[4:8, :, t0:t0 + ntok].rearrange("g p t -> p g t"), o_[:, :, 0:ntok],
                              reads=[(o_, g_) for g_ in range(4)])
                P.barrier()
                if stop_after == "B":
                    return finish_build()

                with contextlib.ExitStack() as st:
                    stg = [kb.sb(st, f"cstg{i}", [128, 768], F32) for i in range(2)]
                    gq0 = kb.sb(st, "gq0", [128, 3], F32)
                    gkv0 = kb.sb(st, "gkv0", [128, 2], F32)
                    P.dma("sp", gq0[:], gq0_d, writes=[(gq0, A)])
                    P.dma("sp", gkv0[:], gkv0_d, writes=[(gkv0, A)])
                    wuq = kb.sb(st, "wuq", [128, 3, 768], BF16)
                    wuqs = kb.sb(st, "wuqs", [128, 3, 768], BF16)
                    wuk = kb.sb(st, "wuk", [128, 2, 512], BF16)
                    wuv = kb.sb(st, "wuv", [128, 2, 512], BF16)
                    for c in range(3):
                        wload(stg, wuq[:, c, :], wuq_d[c * 128:(c + 1) * 128, :], 768, wuq, A,
                              scale_ap=gq0[:, c:c + 1], scale_reads=[(gq0, A)])
                        wload(stg, wuqs[:, c, :], wuqs_d[c * 128:(c + 1) * 128, :], 768, wuqs, A,
                              scale_ap=gq0[:, c:c + 1], scale_reads=[(gq0, A)])
                    for c in range(2):
                        wload(stg, wuk[:, c, :], wuk_d[c * 128:(c + 1) * 128, :], 512, wuk, A,
                              scale_ap=gkv0[:, c:c + 1], scale_reads=[(gkv0, A)])
                        wload(stg, wuv[:, c, :], wuv_d[c * 128:(c + 1) * 128, :], 512, wuv, A,
                              scale_ap=gkv0[:, c:c + 1], scale_reads=[(gkv0, A)])
                    Vb = [kb.sb(st, f"V0_{i}", [128, NT, 128], BF16) for i in range(2)]
                    cqb = [kb.sb(st, f"ccq{i}", [128, 3, 512], BF16) for i in range(2)]
                    ctab = [kb.sb(st, f"cct{i}", [96, 2, 512], F32) for i in range(2)]
                    QT = [kb.sb(st, f"cQT{i}", [96, 512], BF16) for i in range(2)]
                    qt1 = kb.sb(st, "cqt1", [96, 512], F32)
                    qt2 = kb.sb(st, "cqt2", [96, 512], F32)
                    PT = [kb.sb(st, f"cPT{i}", [128, 512], BF16) for i in range(4)]
                    rsb = kb.sb(st, "crsb", [128, 512], F32)
                    bcs = kb.sb(st, "cbcs", [128, 512], F32)
                    ao = [kb.sb(st, f"cao{i}", [128, 512], BF16) for i in range(2)]
                    qblocks = [(g * 512, 512, list(range(NT))) for g in range(16)] + [(S, 256, [64, 65])]
                    SC = 96.0 ** -0.5
                    chunks = [(g * 512, 512) for g in range(16)] + [(S, 256)]

                    def build_steps(h):
                        KTh, Vh = KTb[h % 2], Vb[h % 2]
                        even = (h % 2 == 0)
                        voff = 0 if even else 64
                        steps = []

                        def k_chunk(ci):
                            c0, cn = chunks[ci]
                            pp, ppk = bank(6 + ci % 2)
                            for kc in range(2):
                                P.mm(pp[0:64, 0:cn], wuk[:, kc, h * 64:(h + 1) * 64], ckvnT[:, kc, c0:c0 + cn],
                                     start=(kc == 0), stop=(kc == 1), reads=[(wuk, A), (ckvnT, ci)], writes=ppk)
                            P.v("dve", "tensor_copy", KTh[0:64, c0:c0 + cn], pp[0:64, 0:cn], reads=ppk,
                                writes=[(KTh, ("n", ci))])

                        def v_init():
                            if even:
                                P.v("pool", "memset", Vh[:, :, 64:65], 1.0, writes=[(Vh, A)])
                            else:
                                P.v("pool", "memset", Vh[:, :, 0:64], 0.0, writes=[(Vh, A)])
                                P.v("pool", "memset", Vh[:, :, 0:1], 1.0, writes=[(Vh, A)])

                        def v_batch(b0):
                            nb = min(8, NT - b0)
                            pp, ppk = bank(7)
                            pp3 = pp.rearrange("p (i d) -> p i d", d=64)
                            for i in range(nb):
                                t = b0 + i
                                for kc in range(2):
                                    P.mm(pp3[:, i, :], ckvnT[:, kc, t * 128:(t + 1) * 128], wuv[:, kc, h * 64:(h + 1) * 64],
                                         start=(kc == 0), stop=(kc == 1), reads=[(ckvnT, t // 4), (wuv, A)], writes=ppk)
                            P.v("dve", "tensor_copy", Vh[:, b0:b0 + nb, voff:voff + 64], pp3[:, 0:nb, :], reads=ppk,
                                writes=[(Vh, b0 // 8)])

                        steps.append(v_init)
                        for ci in range(len(chunks)):
                            steps.append(lambda ci=ci: k_chunk(ci))
                        for b0 in range(0, NT, 8):
                            steps.append(lambda b0=b0: v_batch(b0))
                        return steps

                    for st_ in build_steps(0):
                        st_()
                    for h in range(8):
                        even = (h % 2 == 0)
                        KTh, V = KTb[h % 2], Vb[h % 2]
                        nxt = build_steps(h + 1) if h + 1 < 8 else []
                        M = 65 if even else 128
                        srow = 64 if even else 0
                        r0 = 0 if even else 64

                        def prep_q(qi):
                            t0, nq, _ = qblocks[qi]
                            cb_, tb_, q_ = cqb[qi % 2], ctab[qi % 2], QT[qi % 2]
                            P.dma("sp", cb_[:, :, 0:nq], cqnT_d[:, :, t0:t0 + nq].rearrange("c p t -> p c t"),
                                  writes=[(cb_, A)])
                            P.dma("sp", tb_[64:96, 0, 0:nq], ropeA_c_d[:, t0:t0 + nq], writes=[(tb_, 0)])
                            P.dma("sp", tb_[64:96, 1, 0:nq], ropeA_s_d[:, t0:t0 + nq], writes=[(tb_, 1)])
                            p1, p1k = bank(6)
                            p2, p2k = bank(7)
                            for kc in range(3):
                                P.mm(p1[0:96, 0:nq], wuq[:, kc, h * 96:(h + 1) * 96], cb_[:, kc, 0:nq], start=(kc == 0),
                                     stop=(kc == 2), reads=[(wuq, A), (cb_, A)], writes=p1k)
                            for kc in range(3):
                                P.mm(p2[0:96, 0:nq], wuqs[:, kc, h * 96:(h + 1) * 96], cb_[:, kc, 0:nq], start=(kc == 0),
                                     stop=(kc == 2), reads=[(wuqs, A), (cb_, A)], writes=p2k)
                            P.v("dve", "tensor_copy", q_[0:64, 0:nq], p1[0:64, 0:nq], reads=p1k, writes=[(q_, "n")])
                            P.v("dve", "tensor_tensor", qt1[64:96, 0:nq], p1[64:96, 0:nq], tb_[64:96, 0, 0:nq], ALU.mult,
                                reads=p1k + [(tb_, 0)], writes=[(qt1, A)])
                            P.v("dve", "tensor_tensor", qt2[64:96, 0:nq], p2[64:96, 0:nq], tb_[64:96, 1, 0:nq], ALU.mult,
                                reads=p2k + [(tb_, 1)], writes=[(qt2, A)])
                            P.v("dve", "tensor_tensor", q_[64:96, 0:nq], qt1[64:96, 0:nq], qt2[64:96, 0:nq], ALU.add,
                                reads=[(qt1, A), (qt2, A)], writes=[(q_, "r")])

                        def finalize(qi):
                            t0, nq, _ = qblocks[qi]
                            po_, pok = bank(4 + qi % 2)
                            P.v("dve", "reciprocal", rsb[srow:srow + 1, 0:nq], po_[srow:srow + 1, 0:nq], reads=pok,
                                writes=[(rsb, A)])
                            pb_, pbk = bank(6)
                            P.mm(pb_[:, 0:nq], ones_f[srow:srow + 1, :], rsb[srow:srow + 1, 0:nq], start=True, stop=True,
                                 reads=[(ones_f, A), (rsb, A)], writes=pbk)
                            P.act(bcs[r0:r0 + 64, 0:nq], pb_[r0:r0 + 64, 0:nq], AF.Copy, reads=pbk, writes=[(bcs, A)])
                            a_ = ao[qi % 2]
                            P.v("dve", "tensor_tensor", a_[r0:r0 + 64, 0:nq], po_[r0:r0 + 64, 0:nq], bcs[r0:r0 + 64, 0:nq],
                                ALU.mult, reads=pok + [(bcs, A)], writes=[(a_, A)])
                            P.dma("pool", aoT_d[h // 2, r0:r0 + 64, t0:t0 + nq], a_[r0:r0 + 64, 0:nq], reads=[(a_, A)])

                        prep_q(0)
                        pending = None
                        for qi, (t0, nq, kts) in enumerate(qblocks):
                            q_ = QT[qi % 2]
                            po_, pok = bank(4 + qi % 2)
                            nk = len(kts)
                            npair = (nk + 1) // 2

                            SBK = [0, 1, 2, 3]

                            def qk(jj):
                                kt = kts[jj]
                                ps_, psk = bank(SBK[jj % 4])
                                ci = kt // 4
                                P.mm(ps_[:, 0:nq], KTh[0:96, kt * 128:(kt + 1) * 128], q_[0:96, 0:nq], start=True, stop=True,
                                     reads=[(KTh, ("n", ci)), (KTh, ("r", ci)), (q_, "n"), (q_, "r")], writes=psk)

                            def ex_pv(jj):
                                kt = kts[jj]
                                ps_, psk = bank(SBK[jj % 4])
                                pt_ = PT[jj % 4]
                                P.act(pt_[:, 0:nq], ps_[:, 0:nq], AF.Exp, scale=SC, reads=psk, writes=[(pt_, A)])
                                P.mm(po_[0:M, 0:nq], V[:, kt, 0:M], pt_[:, 0:nq], start=(jj == 0), stop=(jj == nk - 1),
                                     reads=[(V, kt // 8), (pt_, A)], writes=pok)

                            for jj in range(min(3, nk)):
                                qk(jj)
                            for jj in range(nk):
                                if jj + 3 < nk:
                                    qk(jj + 3)
                                ex_pv(jj)
                                if jj == 3 and pending is not None:
                                    finalize(pending)
                                    pending = None
                                if jj == min(8, nk - 1) and qi + 1 < len(qblocks):
                                    prep_q(qi + 1)
                                if jj in (20, 40) and nxt:
                                    nxt.pop(0)()
                            if pending is not None:
                                finalize(pending)
                            pending = qi
                        finalize(pending)
                        while nxt:
                            nxt.pop(0)()
                P.barrier()
            if stop_after == "C":
                return finish_build()

            with contextlib.ExitStack() as st:
                stg = [kb.sb(st, f"dstg{i}", [128, D], F32) for i in range(2)]
                wo = kb.sb(st, "wout0", [128, 8, D], BF16)
                for k in range(8):
                    wload(stg, wo[:, k, :], wout0_d[k * 128:(k + 1) * 128, :], D, wo, k)
                WO_R = [(wo, k) for k in range(8)]
                aog = [kb.sb(st, f"daog{i}", [128, 8, 512], BF16) for i in range(2)]
                xg = [kb.sb(st, f"dxg{i}", [128, 4, D], F32) for i in range(2)]
                tmp = [kb.sb(st, f"dtmp{i}", [128, D], F32) for i in range(2)]
                xo = [kb.sb(st, f"dxo{i}", [128, D], F32) for i in range(2)]
                for gi, (t0, n, which) in enumerate(groups):
                    ntok = n * 128
                    ab, xb = aog[gi % 2], xg[gi % 2]
                    P.dma("sp", ab[:, :, 0:ntok], aoT_d[:, :, t0:t0 + ntok].rearrange("c p t -> p c t"), writes=[(ab, A)])
                    P.dma("sp", xb[:, 0:n, :], src_rows(t0, ntok).rearrange("(j p) d -> p j d", p=128), writes=[(xb, A)])
                    for j in range(n):
                        pv, pk = bank((j % 2) * 2 + (4 if (j // 2) % 2 else 0), 2)
                        for hf in range(2):
                            for c in range(8):
                                P.mm(pv[:, hf * 512:(hf + 1) * 512], ab[:, c, j * 128:(j + 1) * 128],
                                     wo[:, c, hf * 512:(hf + 1) * 512], start=(c == 0), stop=(c == 7),
                                     reads=[(ab, A)] + WO_R, writes=[pk[hf]])
                        t_, o_ = tmp[j % 2], xo[j % 2]
                        P.v("dve", "tensor_tensor", t_[:], pv, G["gates"][:, which, 0, :], ALU.mult,
                            reads=pk + [(G["gates"], (which, 0))], writes=[(t_, A)])
                        P.v("pool", "tensor_tensor", o_[:], t_[:], xb[:, j, :], ALU.add, reads=[(t_, A), (xb, A)],
                            writes=[(o_, A)])
                        P.dma("pool", xs1_d[t0 + j * 128:t0 + (j + 1) * 128, :], o_[:], reads=[(o_, A)])
            P.barrier()

            if stop_after == "D":
                return finish_build()
            wins = []
            for wi in range(S // 256):
                s0 = wi * 256
                wins.append(dict(src=xs1_d[s0:s0 + 256, :], res=xs1_d[s0:s0 + 256, :], mid=xmid_d[s0:s0 + 256, :],
                                 which=0,
                                 prev=xs1_d[max(s0 - 1, 0):max(s0 - 1, 0) + 1, :],
                                 next=xs1_d[min(s0 + 256, S - 1):min(s0 + 256, S - 1) + 1, :],
                                 pmask=(0 if wi == 0 else None), nmask=(0 if wi == S // 256 - 1 else None),
                                 dst=xs2_d[128 + s0:128 + s0 + 256, :]))
            wins.append(dict(src=xs1_d[S:S + 256, :], res=xs1_d[S:S + 256, :], mid=xmid_d[S:S + 256, :], which=1,
                             prev=None, next=None, pmask=0, nmask=0, dst=xcs2_d))
            ffn_phase(0, wins, final=False)

        if do1:
          with contextlib.ExitStack() as lay1:
            G["gates"] = kb.sb(lay1, "gates1", [128, 1, 2, D], F32)
            mod_phase(1, False)
            eblocks = [(g * 512, 4) for g in range(8)] + [(4096, 2)]
            with contextlib.ExitStack() as l1:
                KT1 = kb.sb(l1, "KT1", [128, 2, NTOK], BF16)
                V1 = kb.sb(l1, "V1", [128, NT, 256], BF16)
                gqb = kb.sb(l1, "gqb", [128, 128], F32)
                gkb = kb.sb(l1, "gkb", [128, 128], F32)
                P.dma("sp", gqb[:], gq1_d.partition_broadcast(128), writes=[(gqb, A)])
                P.dma("sp", gkb[:], gk1_d.partition_broadcast(128), writes=[(gkb, A)])
                xn = kb.sb(l1, "gxn", [128, 4, D], BF16)
                junk = kb.sb(l1, "gjunk", [128, D], BF16)
                hT = [kb.sb(l1, f"ghT{i}", [128, 8, 512], BF16) for i in range(2)]
                mean = [kb.sb(l1, f"gmean{i}", [128, 4], F32) for i in range(2)]
                rstd = [kb.sb(l1, f"grstd{i}", [128, 4], F32) for i in range(2)]
                hs = kb.sb(l1, "ghs", [128, 8], F32)
                hr = kb.sb(l1, "ghr", [128, 8], F32)
                qn = kb.sb(l1, "gqn", [128, D], F32)
                rt = [kb.sb(l1, f"grt{i}", [128, 8, 64], F32) for i in range(2)]

                def head_norm_rope(src_ap, src_keys, nh, gain, tab, tab_reads, dst_ap, dst_w):
                    w_ = nh * 128
                    import os as _os2
                    _hnr_steps = int(_os2.environ.get("HNR_STEPS", "99"))
                    q3 = qn[:, 0:w_].rearrange("p (h d) -> p h d", d=128)
                    if _hnr_steps >= 1:
                        P.act(qn[:, 0:w_], src_ap, AF.Square, scale=128.0 ** -0.5, reads=src_keys, writes=[(qn, A)])
                    if _hnr_steps >= 2:
                        P.v("dve", "tensor_reduce", hs[:, 0:nh], q3, AX.X, ALU.add, reads=[(qn, A)], writes=[(hs, A)])
                    if _hnr_steps >= 3:
                        P.act(hr[:, 0:nh], hs[:, 0:nh], AF.Ln, bias=EPS, reads=[(hs, A)], writes=[(hr, A)])
                    if _hnr_steps >= 4:
                        P.act(hr[:, 0:nh], hr[:, 0:nh], AF.Exp, scale=-0.5, reads=[(hr, A)], writes=[(hr, A)])
                    if _hnr_steps >= 5:
                        for hh in range(nh):
                            P.act(qn[:, hh * 128:(hh + 1) * 128], src_ap[:, hh * 128:(hh + 1) * 128], AF.Copy,
                                  scale=hr[:, hh:hh + 1], reads=src_keys + [(hr, A)], writes=[(qn, A)])
                    if _hnr_steps >= 6:
                        P.v("dve", "tensor_tensor", q3, q3, gain[:].unsqueeze(1).to_broadcast([128, nh, 128]), ALU.mult,
                            reads=[(qn, A), (gain, A)], writes=[(qn, A)])
                    q4 = qn[:, 0:w_].rearrange("p (h i two) -> p h i two", two=2, i=64)
                    cs = tab[:, 0:64].unsqueeze(1).to_broadcast([128, nh, 64])
                    sn = tab[:, 64:128].unsqueeze(1).to_broadcast([128, nh, 64])
                    R_ = [(qn, A)] + list(tab_reads)
                    t0_, t1_ = rt[0][:, 0:nh, :], rt[1][:, 0:nh, :]
                    x0, x1 = q4[:, :, :, 0], q4[:, :, :, 1]
                    if _hnr_steps >= 7:
                        P.v("dve", "tensor_tensor", t0_, x0, cs, ALU.mult, reads=R_, writes=[(rt[0], A)])
                    if _hnr_steps >= 8:
                        P.v("dve", "tensor_tensor", t1_, x1, sn, ALU.mult, reads=R_, writes=[(rt[1], A)])
                    if _hnr_steps >= 9:
                        P.v("dve", "tensor_tensor", t0_, t0_, t1_, ALU.subtract, reads=[(rt[0], A), (rt[1], A)],
                            writes=[(rt[0], A)])
                    if _hnr_steps >= 10:
                        P.v("dve", "tensor_tensor", t1_, x0, sn, ALU.mult, reads=R_, writes=[(rt[1], A)])
                    if _hnr_steps >= 11:
                        P.v("dve", "tensor_tensor", x1, x1, cs, ALU.mult, reads=R_, writes=[(qn, A)])
                    if _hnr_steps >= 12:
                        P.v("dve", "tensor_tensor", x1, x1, t1_, ALU.add, reads=[(qn, A), (rt[1], A)], writes=[(qn, A)])
                    if _hnr_steps >= 13:
                        P.v("dve", "tensor_copy", x0, t0_, reads=[(rt[0], A), (qn, A)], writes=[(qn, A)])
                    if _hnr_steps >= 14:
                        P.v("dve", "tensor_copy", dst_ap, qn[:, 0:w_], reads=[(qn, A)], writes=dst_w)

                with contextlib.ExitStack() as st:
                    stg = [kb.sb(st, f"gstg{i}", [128, 512], F32) for i in range(2)]
                    win = kb.sb(st, "win1kv", [128, 8, 512], BF16)
                    for k in range(8):
                        wload(stg, win[:, k, :], win1_d[k * 128:(k + 1) * 128, 1024:1536], 512, win, k)
                    WIN_R = [(win, k) for k in range(8)]
                    xg = [kb.sb(st, f"exg{i}", [128, 4, D], F32) for i in range(2)]
                    rk = [kb.sb(st, f"erk{i}", [128, 4, 128], F32) for i in range(2)]
                    ktok = kb.sb(st, "ektok", [128, 4, 256], BF16)
                    groups1 = [(g * 512, 4, 0) for g in range(16)] + [(S, 2, 1)]

                    def e_prep(gi):
                        t0, n, which = groups1[gi]
                        xb = xg[gi % 2]
                        src = xs2_d[128 + t0:128 + t0 + n * 128, :] if which == 0 else xcs2_d
                        P.dma("sp", xb[:, 0:n, :], src.rearrange("(j p) d -> p j d", p=128),
                              writes=[(xb, j) for j in range(n)])
                        P.dma("sp", rk[gi % 2][:, 0:n, :],
                              ropeC_k_d[t0:t0 + n * 128, :].rearrange("(j p) c -> p j c", p=128), writes=[(rk[gi % 2], A)])
                        rms_prep(xb, [(j, 128) for j in range(n)], mean[gi % 2], rstd[gi % 2], xn, junk)
                        trans_mod(xn, [(j, 0, 128, j * 128) for j in range(n)], [(0, 0, n * 128)], hT[gi % 2], which, 0, 0)

                    e_prep(0)
                    for gi, (t0, n, which) in enumerate(groups1):
                        h_ = hT[gi % 2]
                        ntok = n * 128
                        for j in range(n):
                            tt = t0 // 128 + j
                            pp, ppk = bank(2 + j % 2)
                            for k in range(8):
                                P.mm(pp[:, 0:512], h_[:, k, j * 128:(j + 1) * 128], win[:, k, :], start=(k == 0),
                                     stop=(k == 7), reads=[(h_, A)] + WIN_R, writes=ppk)
                            head_norm_rope(pp[:, 0:256], ppk, 2, gkb, rk[gi % 2][:, j, :], [(rk[gi % 2], A)],
                                           ktok[:, j, :], [(ktok, j)])
                            P.act(V1[:, tt, :], pp[:, 256:512], AF.Copy, reads=ppk, writes=[(V1, tt)])
                        pv, pk6 = bank_bf(6, 1)
                        pv3 = pv.rearrange("p (c t) -> p c t", c=2)
                        for j in range(n):
                            for c in range(2):
                                P.tr(pv3[:, c, j * 128:(j + 1) * 128], ktok[:, j, c * 128:(c + 1) * 128], ident[:],
                                     reads=[(ktok, j), (ident, A)], writes=pk6)
                        for c in range(2):
                            P.v("dve", "tensor_copy", KT1[:, c, t0:t0 + ntok], pv3[:, c, 0:ntok], reads=pk6,
                                writes=[(KT1, gi)])
                        if gi + 1 < len(groups1):
                            e_prep(gi + 1)
                P.barrier()
                if stop_after == "E":
                    return finish_build()

                with contextlib.ExitStack() as st:
                    win = kb.sb(st, "win1q", [128, 8, D], BF16)
                    for k in range(8):
                        wload([qn], win[:, k, :], win1_d[k * 128:(k + 1) * 128, 0:1024], D, win, k)
                    WIN_R = [(win, k) for k in range(8)]
                    m01 = kb.sb(st, "m01_sb", [128, 2], F32)
                    P.dma("sp", m01[:], m01_d, writes=[(m01, A)])
                    xa = kb.sb(st, "hxa", [128, 4, D], F32)
                    xbb = kb.sb(st, "hxb", [128, D], F32)
                    rq = [kb.sb(st, f"hrq{i}", [128, 4, 128], F32) for i in range(2)]
                    QT1 = [kb.sb(st, f"hQT{i}", [128, 8, 512], BF16) for i in range(2)]
                    qb16 = kb.sb(st, "hqb16", [128, D], BF16)
                    PT = [kb.sb(st, f"hPT{i}", [128, 512], BF16) for i in range(4)]
                    rsb = kb.sb(st, "hrsb", [128, 512], F32)
                    ao = [kb.sb(st, f"hao{i}", [128, 512], BF16) for i in range(2)]
                    SC1 = 128.0 ** -0.5
                    SB_ = [0, 1, 7]

                    def f_prep(bi):
                        e0, n = eblocks[bi]
                        P.dma("sp", xa[:, 0:n, :], xs2_d[e0:e0 + n * 128, :].rearrange("(j p) d -> p j d", p=128),
                              writes=[(xa, j) for j in range(n)])
                        P.dma("sp", rq[bi % 2][:, 0:n, :], ropeC_q_d[e0:e0 + n * 128, :].rearrange("(j p) c -> p j c", p=128),
                              writes=[(rq[bi % 2], A)])
                        for j in range(n):
                            P.dma("sp", xbb[:], xs2_d[HALF + e0 + j * 128:HALF + e0 + (j + 1) * 128, :], writes=[(xbb, A)])
                            P.v("dve", "tensor_scalar", xa[:, j, :], xa[:, j, :], m01[:, 0:1], None, ALU.mult,
                                reads=[(xa, j), (m01, A)], writes=[(xa, j)])
                            P.v("dve", "scalar_tensor_tensor", xa[:, j, :], xbb[:], m01[:, 1:2], xa[:, j, :], ALU.mult,
                                ALU.add, reads=[(xbb, A), (xa, j), (m01, A)], writes=[(xa, j)])
                        P.dma("pool", xE_d[e0:e0 + n * 128, :].rearrange("(j p) d -> p j d", p=128), xa[:, 0:n, :],
                              reads=[(xa, j) for j in range(n)])
                        rms_prep(xa, [(j, 128) for j in range(n)], mean[bi % 2], rstd[bi % 2], xn, junk)
                        trans_mod(xn, [(j, 0, 128, j * 128) for j in range(n)], [(0, 0, n * 128)], hT[bi % 2], 0, 0, 0)
                        h_ = hT[bi % 2]
                        q_ = QT1[bi % 2]
                        for j in range(n):
                            pp, ppk = bank(2, 2)
                            for hf in range(2):
                                for k in range(8):
                                    P.mm(pp[:, hf * 512:(hf + 1) * 512], h_[:, k, j * 128:(j + 1) * 128],
                                         win[:, k, hf * 512:(hf + 1) * 512], start=(k == 0), stop=(k == 7),
                                         reads=[(h_, A)] + WIN_R, writes=[ppk[hf]])
                            head_norm_rope(pp, ppk, 8, gqb, rq[bi % 2][:, j, :], [(rq[bi % 2], A)], qb16[:], [(qb16, A)])
                            pv, pkq = bank_bf(6, 1)
                            pv3 = pv.rearrange("p (h t) -> p h t", h=8)
                            for hh in range(8):
                                P.tr(pv3[:, hh, :], qb16[:, hh * 128:(hh + 1) * 128], ident[:],
                                     reads=[(qb16, A), (ident, A)], writes=pkq)
                            P.v("dve", "tensor_copy", q_[:, :, j * 128:(j + 1) * 128], pv3, reads=pkq, writes=[(q_, A)])

                    f_prep(0)
                    for bi, (e0, n) in enumerate(eblocks):
                        nq = n * 128
                        q_ = QT1[bi % 2]
                        for h in range(8):
                            kvh = h // 4
                            po_, pok = bank(4)
                            psm, psmk = bank(5)

                            def qk(jj):
                                ps_, psk = bank(SB_[jj % 3])
                                P.mm(ps_[:, 0:nq], KT1[:, kvh, jj * 128:(jj + 1) * 128], q_[:, h, 0:nq], start=True,
                                     stop=True, reads=[(KT1, jj // 4), (q_, A)], writes=psk)

                            def ex_pv(jj):
                                ps_, psk = bank(SB_[jj % 3])
                                pt_ = PT[jj % 4]
                                P.act(pt_[:, 0:nq], ps_[:, 0:nq], AF.Exp, scale=SC1, reads=psk, writes=[(pt_, A)])
                                P.mm(po_[:, 0:nq], V1[:, jj, kvh * 128:(kvh + 1) * 128], pt_[:, 0:nq], start=(jj == 0),
                                     stop=(jj == NT - 1), reads=[(V1, jj), (pt_, A)], writes=pok)
                                P.mm(psm[:, 0:nq], ones_b[:], pt_[:, 0:nq], start=(jj == 0), stop=(jj == NT - 1),
                                     reads=[(ones_b, A), (pt_, A)], writes=psmk)

                            qk(0)
                            qk(1)
                            for jj in range(NT):
                                if jj + 2 < NT:
                                    qk(jj + 2)
                                ex_pv(jj)
                                if h == 3 and jj == 8 and bi + 1 < len(eblocks):
                                    f_prep(bi + 1)
                            a_ = ao[h % 2]
                            P.v("dve", "reciprocal", rsb[:, 0:nq], psm[:, 0:nq], reads=psmk, writes=[(rsb, A)])
                            P.v("dve", "tensor_tensor", a_[:, 0:nq], po_[:, 0:nq], rsb[:, 0:nq], ALU.mult,
                                reads=pok + [(rsb, A)], writes=[(a_, A)])
                            P.dma("pool", aoT1_d[h, :, e0:e0 + nq], a_[:, 0:nq], reads=[(a_, A)])
                P.barrier()
            if stop_after == "F":
                return finish_build()

            with contextlib.ExitStack() as st:
                stg = [kb.sb(st, f"jstg{i}", [128, D], F32) for i in range(2)]
                wo = kb.sb(st, "wout1", [128, 8, D], BF16)
                for k in range(8):
                    wload(stg, wo[:, k, :], wout1_d[k * 128:(k + 1) * 128, :], D, wo, k)
                WO_R = [(wo, k) for k in range(8)]
                aog = [kb.sb(st, f"jaog{i}", [128, 8, 512], BF16) for i in range(2)]
                xg = [kb.sb(st, f"jxg{i}", [128, 4, D], F32) for i in range(2)]
                tmp = [kb.sb(st, f"jtmp{i}", [128, D], F32) for i in range(2)]
                xo = [kb.sb(st, f"jxo{i}", [128, D], F32) for i in range(2)]
                for bi, (e0, n) in enumerate(eblocks):
                    ntok = n * 128
                    ab, xb = aog[bi % 2], xg[bi % 2]
                    P.dma("sp", ab[:, :, 0:ntok], aoT1_d[:, :, e0:e0 + ntok].rearrange("c p t -> p c t"), writes=[(ab, A)])
                    P.dma("sp", xb[:, 0:n, :], xE_d[e0:e0 + ntok, :].rearrange("(j p) d -> p j d", p=128), writes=[(xb, A)])
                    for j in range(n):
                        pv, pk = bank((j % 2) * 2 + (4 if (j // 2) % 2 else 0), 2)
                        for hf in range(2):
                            for c in range(8):
                                P.mm(pv[:, hf * 512:(hf + 1) * 512], ab[:, c, j * 128:(j + 1) * 128],
                                     wo[:, c, hf * 512:(hf + 1) * 512], start=(c == 0), stop=(c == 7),
                                     reads=[(ab, A)] + WO_R, writes=[pk[hf]])
                        t_, o_ = tmp[j % 2], xo[j % 2]
                        P.v("dve", "tensor_tensor", t_[:], pv, G["gates"][:, 0, 0, :], ALU.mult,
                            reads=pk + [(G["gates"], (0, 0))], writes=[(t_, A)])
                        P.v("pool", "tensor_tensor", o_[:], t_[:], xb[:, j, :], ALU.add, reads=[(t_, A), (xb, A)],
                            writes=[(o_, A)])
                        P.dma("pool", xs3_d[e0 + j * 128:e0 + (j + 1) * 128, :], o_[:], reads=[(o_, A)])
            P.barrier()
            if stop_after == "G":
                return finish_build()

            wins = []
            nw = HALF // 256
            for wi in range(nw):
                e0 = 128 + wi * 256
                wins.append(dict(src=xs3_d[e0:e0 + 256, :], res=xs3_d[e0:e0 + 256, :], mid=xmid_d[e0:e0 + 256, :], which=0,
                                 prev=xs3_d[e0 - 1:e0, :], next=xs3_d[e0 + 256:e0 + 257, :],
                                 pmask=("A" if wi == 0 else None), nmask=("B" if wi == nw - 1 else None),
                                 dst=out_d[wi * 256:(wi + 1) * 256, :]))
            ffn_phase(1, wins, final=True)

        return finish_build()


def _rope_tables(n_tokens, rope_dim, grid_w=64, theta=10000.0):
    rows = n_tokens // grid_w
    row = np.repeat(np.arange(rows, dtype=np.float32), grid_w)
    col = np.tile(np.arange(grid_w, dtype=np.float32), rows)
    n_freq = rope_dim // 4
    freq = (np.float32(theta) ** (-np.arange(n_freq, dtype=np.float32) / np.float32(n_freq))).astype(np.float32)
    ang = np.concatenate([row[:, None] * freq, col[:, None] * freq], axis=-1).astype(np.float32)
    return np.cos(ang).astype(np.float32), np.sin(ang).astype(np.float32)


def _invcnt(T):
    t = np.arange(T)
    out = np.zeros((4, T), np.float32)
    for g, w in enumerate((2, 4, 8, 16)):
        lo = np.clip(t - w // 2, 0, T)
        hi = np.clip(t - w // 2 + w, 0, T)
        out[g] = (np.float32(1.0) / (hi - lo).astype(np.float32)).astype(np.float32)
    return out


_CONST = {}


def _consts():
    if _CONST:
        return _CONST
    cA, sA = _rope_tables(S, 32)
    cC, sC = _rope_tables(S, 128)
    tokA = np.zeros((NTOK, 32), np.float32)
    tokA[:S, :16] = cA
    tokA[:S, 16:] = sA
    tokA[S:, :16] = 1.0
    featc = np.ones((32, NTOK), np.float32)
    feats = np.zeros((32, NTOK), np.float32)
    for i in range(16):
        featc[2 * i, :S] = cA[:, i]
        featc[2 * i + 1, :S] = cA[:, i]
        feats[2 * i, :S] = -sA[:, i]
        feats[2 * i + 1, :S] = sA[:, i]
    tokC = np.zeros((NTOK, 128), np.float32)
    tokC[:S, :64] = cC
    tokC[:S, 64:] = sC
    tokC[S:, :64] = 1.0
    _CONST.update(ropeA_tok=tokA, ropeA_c=featc, ropeA_s=feats, ropeC_k=tokC, cC=cC, sC=sC,
                  invcnt=_invcnt(S), invcnt_ctx=_invcnt(CT),
                  ident=np.eye(128, dtype=np.float32).astype(ml_dtypes.bfloat16))
    return _CONST


def host_inputs(inputs, core, mode="full"):
    K = _consts()
    b, h = core // 2, core % 2
    f = lambda a: np.ascontiguousarray(np.asarray(a, dtype=np.float32))
    m = {}
    m["ident"] = K["ident"]
    cc = np.zeros((128, 8, 2), np.float32)
    cc[:, :, 0] = f(inputs["c"])[b].reshape(8, 128).T
    cc[:, :, 1] = f(inputs["c_ctx"]).reshape(8, 128).T
    m["cc"] = cc
    m["w_mod"] = f(inputs["w_mod"])
    m["b_mod"] = f(inputs["b_mod"])
    m["b_modT"] = np.ascontiguousarray(f(inputs["b_mod"]).reshape(2, 48, 128).transpose(0, 2, 1))
    m["ffn_w_up"] = f(inputs["ffn_w_up"])
    m["ffn_w_down"] = f(inputs["ffn_w_down"])
    m["conv_wT"] = np.ascontiguousarray(f(inputs["ffn_conv_w"]).reshape(2, 3, NFC, 128).transpose(0, 3, 1, 2))
    m["conv_bT"] = np.ascontiguousarray(f(inputs["ffn_conv_b"]).reshape(2, NFC, 128).transpose(0, 2, 1))
    if mode in ("full", "l0"):
        m["x"] = f(inputs["x"])[b]
        m["ctx"] = f(inputs["ctx"])[b]
        m["mix0_w_in"] = f(inputs["mix0_w_in"])[0]
        wuq = f(inputs["mla_w_uq"])[0]
        m["w_uq"] = wuq
        sw = wuq.copy()
        for hh in range(8):
            base = hh * 96 + 64
            sw[:, base:base + 32:2] = wuq[:, base + 1:base + 32:2]
            sw[:, base + 1:base + 32:2] = wuq[:, base:base + 32:2]
        m["w_uq_sw"] = sw
        m["g_q0T"] = np.ascontiguousarray(f(inputs["mla_g_q"])[0].reshape(3, 128).T)
        m["g_kv0T"] = np.ascontiguousarray(f(inputs["mla_g_kv"])[0].reshape(2, 128).T)
        m["w_uk"] = f(inputs["mla_w_uk"])[0]
        m["w_uv"] = f(inputs["mla_w_uv"])[0]
        m["pool_w"] = f(inputs["pool_w"])[0]
        m["pool_sT"] = np.ascontiguousarray(f(inputs["pool_scale"])[0].reshape(4, 128).T)
        m["mix0_w_out"] = f(inputs["mix0_w_out"])[0]
        m["ropeA_tok"] = K["ropeA_tok"]
        m["ropeA_c"] = K["ropeA_c"]
        m["ropeA_s"] = K["ropeA_s"]
        m["invcnt"] = K["invcnt"]
        m["invcnt_ctx"] = K["invcnt_ctx"]
    if mode in ("full", "l1"):
        m["gqa_w_in"] = f(inputs["gqa_w_in"])[0]
        m["gqa_g_q"] = f(inputs["gqa_g_q"])[0]
        m["gqa_g_k"] = f(inputs["gqa_g_k"])[0]
        m["gqa_w_out"] = f(inputs["gqa_w_out"])[0]
        m["g_final"] = f(inputs["g_final"])
        m["ropeC_k"] = K["ropeC_k"]
        rq = np.zeros((EN, 128), np.float32)
        rq[:, :64] = 1.0
        tok = (h * 32) * 128 - 128 + np.arange(EN)
        ok = (tok >= 0) & (tok < S)
        rq[ok, :64] = K["cC"][tok[ok]]
        rq[ok, 64:] = K["sC"][tok[ok]]
        m["ropeC_q"] = rq
        m01 = np.zeros((128, 2), np.float32)
        m01[:, 0] = 1.0 - h
        m01[:, 1] = float(h)
        m["m01"] = m01
        mab = np.zeros((128, 2), np.float32)
        mab[:, 0] = float(h == 1)
        mab[:, 1] = float(h == 0)
        m["mAB"] = mab
    return m


_PROG = {}


def kernel(**inputs):
    if "full" not in _PROG:
        _PROG["full"] = build("full")
    kb = _PROG["full"]
    n = 8
    in_maps = [host_inputs(inputs, c, "full") for c in range(n)]
    res = run_bass_kernel_spmd(kb.nc, in_maps, core_ids=list(range(n)))
    B = 4
    out = np.zeros((B, S, D), np.float32)
    for c in range(n):
        b, h = c // 2, c % 2
        out[b, h * HALF:(h + 1) * HALF] = res.results[c]["out"]
    return out
```

```python
import contextlib
import numpy as np
import ml_dtypes
import concourse.bass as bass
import concourse.mybir as mybir
from concourse.bass_utils import run_bass_kernel_spmd

F32 = mybir.dt.float32
BF16 = mybir.dt.bfloat16
AF = mybir.ActivationFunctionType
ALU = mybir.AluOpType
AX = mybir.AxisListType
ALLK = "__all__"

D = 1024
S = 8192
CT = 256
NTOK = S + CT
NT = NTOK // 128
DFF = 2816
NFC = DFF // 128
EPS = 1e-6
HALF = S // 2
ET = HALF // 128 + 2
EN = ET * 128


class Buf:
    def __init__(self, t, name):
        self.t = t
        self.name = name
        self.st = {}

    def __getitem__(self, idx):
        return self.t[idx]


class Op:
    __slots__ = ("eng", "fn", "deps", "is_dma", "pos", "tok", "waits", "signal", "vc", "pre")

    def __init__(self, eng, fn, is_dma):
        self.eng = eng
        self.fn = fn
        self.deps = []
        self.is_dma = is_dma
        self.pos = -1
        self.tok = None
        self.waits = []
        self.signal = False
        self.vc = None
        self.pre = None


class Prog:
    ENGS = ("pe", "act", "dve", "pool", "sp")
    NDMA = 12

    def __init__(self, nc):
        self.nc = nc
        self.ops = []
        self.streams = {e: [] for e in self.ENGS}
        self.dma_count = {e: 0 for e in self.ENGS}
        self.dma_ops = {e: [] for e in self.ENGS}
        self.pending_dma = []

    def _deps(self, op, reads, writes):
        deps = set()

        def conflicts(buf, key):
            st = buf.st
            if key == ALLK:
                return list(st.values())
            out = []
            if key in st:
                out.append(st[key])
            if ALLK in st:
                out.append(st[ALLK])
            return out

        for (buf, key) in reads:
            for s in conflicts(buf, key):
                if s[0] is not None:
                    deps.add(s[0])
        for (buf, key) in writes:
            for s in conflicts(buf, key):
                if s[0] is not None:
                    deps.add(s[0])
                for r in s[1]:
                    deps.add(r)
        deps.discard(op)
        for (buf, key) in reads:
            s = buf.st.setdefault(key, [None, []])
            s[1].append(op)
        for (buf, key) in writes:
            if key == ALLK:
                buf.st.clear()
            buf.st[key] = [op, []]
        return deps

    def add(self, eng, fn, reads=(), writes=(), is_dma=False):
        op = Op(eng, fn, is_dma)
        deps = self._deps(op, list(reads), list(writes))
        if eng == "pe" and not is_dma:
            deps = {d for d in deps if not (d.eng == "pe" and not d.is_dma)}
        op.deps = sorted(deps, key=lambda o: o.pos)
        op.pos = len(self.ops)
        self.ops.append(op)
        self.streams[eng].append(op)
        if is_dma:
            i = self.dma_count[eng]
            self.dma_count[eng] += 1
            op.tok = (("dma", eng, i % self.NDMA), 16 * (i // self.NDMA + 1))
            if i >= self.NDMA:
                op.pre = self.dma_ops[eng][i - self.NDMA]
            self.dma_ops[eng].append(op)
            self.pending_dma.append(op)
        return op

    def barrier(self):
        lasts = [s[-1] for s in self.streams.values() if s]
        deps = lasts + self.pending_dma
        self.pending_dma = []
        for e in self.ENGS:
            op = self.add(e, lambda en: en.nop())
            op.deps = sorted(set(deps) - {op}, key=lambda o: o.pos)

    def dma(self, q, out, in_, reads=(), writes=()):
        return self.add(q, lambda e: e.dma_start(out=out, in_=in_), reads, writes, is_dma=True)

    def mm(self, out, lhsT, rhs, start, stop, reads=(), writes=()):
        return self.add("pe", lambda e: e.matmul(out, lhsT, rhs, start=start, stop=stop), reads, writes)

    def tr(self, out, in_, ident, reads=(), writes=()):
        return self.add("pe", lambda e: e.transpose(out, in_, ident), reads, writes)

    def act(self, out, in_, func, bias=None, scale=None, accum_out=None, reads=(), writes=()):
        kw = {}
        if bias is not None:
            kw["bias"] = bias
        if scale is not None:
            kw["scale"] = scale
        if accum_out is not None:
            kw["accum_out"] = accum_out
        return self.add("act", lambda e: e.activation(out, in_, func, **kw), reads, writes)

    def v(self, eng, name, *args, reads=(), writes=(), **kw):
        return self.add(eng, lambda e: getattr(e, name)(*args, **kw), reads, writes)

    def lower(self):
        known = {e: {} for e in self.ENGS}
        for op in self.ops:
            kn = known[op.eng]
            deps = list(op.deps)
            if op.pre is not None:
                deps.append(op.pre)
            for d in deps:
                key = d.tok[0] if d.is_dma else ("eng", d.eng)
                val = d.tok[1] if d.is_dma else d.pos
                if kn.get(key, -1) >= val:
                    continue
                op.waits.append(d)
                d.signal = True
                for k2, v2 in d.vc.items():
                    if kn.get(k2, -1) < v2:
                        kn[k2] = v2
                kn[key] = max(kn.get(key, -1), val)
            vc = dict(kn)
            if op.is_dma:
                vc[op.tok[0]] = max(vc.get(op.tok[0], -1), op.tok[1])
            else:
                vc[("eng", op.eng)] = op.pos
            op.vc = vc
        cnt = {e: 0 for e in self.ENGS}
        for op in self.ops:
            if op.is_dma:
                continue
            if op.signal:
                cnt[op.eng] += 1
                op.tok = (("eng", op.eng), cnt[op.eng])
        for op in self.ops:
            op.vc = None

    def emit(self, stack):
        nc = self.nc
        self.lower()
        sems = {}

        def getsem(key):
            if key not in sems:
                sems[key] = stack.enter_context(nc.semaphore("s_" + "_".join(str(x) for x in key)))
            return sems[key]

        for op in self.ops:
            if op.is_dma or op.signal:
                getsem(op.tok[0])
        block = stack.enter_context(nc.Block())

        def run_stream(ename, eobj):
            for op in self.streams[ename]:
                for d in op.waits:
                    eobj.wait_ge(getsem(d.tok[0]), d.tok[1])
                ins = op.fn(eobj)
                if op.is_dma:
                    ins.then_inc(getsem(op.tok[0]), 16)
                elif op.signal:
                    ins.then_inc(getsem(op.tok[0]), 1)

        @block.tensor
        def _(e):
            run_stream("pe", e)

        @block.scalar
        def _(e):
            run_stream("act", e)

        @block.vector
        def _(e):
            run_stream("dve", e)

        @block.gpsimd
        def _(e):
            run_stream("pool", e)

        @block.sync
        def _(e):
            run_stream("sp", e)


class KB:
    def __init__(self, mode, dbg=False):
        self.mode = mode
        self.dbg = dbg
        self.nc = bass.Bass("TRN2", target_bir_lowering=False)
        self.P = Prog(self.nc)
        self.din = {}
        self.dout = {}
        self.rr = 0

    def inp(self, name, shape, dt=F32):
        t = self.nc.dram_tensor(name, list(shape), dt, kind="ExternalInput").ap()
        self.din[name] = t
        return t

    def outp(self, name, shape, dt=F32):
        t = self.nc.dram_tensor(name, list(shape), dt, kind="ExternalOutput").ap()
        self.dout[name] = t
        return t

    def scratch(self, name, shape, dt=F32, external=None):
        if external == "in":
            return self.inp(name, shape, dt)
        if external == "out" or self.dbg:
            return self.outp(name, shape, dt)
        return self.nc.dram_tensor(name, list(shape), dt).ap()

    def sb(self, st, name, shape, dt):
        self.rr += 1
        name = f"{name}_{self.rr}"
        return Buf(st.enter_context(self.nc.sbuf_tensor(name, list(shape), dt)), name)


def build(mode="full", dbg=False, stop_after=None):
    kb = KB(mode, dbg)
    nc, P = kb.nc, kb.P
    do0 = mode in ("full", "l0")
    do1 = mode in ("full", "l1")
    A = ALLK

    ident_d = kb.inp("ident", [128, 128], BF16)
    cc_d = kb.inp("cc", [128, 8, 2])
    wmod_d = kb.inp("w_mod", [2, D, 6 * D])
    bmod_d = kb.inp("b_mod", [2, 6 * D])
    bmodT_d = kb.inp("b_modT", [2, 128, 48])
    wup_d = kb.inp("ffn_w_up", [2, D, 2 * DFF])
    wdn_d = kb.inp("ffn_w_down", [2, DFF, D])
    convw_d = kb.inp("conv_wT", [2, 128, 3, NFC])
    convb_d = kb.inp("conv_bT", [2, 128, NFC])
    if do0:
        x_d = kb.inp("x", [S, D])
        ctx_d = kb.inp("ctx", [CT, D])
        win0_d = kb.inp("mix0_w_in", [D, 1184])
        wuq_d = kb.inp("w_uq", [384, 768])
        wuqs_d = kb.inp("w_uq_sw", [384, 768])
        gq0_d = kb.inp("g_q0T", [128, 3])
        gkv0_d = kb.inp("g_kv0T", [128, 2])
        wuk_d = kb.inp("w_uk", [256, 512])
        wuv_d = kb.inp("w_uv", [256, 512])
        poolw_d = kb.inp("pool_w", [4, 128, 128])
        pools_d = kb.inp("pool_sT", [128, 4])
        wout0_d = kb.inp("mix0_w_out", [D, D])
        ropeA_tok_d = kb.inp("ropeA_tok", [NTOK, 32])
        ropeA_c_d = kb.inp("ropeA_c", [32, NTOK])
        ropeA_s_d = kb.inp("ropeA_s", [32, NTOK])
        invc_d = kb.inp("invcnt", [4, S])
        invcc_d = kb.inp("invcnt_ctx", [4, CT])
    if do1:
        win1_d = kb.inp("gqa_w_in", [D, 1536])
        gq1_d = kb.inp("gqa_g_q", [128])
        gk1_d = kb.inp("gqa_g_k", [128])
        wout1_d = kb.inp("gqa_w_out", [D, D])
        gfin_d = kb.inp("g_final", [D])
        ropeC_k_d = kb.inp("ropeC_k", [NTOK, 128])
        ropeC_q_d = kb.inp("ropeC_q", [EN, 128])
        m01_d = kb.inp("m01", [128, 2])
        mAB_d = kb.inp("mAB", [128, 2])
        out_d = kb.outp("out", [HALF, D])

    if do0:
        cqnT_d = kb.scratch("cqnT", [3, 128, NTOK], BF16)
        pT_d = kb.scratch("pT", [4, 128, S + 16])
        pTc_d = kb.scratch("pTc", [4, 128, CT + 16])
        aoT_d = kb.scratch("aoT", [8, 128, NTOK], BF16)
        xs1_d = kb.scratch("xs1", [NTOK, D])
    ext = "out" if mode == "l0" else ("in" if mode == "l1" else None)
    xs2_d = kb.scratch("xs2", [S + 256, D], external=ext)
    xcs2_d = kb.scratch("xcs2", [CT, D], external=ext)
    xmid_d = kb.scratch("xmid", [NTOK, D])
    if do1:
        xs3_d = kb.scratch("xs3", [EN, D])
        xE_d = kb.scratch("xE", [EN, D])
        aoT1_d = kb.scratch("aoT1", [8, 128, EN], BF16)

    with contextlib.ExitStack() as top:
        def finish_build():
            P.barrier()
            P.emit(top)
            return kb

        ident = kb.sb(top, "ident_sb", [128, 128], BF16)
        ones_f = kb.sb(top, "ones_f", [128, 128], F32)
        ones_b = kb.sb(top, "ones_b", [128, 128], BF16)
        cc = kb.sb(top, "cc_sb", [128, 8, 2], F32)
        sc2 = kb.sb(top, "sc2", [128, 8, 2], F32)
        modv = kb.sb(top, "modv", [128, 6, 8, 2], F32)
        G = {}
        psA = Buf(top.enter_context(nc.psum_tensor("psA", [128, 2048], F32)), "psA")
        psB = Buf(top.enter_context(nc.psum_tensor("psB", [128, 2048], F32)), "psB")

        def bank(i, n=1):
            assert (i % 4) + n <= 4
            b = psA if i < 4 else psB
            o = (i % 4) * 512
            return b[:, o:o + 512 * n], [(b, i + q) for q in range(n)]

        def bank_bf(i, n=1):
            ap, keys = bank(i, n)
            return ap.bitcast(BF16), keys

        P.dma("sp", ident[:], ident_d, writes=[(ident, A)])
        P.dma("sp", cc[:], cc_d, writes=[(cc, A)])
        P.v("pool", "memset", ones_f[:], 1.0, writes=[(ones_f, A)])
        P.v("pool", "memset", ones_b[:], 1.0, writes=[(ones_b, A)])
        P.act(sc2[:], cc[:], AF.Silu, reads=[(cc, A)], writes=[(sc2, A)])

        cast_rr = [0]

        def wload(stg_bufs, dst_ap, src_ap, n, dst_buf, dst_key, scale_ap=None, scale_reads=()):
            i = cast_rr[0]
            cast_rr[0] += 1
            stg = stg_bufs[i % len(stg_bufs)]
            P.dma("sp", stg[:, 0:n], src_ap, writes=[(stg, A)])
            if scale_ap is not None:
                P.v("dve", "tensor_scalar", dst_ap, stg[:, 0:n], scale_ap, None, ALU.mult,
                    reads=[(stg, A)] + list(scale_reads), writes=[(dst_buf, dst_key)])
            else:
                eng = "pool" if i % 2 == 0 else "dve"
                P.v(eng, "tensor_copy", dst_ap, stg[:, 0:n], reads=[(stg, A)], writes=[(dst_buf, dst_key)])

        def mod_phase(l, need_ctx_gates):
            with contextlib.ExitStack() as st:
                wm = [kb.sb(st, f"wm{i}", [128, 8, D], F32) for i in range(2)]
                bmT = kb.sb(st, "bmT", [128, 48], F32)
                bmb = [kb.sb(st, f"bmb{i}", [128, D], F32) for i in range(2)]
                scb = kb.sb(st, "scb", [128, 8, 2, 128], F32)
                P.dma("sp", bmT[:], bmodT_d[l], writes=[(bmT, A)])
                for k in range(8):
                    for w in range(2):
                        P.v("dve", "tensor_copy", scb[:, k, w, :], sc2[:, k, w:w + 1].to_broadcast([128, 128]),
                            reads=[(sc2, A)], writes=[(scb, (k, w))])
                for mi, m in enumerate((0, 1, 3, 4, 2, 5)):
                    wb = wm[mi % 2]
                    P.dma("sp", wb[:], wmod_d[l, :, m * D:(m + 1) * D].rearrange("(k p) n -> p k n", p=128),
                          writes=[(wb, A)])
                    if m in (0, 1, 3, 4):
                        pv, pk = bank(mi % 2)
                        pv3 = pv[:, 0:16].rearrange("p (j w) -> p j w", w=2)
                        for j in range(8):
                            for k in range(8):
                                P.mm(pv3[:, j, :], wb[:, k, j * 128:(j + 1) * 128], sc2[:, k, :], start=(k == 0),
                                     stop=(k == 7), reads=[(wb, A), (sc2, A)], writes=pk)
                        P.v("dve", "tensor_tensor", modv[:, m, :, :], pv3,
                            bmT[:, m * 8:(m + 1) * 8].unsqueeze(2).to_broadcast([128, 8, 2]), ALU.add,
                            reads=pk + [(bmT, A)], writes=[(modv, m)])
                        if m in (1, 4):
                            P.v("dve", "tensor_scalar_add", modv[:, m, :, :], modv[:, m, :, :], 1.0,
                                reads=[(modv, m)], writes=[(modv, m)])
                    else:
                        gi = 0 if m == 2 else 1
                        bb = bmb[gi]
                        P.dma("sp", bb[:], bmod_d[l, m * D:(m + 1) * D].partition_broadcast(128), writes=[(bb, A)])
                        for w in range(2 if need_ctx_gates else 1):
                            pv, pk = bank(2 + 2 * w if w == 0 else 4, 2)
                            for hf in range(2):
                                for k in range(8):
                                    P.mm(pv[:, hf * 512:(hf + 1) * 512], scb[:, k, w, :],
                                         wb[:, k, hf * 512:(hf + 1) * 512], start=(k == 0), stop=(k == 7),
                                         reads=[(scb, (k, w)), (wb, A)], writes=[pk[hf]])
                            P.v("dve", "tensor_tensor", G["gates"][:, w, gi, :], pv, bb[:], ALU.add,
                                reads=pk + [(bb, A)], writes=[(G["gates"], (w, gi))])
            P.barrier()

        def rms_prep(xt, tiles, mean, rstd, xn, junk, dim_scale=1.0 / 32.0):
            nt_ = len(tiles)
            for (j, p_) in tiles:
                P.act(junk[0:p_, :], xt[0:p_, j, :], AF.Square, scale=dim_scale, accum_out=mean[0:p_, j:j + 1],
                      reads=[(xt, j)], writes=[(junk, A), (mean, j)])
            jmax = max(j for j, _ in tiles) + 1
            P.act(rstd[:, 0:jmax], mean[:, 0:jmax], AF.Ln, bias=EPS, reads=[(mean, j) for j, _ in tiles],
                  writes=[(rstd, A)])
            P.act(rstd[:, 0:jmax], rstd[:, 0:jmax], AF.Exp, scale=-0.5, reads=[(rstd, A)], writes=[(rstd, A)])
            for (j, p_) in tiles:
                P.act(xn[0:p_, j, :], xt[0:p_, j, :], AF.Copy, scale=rstd[0:p_, j:j + 1],
                      reads=[(xt, j), (rstd, A)], writes=[(xn, j)])

        def trans_mod(xn, trans, evac, hT, which, mset, pbase, halves=(0, 1), act_main=False):
            m_shift, m_scale = (0, 1) if mset == 0 else (3, 4)
            for half in halves:
                pv, pk = bank_bf(pbase, 2)
                pv3 = pv.rearrange("p (k t) -> p k t", k=4)
                for (j, p0, np_, dc) in trans:
                    for kk in range(4):
                        k = half * 4 + kk
                        P.tr(pv3[:, kk, dc:dc + np_], xn[p0:p0 + np_, j, k * 128:(k + 1) * 128],
                             ident[p0:p0 + np_, p0:p0 + np_], reads=[(xn, j), (ident, A)], writes=pk)
                for kk in range(4):
                    k = half * 4 + kk
                    for (sc_, dc_, nc_) in evac:
                        if act_main and nc_ >= 64:
                            P.act(hT[:, k, dc_:dc_ + nc_], pv3[:, kk, sc_:sc_ + nc_], AF.Identity,
                                  bias=modv[:, m_shift, k, which:which + 1], scale=modv[:, m_scale, k, which:which + 1],
                                  reads=pk + [(modv, m_scale), (modv, m_shift)], writes=[(hT, A)])
                        else:
                            P.v("dve", "tensor_scalar", hT[:, k, dc_:dc_ + nc_], pv3[:, kk, sc_:sc_ + nc_],
                                modv[:, m_scale, k, which:which + 1], modv[:, m_shift, k, which:which + 1], ALU.mult,
                                ALU.add, reads=pk + [(modv, m_scale), (modv, m_shift)], writes=[(hT, A)])

        def ffn_phase(l, windows, final=False):
            for half in range(2):
                ffn_pass(l, windows, final, half)

        def ffn_pass(l, windows, final, half):
            NH = NFC // 2
            HW = NH * 128
            last = (half == 1)
            with contextlib.ExitStack() as st:
                stg = [kb.sb(st, f"fstg{i}", [128, HW], F32) for i in range(2)]
                wup = kb.sb(st, "wup", [128, 8, 2 * HW], BF16)
                wdn = kb.sb(st, "wdn", [128, NH, D], BF16)
                cw = kb.sb(st, "cw", [128, 3, NFC], F32)
                cb = kb.sb(st, "cb", [128, NFC], F32)
                P.dma("sp", cw[:], convw_d[l], writes=[(cw, A)])
                P.dma("sp", cb[:], convb_d[l], writes=[(cb, A)])
                for k in range(8):
                    for gu_ in range(2):
                        c0 = gu_ * DFF + half * HW
                        wload(stg, wup[:, k, gu_ * HW:(gu_ + 1) * HW], wup_d[l, k * 128:(k + 1) * 128, c0:c0 + HW], HW,
                              wup, (k, gu_))
                for c in range(NH):
                    r0 = (half * NH + c) * 128
                    wload(stg, wdn[:, c, :], wdn_d[l, r0:r0 + 128, :], D, wdn, c)
                WUP_R = [(wup, (k, g2)) for k in range(8) for g2 in range(2)]
                WDN_R = [(wdn, c) for c in range(NH)]
                if final:
                    gfb = kb.sb(st, "gfb", [128, D], F32)
                    P.dma("sp", gfb[:], gfin_d.partition_broadcast(128), writes=[(gfb, A)])
                    mAB = kb.sb(st, "mAB_sb", [128, 2], F32)
                    P.dma("sp", mAB[:], mAB_d, writes=[(mAB, A)])
                xw = kb.sb(st, "fxw", [128, 3, D], F32)
                mean = [kb.sb(st, f"fmean{i}", [128, 4], F32) for i in range(2)]
                rstd = [kb.sb(st, f"frstd{i}", [128, 4], F32) for i in range(2)]
                fm2 = kb.sb(st, "fm2", [128, 4], F32)
                fr2 = kb.sb(st, "fr2", [128, 4], F32)
                xn = kb.sb(st, "fxn", [128, 3, D], BF16)
                junk = kb.sb(st, "fjunk", [128, D], BF16)
                hT = [kb.sb(st, f"fhT{i}", [128, 8, 258], BF16) for i in range(2)]
                acc = [kb.sb(st, f"facc{i}", [128, 256], F32) for i in range(3)]
                sil = [kb.sb(st, f"fsil{i}", [128, 256], F32) for i in range(2)]
                gu = kb.sb(st, "fgu", [128, NH, 256], BF16)
                tmp = [kb.sb(st, f"ftmp{i}", [128, D], F32) for i in range(4)]
                xo = [kb.sb(st, f"fxo{i}", [128, D], F32) for i in range(4)]
                xr = [kb.sb(st, f"fxr{i}", [128, D], F32) for i in range(2)]
                P.v("pool", "memset", xw[:], 0.0, writes=[(xw, A)])
                for mb_ in mean + rstd:
                    P.v("pool", "memset", mb_[:], 1.0, writes=[(mb_, A)])

                def prep_a(wi):
                    w = windows[wi]
                    P.dma("sp", xw[:, 0:2, :], w["src"].rearrange("(j p) d -> p j d", p=128),
                          writes=[(xw, 0), (xw, 1)])
                    tiles = [(0, 128), (1, 128)]
                    if w["prev"] is not None:
                        P.dma("sp", xw[0:1, 2, :], w["prev"], writes=[(xw, 2)])
                        P.dma("sp", xw[32:33, 2, :], w["next"], writes=[(xw, 2)])
                        tiles.append((2, 33))
                    rms_prep(xw, tiles, mean[wi % 2], rstd[wi % 2], xn, junk)

                def prep(wi):
                    prep_a(wi)
                    prep_b(wi)

                def prep_b(wi, halves=(0, 1)):
                    w = windows[wi]
                    h_ = hT[wi % 2]
                    trans = [(0, 0, 128, 0), (1, 0, 128, 128)]
                    evac = [(0, 1, 256)]
                    halo = w["prev"] is not None
                    if halo:
                        trans.append((2, 0, 33, 256))
                        evac += [(256, 0, 1), (288, 257, 1)]
                    trans_mod(xn, trans, evac, h_, w["which"], 1, 0, halves=halves, act_main=True)
                    for (mk, col) in ((w["pmask"], 0), (w["nmask"], 257)):
                        for k in [kk_ + 4 * hv for hv in halves for kk_ in range(4)]:
                            if (not halo) or mk == 0:
                                P.v("dve", "memset", h_[:, k, col:col + 1], 0.0, writes=[(h_, A)])
                            elif mk in ("A", "B"):
                                mi = 0 if mk == "A" else 1
                                P.v("dve", "tensor_scalar", h_[:, k, col:col + 1], h_[:, k, col:col + 1],
                                    mAB[:, mi:mi + 1], None, ALU.mult, reads=[(h_, A), (mAB, A)], writes=[(h_, A)])

                def finish(wi):
                    w = windows[wi]
                    which = w["which"]
                    dst = w["mid"] if half == 0 else w["dst"]
                    for j in range(2):
                        r_ = xr[j]
                        t_, o_ = tmp[(2 * wi + j) % 4], xo[(2 * wi + j) % 4]
                        for hf in range(2):
                            pv, pk = bank(hf)
                            for c in range(NH):
                                P.mm(pv, gu[:, c, j * 128:(j + 1) * 128],
                                     wdn[:, c, hf * 512:(hf + 1) * 512], start=(c == 0), stop=(c == NH - 1),
                                     reads=[(gu, c)] + WDN_R, writes=pk)
                            P.v("dve", "tensor_tensor", t_[:, hf * 512:(hf + 1) * 512], pv,
                                G["gates"][:, which, 1, hf * 512:(hf + 1) * 512], ALU.mult,
                                reads=pk + [(G["gates"], (which, 1))], writes=[(t_, hf)])
                        P.v("pool", "tensor_tensor", o_[:], t_[:], r_[:], ALU.add,
                            reads=[(t_, 0), (t_, 1), (r_, A)], writes=[(o_, A)])
                        if not (final and last):
                            P.dma("pool", dst[j * 128:(j + 1) * 128, :], o_[:], reads=[(o_, A)])

                def finish_b(wi):
                    if not (final and last):
                        return
                    w = windows[wi]
                    dst = w["dst"]
                    for j in range(2):
                        t_, o_ = tmp[(2 * wi + j) % 4], xo[(2 * wi + j) % 4]
                        mj = (2 * wi + j) % 4
                        P.act(junk[:], o_[:], AF.Square, scale=1.0 / 32.0, accum_out=fm2[:, mj:mj + 1],
                              reads=[(o_, A)], writes=[(junk, A), (fm2, mj)])
                        P.act(fr2[:, mj:mj + 1], fm2[:, mj:mj + 1], AF.Ln, bias=EPS, reads=[(fm2, mj)], writes=[(fr2, mj)])
                        P.act(fr2[:, mj:mj + 1], fr2[:, mj:mj + 1], AF.Exp, scale=-0.5, reads=[(fr2, mj)],
                              writes=[(fr2, mj)])
                        P.v("dve", "scalar_tensor_tensor", t_[:], o_[:], fr2[:, mj:mj + 1], gfb[:], ALU.mult, ALU.mult,
                            reads=[(o_, A), (fr2, mj), (gfb, A)], writes=[(t_, 0), (t_, 1)])
                        P.dma("pool", dst[j * 128:(j + 1) * 128, :], t_[:], reads=[(t_, 0), (t_, 1)])

                GB = [2, 3, 6]
                UB = [4, 5, 7]

                def st_pe(wi, c):
                    h_ = hT[wi % 2]
                    pg, pgk = bank(GB[c % 3])
                    pu, puk = bank(UB[c % 3])
                    for k in range(8):
                        P.mm(pg[:, 0:258], wup[:, k, c * 128:(c + 1) * 128], h_[:, k, 0:258], start=(k == 0),
                             stop=(k == 7), reads=[(h_, A)] + WUP_R, writes=pgk)
                    for k in range(8):
                        P.mm(pu[:, 0:256], wup[:, k, HW + c * 128:HW + (c + 1) * 128], h_[:, k, 1:257],
                             start=(k == 0), stop=(k == 7), reads=[(h_, A)] + WUP_R, writes=puk)

                def st_id(c):
                    cg = half * NH + c
                    pg, pgk = bank(GB[c % 3])
                    a_ = acc[c % 3]
                    P.act(a_[:], pg[:, 1:257], AF.Identity, bias=cb[:, cg:cg + 1], scale=cw[:, 1, cg:cg + 1],
                          reads=pgk + [(cb, A), (cw, A)], writes=[(a_, A)])

                def st_conv(c, which_tap):
                    cg = half * NH + c
                    pg, pgk = bank(GB[c % 3])
                    a_ = acc[c % 3]
                    if which_tap == 0:
                        P.v("dve", "scalar_tensor_tensor", a_[:], pg[:, 0:256], cw[:, 0, cg:cg + 1], a_[:], ALU.mult,
                            ALU.add, reads=pgk + [(cw, A), (a_, A)], writes=[(a_, A)])
                    else:
                        P.v("dve", "scalar_tensor_tensor", a_[:], pg[:, 2:258], cw[:, 2, cg:cg + 1], a_[:], ALU.mult,
                            ALU.add, reads=pgk + [(cw, A), (a_, A)], writes=[(a_, A)])

                def st_silu(c):
                    a_, s_ = acc[c % 3], sil[c % 2]
                    P.act(s_[:], a_[:], AF.Silu, reads=[(a_, A)], writes=[(s_, A)])

                def st_mult(c):
                    s_ = sil[c % 2]
                    pu, puk = bank(UB[c % 3])
                    P.v("dve", "tensor_tensor", gu[:, c, :], s_[:], pu[:, 0:256], ALU.mult,
                        reads=[(s_, A)] + puk, writes=[(gu, c)])

                import os as _os3
                _PA = int(_os3.environ.get("FFN_PA", "3"))
                _PB = int(_os3.environ.get("FFN_PB", "6"))
                _PB2 = int(_os3.environ.get("FFN_PB2", "9"))
                prep(0)
                for wi, w in enumerate(windows):
                    base = w["res"] if half == 0 else w["mid"]
                    for j in range(2):
                        P.dma("sp", xr[j][:], base[j * 128:(j + 1) * 128, :], writes=[(xr[j], A)])
                    for c in range(NH + 1):
                        if c < NH:
                            st_pe(wi, c)
                            st_id(c)
                            st_conv(c, 0)
                        if c >= 1:
                            st_silu(c - 1)
                            st_mult(c - 1)
                        if c < NH:
                            st_conv(c, 1)
                        if c == 2 and wi >= 1:
                            finish_b(wi - 1)
                        if c == _PA and wi + 1 < len(windows):
                            prep_a(wi + 1)
                        if c == _PB and wi + 1 < len(windows):
                            prep_b(wi + 1, halves=(0,))
                        if c == _PB2 and wi + 1 < len(windows):
                            prep_b(wi + 1, halves=(1,))
                    finish(wi)
                finish_b(len(windows) - 1)
            P.barrier()

        if do0:
          with contextlib.ExitStack() as lay0:
            G["gates"] = kb.sb(lay0, "gates0", [128, 2, 2, D], F32)
            mod_phase(0, True)
            if stop_after == "mod":
                return finish_build()

            with contextlib.ExitStack() as l0:
                ckvnT = kb.sb(l0, "ckvnT", [128, 2, NTOK], BF16)
                KTb = [kb.sb(l0, f"KT{i}", [96, NTOK], BF16) for i in range(2)]

                def src_rows(t0, n):
                    if t0 < S:
                        return x_d[t0:t0 + n, :]
                    return ctx_d[t0 - S:t0 - S + n, :]

                groups = [(g * 512, 4, 0) for g in range(16)] + [(S, 2, 1)]

                with contextlib.ExitStack() as st:
                    zero_f = kb.sb(st, "zero_f", [128, 1024], F32)
                    P.v("pool", "memset", zero_f[:], 0.0, writes=[(zero_f, A)])
                    for g_ in range(4):
                        P.dma("sp", pT_d[g_, :, 0:8], zero_f[:, 0:8], reads=[(zero_f, A)])
                        P.dma("sp", pT_d[g_, :, S + 8:S + 16], zero_f[:, 0:8], reads=[(zero_f, A)])
                        P.dma("sp", pTc_d[g_, :, 0:8], zero_f[:, 0:8], reads=[(zero_f, A)])
                        P.dma("sp", pTc_d[g_, :, CT + 8:CT + 16], zero_f[:, 0:8], reads=[(zero_f, A)])
                    P.dma("sp", xs2_d[0:128, :], zero_f[:], reads=[(zero_f, A)])
                    P.dma("sp", xs2_d[128 + S:256 + S, :], zero_f[:], reads=[(zero_f, A)])
                    stg = [kb.sb(st, f"astg{i}", [128, 1184], F32) for i in range(2)]
                    win = kb.sb(st, "win0", [128, 8, 1184], BF16)
                    for k in range(8):
                        wload(stg, win[:, k, :], win0_d[k * 128:(k + 1) * 128, :], 1184, win, k)
                    WIN_R = [(win, k) for k in range(8)]
                    ropeA = kb.sb(st, "ropeA", [128, NT, 32], F32)
                    for t8 in range(0, NT, 8):
                        n8 = min(8, NT - t8)
                        P.dma("sp", ropeA[:, t8:t8 + n8, :],
                              ropeA_tok_d[t8 * 128:(t8 + n8) * 128, :].rearrange("(t p) c -> p t c", p=128),
                              writes=[(ropeA, A)])
                    xg = [kb.sb(st, f"axg{i}", [128, 4, D], F32) for i in range(2)]
                    mean = [kb.sb(st, f"amean{i}", [128, 4], F32) for i in range(2)]
                    rstd = [kb.sb(st, f"arstd{i}", [128, 4], F32) for i in range(2)]
                    xn = kb.sb(st, "axn", [128, 4, D], BF16)
                    junk = kb.sb(st, "ajunk", [128, D], BF16)
                    hT = [kb.sb(st, f"ahT{i}", [128, 8, 512], BF16) for i in range(2)]
                    m2 = kb.sb(st, "am2", [128, 4, 2], F32)
                    r2 = kb.sb(st, "ar2", [128, 4, 2], F32)
                    cq_tok = kb.sb(st, "acq", [128, 4, 384], BF16)
                    kv_tok = kb.sb(st, "akv", [128, 4, 288], BF16)
                    rtmp = kb.sb(st, "artmp", [128, 8, 16], F32)
                    cqT = [kb.sb(st, f"acqT{i}", [128, 3, 512], BF16) for i in range(2)]
                    pst = [kb.sb(st, f"apst{i}", [128, 4, 512], F32) for i in range(1)]

                    def a_prep(gi):
                        t0, n, which = groups[gi]
                        xb = xg[gi % 2]
                        P.dma("sp", xb[:, 0:n, :], src_rows(t0, n * 128).rearrange("(j p) d -> p j d", p=128),
                              writes=[(xb, j) for j in range(n)])
                        rms_prep(xb, [(j, 128) for j in range(n)], mean[gi % 2], rstd[gi % 2], xn, junk)
                        trans_mod(xn, [(j, 0, 128, j * 128) for j in range(n)], [(0, 0, n * 128)], hT[gi % 2], which, 0, 0)

                    import os as _os
                    _ng = int(_os.environ.get("A_NG", "99"))
                    _parts = int(_os.environ.get("A_PARTS", "15"))
                    a_prep(0)
                    for gi, (t0, n, which) in enumerate(groups):
                        if gi >= _ng:
                            break
                        h_ = hT[gi % 2]
                        ntok = n * 128
                        for j in range(n if (_parts & 1) else 0):
                            tt = t0 // 128 + j
                            pq, pqk = bank(2)
                            pk_, pkk = bank(3)
                            for k in range(8):
                                P.mm(pq[:, 0:384], h_[:, k, j * 128:(j + 1) * 128], win[:, k, 0:384], start=(k == 0),
                                     stop=(k == 7), reads=[(h_, A)] + WIN_R, writes=pqk)
                            for k in range(8):
                                P.mm(pk_[:, 0:288], h_[:, k, j * 128:(j + 1) * 128], win[:, k, 384:672],
                                     start=(k == 0), stop=(k == 7), reads=[(h_, A)] + WIN_R, writes=pkk)
                            if not (int(_os.environ.get("A_SUB", "3")) & 1):
                                continue
                            P.act(junk[:, 0:384], pq[:, 0:384], AF.Square, scale=384.0 ** -0.5,
                                  accum_out=m2[:, j, 0:1], reads=pqk, writes=[(junk, A), (m2, j)])
                            P.act(junk[:, 0:256], pk_[:, 0:256], AF.Square, scale=1.0 / 16.0,
                                  accum_out=m2[:, j, 1:2], reads=pkk, writes=[(junk, A), (m2, j)])
                            P.act(r2[:, j, :], m2[:, j, :], AF.Ln, bias=EPS, reads=[(m2, j)], writes=[(r2, j)])
                            P.act(r2[:, j, :], r2[:, j, :], AF.Exp, scale=-0.5, reads=[(r2, j)], writes=[(r2, j)])
                            P.act(cq_tok[:, j, :], pq[:, 0:384], AF.Copy, scale=r2[:, j, 0:1],
                                  reads=pqk + [(r2, j)], writes=[(cq_tok, j)])
                            P.act(kv_tok[:, j, 0:256], pk_[:, 0:256], AF.Copy, scale=r2[:, j, 1:2],
                                  reads=pkk + [(r2, j)], writes=[(kv_tok, j)])
                            if not (int(_os.environ.get("A_SUB", "3")) & 2):
                                continue
                            kr32 = rtmp[:, 4:6, :].rearrange("p a i -> p (a i)")
                            P.act(kr32, pk_[:, 256:288], AF.Copy, reads=pkk, writes=[(rtmp, 4)])
                            kr = kr32.rearrange("p (i two) -> p i two", two=2)
                            cs, sn = ropeA[:, tt, 0:16], ropeA[:, tt, 16:32]
                            R_ = [(rtmp, 4), (ropeA, A)]
                            P.v("dve", "tensor_tensor", rtmp[:, 0, :], kr[:, :, 0], cs, ALU.mult, reads=R_, writes=[(rtmp, 0)])
                            P.v("dve", "tensor_tensor", rtmp[:, 1, :], kr[:, :, 1], sn, ALU.mult, reads=R_, writes=[(rtmp, 1)])
                            P.v("dve", "tensor_tensor", rtmp[:, 2, :], kr[:, :, 0], sn, ALU.mult, reads=R_, writes=[(rtmp, 2)])
                            P.v("dve", "tensor_tensor", rtmp[:, 3, :], kr[:, :, 1], cs, ALU.mult, reads=R_, writes=[(rtmp, 3)])
                            ro = rtmp[:, 6:8, :].rearrange("p a i -> p (a i)")
                            ro2 = ro.rearrange("p (i two) -> p i two", two=2)
                            P.v("dve", "tensor_tensor", ro2[:, :, 0], rtmp[:, 0, :], rtmp[:, 1, :], ALU.subtract,
                                reads=[(rtmp, 0), (rtmp, 1)], writes=[(rtmp, 6)])
                            P.v("dve", "tensor_tensor", ro2[:, :, 1], rtmp[:, 2, :], rtmp[:, 3, :], ALU.add,
                                reads=[(rtmp, 2), (rtmp, 3), (rtmp, 6)], writes=[(rtmp, 6)])
                            P.v("dve", "tensor_copy", kv_tok[:, j, 256:288], ro, reads=[(rtmp, 6)], writes=[(kv_tok, j)])
                        pv, pk6 = bank_bf(6, 2)
                        pv3 = pv[:, 0:1536].rearrange("p (c t) -> p c t", c=3)
                        for j in range(n if (_parts & 2) else 0):
                            for c in range(3):
                                P.tr(pv3[:, c, j * 128:(j + 1) * 128], cq_tok[:, j, c * 128:(c + 1) * 128], ident[:],
                                     reads=[(cq_tok, j), (ident, A)], writes=pk6)
                        cqb = cqT[gi % 2]
                        for c in range(3 if (_parts & 2) else 0):
                            P.v("dve", "tensor_copy", cqb[:, c, 0:ntok], pv3[:, c, 0:ntok], reads=pk6, writes=[(cqb, A)])
                        if _parts & 2:
                            P.dma("pool", cqnT_d[:, :, t0:t0 + ntok].rearrange("c p t -> p c t"), cqb[:, :, 0:ntok],
                                  reads=[(cqb, A)])
                        for j in range(n if (_parts & 4) else 0):
                            for c in range(2):
                                P.tr(pv3[:, c, j * 128:(j + 1) * 128], kv_tok[:, j, c * 128:(c + 1) * 128], ident[:],
                                     reads=[(kv_tok, j), (ident, A)], writes=pk6)
                            P.tr(pv3[0:96, 2, j * 128:(j + 1) * 128], kv_tok[:, j, 192:288], ident[:],
                                 reads=[(kv_tok, j), (ident, A)], writes=pk6)
                        for c in range(2 if (_parts & 4) else 0):
                            P.v("dve", "tensor_copy", ckvnT[:, c, t0:t0 + ntok], pv3[:, c, 0:ntok], reads=pk6,
                                writes=[(ckvnT, gi)])
                        if _parts & 4:
                            for KT_ in KTb:
                                P.v("dve", "tensor_copy", KT_[64:96, t0:t0 + ntok], pv3[64:96, 2, 0:ntok], reads=pk6,
                                    writes=[(KT_, ("r", gi))])
                        pb = pst[0]
                        for gq in range(4 if (_parts & 8) else 0):
                            pp, ppk = bank(4 + gq % 2)
                            for k in range(8):
                                P.mm(pp[:, 0:ntok], win[:, k, 672 + gq * 128:672 + (gq + 1) * 128], h_[:, k, 0:ntok],
                                     start=(k == 0), stop=(k == 7), reads=[(h_, A)] + WIN_R, writes=ppk)
                            P.act(pb[:, gq, 0:ntok], pp[:, 0:ntok], AF.Copy, reads=ppk, writes=[(pb, gq)])
                        if not (_parts & 8):
                            pass
                        elif which == 0:
                            P.dma("pool", pT_d[:, :, 8 + t0:8 + t0 + ntok].rearrange("g p t -> p g t"), pb[:, :, 0:ntok],
                                  reads=[(pb, g_) for g_ in range(4)])
                        else:
                            P.dma("pool", pTc_d[:, :, 8:8 + ntok].rearrange("g p t -> p g t"), pb[:, :, 0:ntok],
                                  reads=[(pb, g_) for g_ in range(4)])
                        if gi + 1 < len(groups) and gi + 1 < _ng:
                            a_prep(gi + 1)
                P.barrier()
                if stop_after == "A":
                    return finish_build()

                with contextlib.ExitStack() as st:
                    stg = [kb.sb(st, f"bstg{i}", [128, 128], F32) for i in range(2)]
                    pwb = kb.sb(st, "poolw", [128, 4, 128], BF16)
                    for g_ in range(4):
                        wload(stg, pwb[:, g_, :], poolw_d[g_], 128, pwb, g_)
                    psc = kb.sb(st, "pools", [128, 4], F32)
                    P.dma("sp", psc[:], pools_d, writes=[(psc, A)])
                    pw = [kb.sb(st, f"bpw{i}", [128, 4, 528], F32) for i in range(2)]
                    inv = [kb.sb(st, f"binv{i}", [128, 4, 512], F32) for i in range(2)]
                    t1 = [kb.sb(st, f"bt1_{i}", [128, 528], F32) for i in range(4)]
                    t2 = [kb.sb(st, f"bt2_{i}", [128, 528], F32) for i in range(4)]
                    dT = [kb.sb(st, f"bdT{i}", [128, 4, 512], BF16) for i in range(2)]
                    po = [kb.sb(st, f"bpo{i}", [128, 4, 512], BF16) for i in range(2)]
                    blocks = [(g * 512, 512, 0) for g in range(16)] + [(S, 256, 1)]
                    for bi, (t0, ntok, which) in enumerate(blocks):
                        p_, iv = pw[bi % 2], inv[bi % 2]
                        if which == 0:
                            P.dma("sp", p_[:, :, 0:ntok + 16], pT_d[:, :, t0:t0 + ntok + 16].rearrange("g p t -> p g t"),
                                  writes=[(p_, A)])
                            for g_ in range(4):
                                P.dma("sp", iv[:, g_, 0:ntok], invc_d[g_, t0:t0 + ntok].partition_broadcast(128),
                                      writes=[(iv, g_)])
                        else:
                            P.dma("sp", p_[:, :, 0:ntok + 16], pTc_d[:, :, 0:ntok + 16].rearrange("g p t -> p g t"),
                                  writes=[(p_, A)])
                            for g_ in range(4):
                                P.dma("sp", iv[:, g_, 0:ntok], invcc_d[g_, 0:ntok].partition_broadcast(128),
                                      writes=[(iv, g_)])
                        d_, o_ = dT[bi % 2], po[bi % 2]
                        for g_ in range(4):
                            w_ = 2 << g_
                            eng = "dve" if g_ % 2 == 1 else "pool"
                            cur, curlen, curbuf = p_[:, g_, 0:ntok + 16], ntok + 16, None
                            a_, b_ = t1[g_], t2[g_]
                            step = 1
                            while step < w_:
                                nl = curlen - step
                                dst = a_
                                P.v(eng, "tensor_tensor", dst[:, 0:nl], cur[:, 0:nl], cur[:, step:step + nl], ALU.add,
                                    reads=[(p_, A)] + ([(curbuf, A)] if curbuf is not None else []), writes=[(dst, A)])
                                cur, curlen, curbuf = dst[:, 0:nl], nl, dst
                                a_, b_ = b_, a_
                                step *= 2
                            o0 = 8 - w_ // 2
                            dst = a_
                            P.v("dve", "tensor_tensor", dst[:, 0:ntok], cur[:, o0:o0 + ntok], iv[:, g_, 0:ntok], ALU.mult,
                                reads=[(curbuf, A), (iv, g_)], writes=[(dst, A)])
                            P.v("dve", "tensor_tensor", d_[:, g_, 0:ntok], dst[:, 0:ntok], p_[:, g_, 8:8 + ntok],
                                ALU.subtract, reads=[(dst, A), (p_, A)], writes=[(d_, g_)])
                            pp, ppk = bank(g_)
                            P.mm(pp[:, 0:ntok], pwb[:, g_, :], d_[:, g_, 0:ntok], start=True, stop=True,
                                 reads=[(pwb, g_), (d_, g_)], writes=ppk)
                            P.act(o_[:, g_, 0:ntok], pp[:, 0:ntok], AF.Copy, scale=psc[:, g_:g_ + 1],
                                  reads=ppk + [(psc, A)], writes=[(o_, g_)])
                        P.dma("pool", aoT_d[4:8, :, t0:t0 + ntok].rearrange("g p t -> p g t"), o_[:, :, 0:ntok],
                              reads=[(o_, g_) for g_ in range(4)])
                P.barrier()
                if stop_after == "B":
                    return finish_build()

                with contextlib.ExitStack() as st:
                    stg = [kb.sb(st, f"cstg{i}", [128, 768], F32) for i in range(2)]
                    gq0 = kb.sb(st, "gq0", [128, 3], F32)
                    gkv0 = kb.sb(st, "gkv0", [128, 2], F32)
                    P.dma("sp", gq0[:], gq0_d, writes=[(gq0, A)])
                    P.dma("sp", gkv0[:], gkv0_d, writes=[(gkv0, A)])
                    wuq = kb.sb(st, "wuq", [128, 3, 768], BF16)
                    wuqs = kb.sb(st, "wuqs", [128, 3, 768], BF16)
                    wuk = kb.sb(st, "wuk", [128, 2, 512], BF16)
                    wuv = kb.sb(st, "wuv", [128, 2, 512], BF16)
                    for c in range(3):
                        wload(stg, wuq[:, c, :], wuq_d[c * 128:(c + 1) * 128, :], 768, wuq, A,
                              scale_ap=gq0[:, c:c + 1], scale_reads=[(gq0, A)])
                        wload(stg, wuqs[:, c, :], wuqs_d[c * 128:(c + 1) * 128, :], 768, wuqs, A,
                              scale_ap=gq0[:, c:c + 1], scale_reads=[(gq0, A)])
                    for c in range(2):
                        wload(stg, wuk[:, c, :], wuk_d[c * 128:(c + 1) * 128, :], 512, wuk, A,
                              scale_ap=gkv0[:, c:c + 1], scale_reads=[(gkv0, A)])
                        wload(stg, wuv[:, c, :], wuv_d[c * 128:(c + 1) * 128, :], 512, wuv, A,
                              scale_ap=gkv0[:, c:c + 1], scale_reads=[(gkv0, A)])
                    Vb = [kb.sb(st, f"V0_{i}", [128, NT, 128], BF16) for i in range(2)]
                    cqb = [kb.sb(st, f"ccq{i}", [128, 3, 512], BF16) for i in range(2)]
                    ctab = [kb.sb(st, f"cct{i}", [96, 2, 512], F32) for i in range(2)]
                    QT = [kb.sb(st, f"cQT{i}", [96, 512], BF16) for i in range(2)]
                    qt1 = kb.sb(st, "cqt1", [96, 512], F32)
                    qt2 = kb.sb(st, "cqt2", [96, 512], F32)
                    PT = [kb.sb(st, f"cPT{i}", [128, 512], BF16) for i in range(4)]
                    rsb = kb.sb(st, "crsb", [128, 512], F32)
                    bcs = kb.sb(st, "cbcs", [128, 512], F32)
                    ao = [kb.sb(st, f"cao{i}", [128, 512], BF16) for i in range(2)]
                    qblocks = [(g * 512, 512, list(range(NT))) for g in range(16)] + [(S, 256, [64, 65])]
                    SC = 96.0 ** -0.5
                    chunks = [(g * 512, 512) for g in range(16)] + [(S, 256)]

                    def build_steps(h):
                        KTh, Vh = KTb[h % 2], Vb[h % 2]
                        even = (h % 2 == 0)
                        voff = 0 if even else 64
                        steps = []

                        def k_chunk(ci):
                            c0, cn = chunks[ci]
                            pp, ppk = bank(6 + ci % 2)
                            for kc in range(2):
                                P.mm(pp[0:64, 0:cn], wuk[:, kc, h * 64:(h + 1) * 64], ckvnT[:, kc, c0:c0 + cn],
                                     start=(kc == 0), stop=(kc == 1), reads=[(wuk, A), (ckvnT, ci)], writes=ppk)
                            P.v("dve", "tensor_copy", KTh[0:64, c0:c0 + cn], pp[0:64, 0:cn], reads=ppk,
                                writes=[(KTh, ("n", ci))])

                        def v_init():
                            if even:
                                P.v("pool", "memset", Vh[:, :, 64:65], 1.0, writes=[(Vh, A)])
                            else:
                                P.v("pool", "memset", Vh[:, :, 0:64], 0.0, writes=[(Vh, A)])
                                P.v("pool", "memset", Vh[:, :, 0:1], 1.0, writes=[(Vh, A)])

                        def v_batch(b0):
                            nb = min(8, NT - b0)
                            pp, ppk = bank(7)
                            pp3 = pp.rearrange("p (i d) -> p i d", d=64)
                            for i in range(nb):
                                t = b0 + i
                                for kc in range(2):
                                    P.mm(pp3[:, i, :], ckvnT[:, kc, t * 128:(t + 1) * 128], wuv[:, kc, h * 64:(h + 1) * 64],
                                         start=(kc == 0), stop=(kc == 1), reads=[(ckvnT, t // 4), (wuv, A)], writes=ppk)
                            P.v("dve", "tensor_copy", Vh[:, b0:b0 + nb, voff:voff + 64], pp3[:, 0:nb, :], reads=ppk,
                                writes=[(Vh, b0 // 8)])

                        steps.append(v_init)
                        for ci in range(len(chunks)):
                            steps.append(lambda ci=ci: k_chunk(ci))
                        for b0 in range(0, NT, 8):
                            steps.append(lambda b0=b0: v_batch(b0))
                        return steps

                    for st_ in build_steps(0):
                        st_()
                    for h in range(8):
                        even = (h % 2 == 0)
                        KTh, V = KTb[h % 2], Vb[h % 2]
                        nxt = build_steps(h + 1) if h + 1 < 8 else []
                        M = 65 if even else 128
                        srow = 64 if even else 0
                        r0 = 0 if even else 64

                        def prep_q(qi):
                            t0, nq, _ = qblocks[qi]
                            cb_, tb_, q_ = cqb[qi % 2], ctab[qi % 2], QT[qi % 2]
                            P.dma("sp", cb_[:, :, 0:nq], cqnT_d[:, :, t0:t0 + nq].rearrange("c p t -> p c t"),
                                  writes=[(cb_, A)])
                            P.dma("sp", tb_[64:96, 0, 0:nq], ropeA_c_d[:, t0:t0 + nq], writes=[(tb_, 0)])
                            P.dma("sp", tb_[64:96, 1, 0:nq], ropeA_s_d[:, t0:t0 + nq], writes=[(tb_, 1)])
                            p1, p1k = bank(6)
                            p2, p2k = bank(7)
                            for kc in range(3):
                                P.mm(p1[0:96, 0:nq], wuq[:, kc, h * 96:(h + 1) * 96], cb_[:, kc, 0:nq], start=(kc == 0),
                                     stop=(kc == 2), reads=[(wuq, A), (cb_, A)], writes=p1k)
                            for kc in range(3):
                                P.mm(p2[0:96, 0:nq], wuqs[:, kc, h * 96:(h + 1) * 96], cb_[:, kc, 0:nq], start=(kc == 0),
                                     stop=(kc == 2), reads=[(wuqs, A), (cb_, A)], writes=p2k)
                            P.v("dve", "tensor_copy", q_[0:64, 0:nq], p1[0:64, 0:nq], reads=p1k, writes=[(q_, "n")])
                            P.v("dve", "tensor_tensor", qt1[64:96, 0:nq], p1[64:96, 0:nq], tb_[64:96, 0, 0:nq], ALU.mult,
                                reads=p1k + [(tb_, 0)], writes=[(qt1, A)])
                            P.v("dve", "tensor_tensor", qt2[64:96, 0:nq], p2[64:96, 0:nq], tb_[64:96, 1, 0:nq], ALU.mult,
                                reads=p2k + [(tb_, 1)], writes=[(qt2, A)])
                            P.v("dve", "tensor_tensor", q_[64:96, 0:nq], qt1[64:96, 0:nq], qt2[64:96, 0:nq], ALU.add,
                                reads=[(qt1, A), (qt2, A)], writes=[(q_, "r")])

                        def finalize(qi):
                            t0, nq, _ = qblocks[qi]
                            po_, pok = bank(4 + qi % 2)
                            P.v("dve", "reciprocal", rsb[srow:srow + 1, 0:nq], po_[srow:srow + 1, 0:nq], reads=pok,
                                writes=[(rsb, A)])
                            pb_, pbk = bank(6)
                            P.mm(pb_[:, 0:nq], ones_f[srow:srow + 1, :], rsb[srow:srow + 1, 0:nq], start=True, stop=True,
                                 reads=[(ones_f, A), (rsb, A)], writes=pbk)
                            P.act(bcs[r0:r0 + 64, 0:nq], pb_[r0:r0 + 64, 0:nq], AF.Copy, reads=pbk, writes=[(bcs, A)])
                            a_ = ao[qi % 2]
                            P.v("dve", "tensor_tensor", a_[r0:r0 + 64, 0:nq], po_[r0:r0 + 64, 0:nq], bcs[r0:r0 + 64, 0:nq],
                                ALU.mult, reads=pok + [(bcs, A)], writes=[(a_, A)])
                            P.dma("pool", aoT_d[h // 2, r0:r0 + 64, t0:t0 + nq], a_[r0:r0 + 64, 0:nq], reads=[(a_, A)])

                        prep_q(0)
                        pending = None
                        for qi, (t0, nq, kts) in enumerate(qblocks):
                            q_ = QT[qi % 2]
                            po_, pok = bank(4 + qi % 2)
                            nk = len(kts)
                            npair = (nk + 1) // 2

                            SBK = [0, 1, 2, 3]

                            def qk(jj):
                                kt = kts[jj]
                                ps_, psk = bank(SBK[jj % 4])
                                ci = kt // 4
                                P.mm(ps_[:, 0:nq], KTh[0:96, kt * 128:(kt + 1) * 128], q_[0:96, 0:nq], start=True, stop=True,
                                     reads=[(KTh, ("n", ci)), (KTh, ("r", ci)), (q_, "n"), (q_, "r")], writes=psk)

                            def ex_pv(jj):
                                kt = kts[jj]
                                ps_, psk = bank(SBK[jj % 4])
                                pt_ = PT[jj % 4]
                                P.act(pt_[:, 0:nq], ps_[:, 0:nq], AF.Exp, scale=SC, reads=psk, writes=[(pt_, A)])
                                P.mm(po_[0:M, 0:nq], V[:, kt, 0:M], pt_[:, 0:nq], start=(jj == 0), stop=(jj == nk - 1),
                                     reads=[(V, kt // 8), (pt_, A)], writes=pok)

                            for jj in range(min(3, nk)):
                                qk(jj)
                            for jj in range(nk):
                                if jj + 3 < nk:
                                    qk(jj + 3)
                                ex_pv(jj)
                                if jj == 3 and pending is not None:
                                    finalize(pending)
                                    pending = None
                                if jj == min(8, nk - 1) and qi + 1 < len(qblocks):
                                    prep_q(qi + 1)
                                if jj in (20, 40) and nxt:
                                    nxt.pop(0)()
                            if pending is not None:
                                finalize(pending)
                            pending = qi
                        finalize(pending)
                        while nxt:
                            nxt.pop(0)()
                P.barrier()
            if stop_after == "C":
                return finish_build()

            with contextlib.ExitStack() as st:
                stg = [kb.sb(st, f"dstg{i}", [128, D], F32) for i in range(2)]
                wo = kb.sb(st, "wout0", [128, 8, D], BF16)
                for k in range(8):
                    wload(stg, wo[:, k, :], wout0_d[k * 128:(k + 1) * 128, :], D, wo, k)
                WO_R = [(wo, k) for k in range(8)]
                aog = [kb.sb(st, f"daog{i}", [128, 8, 512], BF16) for i in range(2)]
                xg = [kb.sb(st, f"dxg{i}", [128, 4, D], F32) for i in range(2)]
                tmp = [kb.sb(st, f"dtmp{i}", [128, D], F32) for i in range(2)]
                xo = [kb.sb(st, f"dxo{i}", [128, D], F32) for i in range(2)]
                for gi, (t0, n, which) in enumerate(groups):
                    ntok = n * 128
                    ab, xb = aog[gi % 2], xg[gi % 2]
                    P.dma("sp", ab[:, :, 0:ntok], aoT_d[:, :, t0:t0 + ntok].rearrange("c p t -> p c t"), writes=[(ab, A)])
                    P.dma("sp", xb[:, 0:n, :], src_rows(t0, ntok).rearrange("(j p) d -> p j d", p=128), writes=[(xb, A)])
                    for j in range(n):
                        pv, pk = bank((j % 2) * 2 + (4 if (j // 2) % 2 else 0), 2)
                        for hf in range(2):
                            for c in range(8):
                                P.mm(pv[:, hf * 512:(hf + 1) * 512], ab[:, c, j * 128:(j + 1) * 128],
                                     wo[:, c, hf * 512:(hf + 1) * 512], start=(c == 0), stop=(c == 7),
                                     reads=[(ab, A)] + WO_R, writes=[pk[hf]])
                        t_, o_ = tmp[j % 2], xo[j % 2]
                        P.v("dve", "tensor_tensor", t_[:], pv, G["gates"][:, which, 0, :], ALU.mult,
                            reads=pk + [(G["gates"], (which, 0))], writes=[(t_, A)])
                        P.v("pool", "tensor_tensor", o_[:], t_[:], xb[:, j, :], ALU.add, reads=[(t_, A), (xb, A)],
                            writes=[(o_, A)])
                        P.dma("pool", xs1_d[t0 + j * 128:t0 + (j + 1) * 128, :], o_[:], reads=[(o_, A)])
            P.barrier()

            if stop_after == "D":
                return finish_build()
            wins = []
            for wi in range(S // 256):
                s0 = wi * 256
                wins.append(dict(src=xs1_d[s0:s0 + 256, :], res=xs1_d[s0:s0 + 256, :], mid=xmid_d[s0:s0 + 256, :],
                                 which=0,
                                 prev=xs1_d[max(s0 - 1, 0):max(s0 - 1, 0) + 1, :],
                                 next=xs1_d[min(s0 + 256, S - 1):min(s0 + 256, S - 1) + 1, :],
                                 pmask=(0 if wi == 0 else None), nmask=(0 if wi == S // 256 - 1 else None),
                                 dst=xs2_d[128 + s0:128 + s0 + 256, :]))
            wins.append(dict(src=xs1_d[S:S + 256, :], res=xs1_d[S:S + 256, :], mid=xmid_d[S:S + 256, :], which=1,
                             prev=None, next=None, pmask=0, nmask=0, dst=xcs2_d))
            ffn_phase(0, wins, final=False)

        if do1:
          with contextlib.ExitStack() as lay1:
            G["gates"] = kb.sb(lay1, "gates1", [128, 1, 2, D], F32)
            mod_phase(1, False)
            eblocks = [(g * 512, 4) for g in range(8)] + [(4096, 2)]
            with contextlib.ExitStack() as l1:
                KT1 = kb.sb(l1, "KT1", [128, 2, NTOK], BF16)
                V1 = kb.sb(l1, "V1", [128, NT, 256], BF16)
                gqb = kb.sb(l1, "gqb", [128, 128], F32)
                gkb = kb.sb(l1, "gkb", [128, 128], F32)
                P.dma("sp", gqb[:], gq1_d.partition_broadcast(128), writes=[(gqb, A)])
                P.dma("sp", gkb[:], gk1_d.partition_broadcast(128), writes=[(gkb, A)])
                xn = kb.sb(l1, "gxn", [128, 4, D], BF16)
                junk = kb.sb(l1, "gjunk", [128, D], BF16)
                hT = [kb.sb(l1, f"ghT{i}", [128, 8, 512], BF16) for i in range(2)]
                mean = [kb.sb(l1, f"gmean{i}", [128, 4], F32) for i in range(2)]
                rstd = [kb.sb(l1, f"grstd{i}", [128, 4], F32) for i in range(2)]
                hs = kb.sb(l1, "ghs", [128, 8], F32)
                hr = kb.sb(l1, "ghr", [128, 8], F32)
                qn = kb.sb(l1, "gqn", [128, D], F32)
                rt = [kb.sb(l1, f"grt{i}", [128, 8, 64], F32) for i in range(2)]

                def head_norm_rope(src_ap, src_keys, nh, gain, tab, tab_reads, dst_ap, dst_w):
                    w_ = nh * 128
                    import os as _os2
                    _hnr_steps = int(_os2.environ.get("HNR_STEPS", "99"))
                    q3 = qn[:, 0:w_].rearrange("p (h d) -> p h d", d=128)
                    if _hnr_steps >= 1:
                        P.act(qn[:, 0:w_], src_ap, AF.Square, scale=128.0 ** -0.5, reads=src_keys, writes=[(qn, A)])
                    if _hnr_steps >= 2:
                        P.v("dve", "tensor_reduce", hs[:, 0:nh], q3, AX.X, ALU.add, reads=[(qn, A)], writes=[(hs, A)])
                    if _hnr_steps >= 3:
                        P.act(hr[:, 0:nh], hs[:, 0:nh], AF.Ln, bias=EPS, reads=[(hs, A)], writes=[(hr, A)])
                    if _hnr_steps >= 4:
                        P.act(hr[:, 0:nh], hr[:, 0:nh], AF.Exp, scale=-0.5, reads=[(hr, A)], writes=[(hr, A)])
                    if _hnr_steps >= 5:
                        for hh in range(nh):
                            P.act(qn[:, hh * 128:(hh + 1) * 128], src_ap[:, hh * 128:(hh + 1) * 128], AF.Copy,
                                  scale=hr[:, hh:hh + 1], reads=src_keys + [(hr, A)], writes=[(qn, A)])
                    if _hnr_steps >= 6:
                        P.v("dve", "tensor_tensor", q3, q3, gain[:].unsqueeze(1).to_broadcast([128, nh, 128]), ALU.mult,
                            reads=[(qn, A), (gain, A)], writes=[(qn, A)])
                    q4 = qn[:, 0:w_].rearrange("p (h i two) -> p h i two", two=2, i=64)
                    cs = tab[:, 0:64].unsqueeze(1).to_broadcast([128, nh, 64])
                    sn = tab[:, 64:128].unsqueeze(1).to_broadcast([128, nh, 64])
                    R_ = [(qn, A)] + list(tab_reads)
                    t0_, t1_ = rt[0][:, 0:nh, :], rt[1][:, 0:nh, :]
                    x0, x1 = q4[:, :, :, 0], q4[:, :, :, 1]
                    if _hnr_steps >= 7:
                        P.v("dve", "tensor_tensor", t0_, x0, cs, ALU.mult, reads=R_, writes=[(rt[0], A)])
                    if _hnr_steps >= 8:
                        P.v("dve", "tensor_tensor", t1_, x1, sn, ALU.mult, reads=R_, writes=[(rt[1], A)])
                    if _hnr_steps >= 9:
                        P.v("dve", "tensor_tensor", t0_, t0_, t1_, ALU.subtract, reads=[(rt[0], A), (rt[1], A)],
                            writes=[(rt[0], A)])
                    if _hnr_steps >= 10:
                        P.v("dve", "tensor_tensor", t1_, x0, sn, ALU.mult, reads=R_, writes=[(rt[1], A)])
                    if _hnr_steps >= 11:
                        P.v("dve", "tensor_tensor", x1, x1, cs, ALU.mult, reads=R_, writes=[(qn, A)])
                    if _hnr_steps >= 12:
                        P.v("dve", "tensor_tensor", x1, x1, t1_, ALU.add, reads=[(qn, A), (rt[1], A)], writes=[(qn, A)])
                    if _hnr_steps >= 13:
                        P.v("dve", "tensor_copy", x0, t0_, reads=[(rt[0], A), (qn, A)], writes=[(qn, A)])
                    if _hnr_steps >= 14:
                        P.v("dve", "tensor_copy", dst_ap, qn[:, 0:w_], reads=[(qn, A)], writes=dst_w)

                with contextlib.ExitStack() as st:
                    stg = [kb.sb(st, f"gstg{i}", [128, 512], F32) for i in range(2)]
                    win = kb.sb(st, "win1kv", [128, 8, 512], BF16)
                    for k in range(8):
                        wload(stg, win[:, k, :], win1_d[k * 128:(k + 1) * 128, 1024:1536], 512, win, k)
                    WIN_R = [(win, k) for k in range(8)]
                    xg = [kb.sb(st, f"exg{i}", [128, 4, D], F32) for i in range(2)]
                    rk = [kb.sb(st, f"erk{i}", [128, 4, 128], F32) for i in range(2)]
                    ktok = kb.sb(st, "ektok", [128, 4, 256], BF16)
                    groups1 = [(g * 512, 4, 0) for g in range(16)] + [(S, 2, 1)]

                    def e_prep(gi):
                        t0, n, which = groups1[gi]
                        xb = xg[gi % 2]
                        src = xs2_d[128 + t0:128 + t0 + n * 128, :] if which == 0 else xcs2_d
                        P.dma("sp", xb[:, 0:n, :], src.rearrange("(j p) d -> p j d", p=128),
                              writes=[(xb, j) for j in range(n)])
                        P.dma("sp", rk[gi % 2][:, 0:n, :],
                              ropeC_k_d[t0:t0 + n * 128, :].rearrange("(j p) c -> p j c", p=128), writes=[(rk[gi % 2], A)])
                        rms_prep(xb, [(j, 128) for j in range(n)], mean[gi % 2], rstd[gi % 2], xn, junk)
                        trans_mod(xn, [(j, 0, 128, j * 128) for j in range(n)], [(0, 0, n * 128)], hT[gi % 2], which, 0, 0)

                    e_prep(0)
                    for gi, (t0, n, which) in enumerate(groups1):
                        h_ = hT[gi % 2]
                        ntok = n * 128
                        for j in range(n):
                            tt = t0 // 128 + j
                            pp, ppk = bank(2 + j % 2)
                            for k in range(8):
                                P.mm(pp[:, 0:512], h_[:, k, j * 128:(j + 1) * 128], win[:, k, :], start=(k == 0),
                                     stop=(k == 7), reads=[(h_, A)] + WIN_R, writes=ppk)
                            head_norm_rope(pp[:, 0:256], ppk, 2, gkb, rk[gi % 2][:, j, :], [(rk[gi % 2], A)],
                                           ktok[:, j, :], [(ktok, j)])
                            P.act(V1[:, tt, :], pp[:, 256:512], AF.Copy, reads=ppk, writes=[(V1, tt)])
                        pv, pk6 = bank_bf(6, 1)
                        pv3 = pv.rearrange("p (c t) -> p c t", c=2)
                        for j in range(n):
                            for c in range(2):
                                P.tr(pv3[:, c, j * 128:(j + 1) * 128], ktok[:, j, c * 128:(c + 1) * 128], ident[:],
                                     reads=[(ktok, j), (ident, A)], writes=pk6)
                        for c in range(2):
                            P.v("dve", "tensor_copy", KT1[:, c, t0:t0 + ntok], pv3[:, c, 0:ntok], reads=pk6,
                                writes=[(KT1, gi)])
                        if gi + 1 < len(groups1):
                            e_prep(gi + 1)
                P.barrier()
                if stop_after == "E":
                    return finish_build()

                with contextlib.ExitStack() as st:
                    win = kb.sb(st, "win1q", [128, 8, D], BF16)
                    for k in range(8):
                        wload([qn], win[:, k, :], win1_d[k * 128:(k + 1) * 128, 0:1024], D, win, k)
                    WIN_R = [(win, k) for k in range(8)]
                    m01 = kb.sb(st, "m01_sb", [128, 2], F32)
                    P.dma("sp", m01[:], m01_d, writes=[(m01, A)])
                    xa = kb.sb(st, "hxa", [128, 4, D], F32)
                    xbb = kb.sb(st, "hxb", [128, D], F32)
                    rq = [kb.sb(st, f"hrq{i}", [128, 4, 128], F32) for i in range(2)]
                    QT1 = [kb.sb(st, f"hQT{i}", [128, 8, 512], BF16) for i in range(2)]
                    qb16 = [kb.sb(st, f"hqb16_{i}", [128, D], BF16) for i in range(2)]
                    PT = [kb.sb(st, f"hPT{i}", [128, 512], BF16) for i in range(4)]
                    rsb = kb.sb(st, "hrsb", [128, 512], F32)
                    ssb = kb.sb(st, "hssb", [128, 512], F32)
                    osb = kb.sb(st, "hosb", [128, 512], F32)
                    qraw = kb.sb(st, "hqraw", [128, D], F32)
                    ao = [kb.sb(st, f"hao{i}", [128, 512], BF16) for i in range(2)]
                    SC1 = 128.0 ** -0.5
                    SB_ = [0, 1, 7]

                    def f_prep_steps(bi):
                        e0, n = eblocks[bi]
                        h_ = hT[bi % 2]
                        q_ = QT1[bi % 2]
                        steps = []

                        def s0():
                            P.dma("sp", xa[:, 0:n, :], xs2_d[e0:e0 + n * 128, :].rearrange("(j p) d -> p j d", p=128),
                                  writes=[(xa, j) for j in range(n)])
                            P.dma("sp", rq[bi % 2][:, 0:n, :],
                                  ropeC_q_d[e0:e0 + n * 128, :].rearrange("(j p) c -> p j c", p=128), writes=[(rq[bi % 2], A)])
                            for j in range(n):
                                P.dma("sp", xbb[:], xs2_d[HALF + e0 + j * 128:HALF + e0 + (j + 1) * 128, :], writes=[(xbb, A)])
                                P.v("dve", "tensor_scalar", xa[:, j, :], xa[:, j, :], m01[:, 0:1], None, ALU.mult,
                                    reads=[(xa, j), (m01, A)], writes=[(xa, j)])
                                P.v("dve", "scalar_tensor_tensor", xa[:, j, :], xbb[:], m01[:, 1:2], xa[:, j, :], ALU.mult,
                                    ALU.add, reads=[(xbb, A), (xa, j), (m01, A)], writes=[(xa, j)])
                            P.dma("pool", xE_d[e0:e0 + n * 128, :].rearrange("(j p) d -> p j d", p=128), xa[:, 0:n, :],
                                  reads=[(xa, j) for j in range(n)])
                            rms_prep(xa, [(j, 128) for j in range(n)], mean[bi % 2], rstd[bi % 2], xn, junk)

                        def s1():
                            trans_mod(xn, [(j, 0, 128, j * 128) for j in range(n)], [(0, 0, n * 128)], h_, 0, 0, 0)

                        def tile_a(j):
                            pp, ppk = bank(2, 2)
                            for hf in range(2):
                                for k in range(8):
                                    P.mm(pp[:, hf * 512:(hf + 1) * 512], h_[:, k, j * 128:(j + 1) * 128],
                                         win[:, k, hf * 512:(hf + 1) * 512], start=(k == 0), stop=(k == 7),
                                         reads=[(h_, A)] + WIN_R, writes=[ppk[hf]])
                            P.act(qraw[:], pp, AF.Copy, reads=ppk, writes=[(qraw, A)])
                            head_norm_rope(qraw[:], [(qraw, A)], 8, gqb, rq[bi % 2][:, j, :], [(rq[bi % 2], A)], qb16[j % 2][:],
                                           [(qb16[j % 2], A)])

                        def tile_b(j):
                            pv, pkq = bank_bf(6, 1)
                            pv3 = pv.rearrange("p (h t) -> p h t", h=8)
                            for hh in range(8):
                                P.tr(pv3[:, hh, :], qb16[j % 2][:, hh * 128:(hh + 1) * 128], ident[:],
                                     reads=[(qb16[j % 2], A), (ident, A)], writes=pkq)
                            P.v("dve", "tensor_copy", q_[:, :, j * 128:(j + 1) * 128], pv3, reads=pkq, writes=[(q_, A)])

                        steps.append(s0)
                        steps.append(s1)
                        for j in range(n):
                            steps.append(lambda j=j: tile_a(j))
                            steps.append(lambda j=j: tile_b(j))
                        return steps

                    import os as _os4
                    _L1_STAGE = int(_os4.environ.get("L1_STAGE", "0"))
                    _L1_EVAC = int(_os4.environ.get("L1_EVAC", "1"))
                    for st_ in f_prep_steps(0):
                        st_()
                    for bi, (e0, n) in enumerate(eblocks):
                        nq = n * 128
                        q_ = QT1[bi % 2]
                        nsteps = f_prep_steps(bi + 1) if bi + 1 < len(eblocks) else []
                        for h in range(8):
                            kvh = h // 4
                            po_, pok = bank(4)
                            psm, psmk = bank(5)
                            seen_acc = set()

                            def qk(jj):
                                ps_, psk = bank(SB_[jj % 3])
                                P.mm(ps_[:, 0:nq], KT1[:, kvh, jj * 128:(jj + 1) * 128], q_[:, h, 0:nq], start=True,
                                     stop=True, reads=[(KT1, jj // 4), (q_, A)], writes=psk)

                            def ex_pv(jj):
                                ps_, psk = bank(SB_[jj % 3])
                                pt_ = PT[jj % 4]
                                P.act(pt_[:, 0:nq], ps_[:, 0:nq], AF.Exp, scale=SC1, reads=psk, writes=[(pt_, A)])
                                P.mm(po_[:, 0:nq], V1[:, jj, kvh * 128:(kvh + 1) * 128], pt_[:, 0:nq], start=(jj == 0),
                                     stop=(jj == NT - 1), reads=[(V1, jj), (pt_, A)], writes=pok)
                                P.mm(psm[:, 0:nq], ones_b[:], pt_[:, 0:nq], start=(jj == 0), stop=(jj == NT - 1),
                                     reads=[(ones_b, A), (pt_, A)], writes=psmk)

                            qk(0)
                            qk(1)
                            for jj in range(NT):
                                if jj + 2 < NT:
                                    qk(jj + 2)
                                ex_pv(jj)
                                if _L1_STAGE:
                                    if h == 3 and jj == 8 and nsteps:
                                        nsteps.pop(0)()
                                        nsteps.pop(0)()
                                    elif h >= 4 and jj in (8, 40) and nsteps:
                                        nsteps.pop(0)()
                                elif h == 3 and jj == 8:
                                    while nsteps:
                                        nsteps.pop(0)()
                            a_ = ao[h % 2]
                            if _L1_EVAC:
                                P.act(ssb[:, 0:nq], psm[:, 0:nq], AF.Copy, reads=psmk, writes=[(ssb, A)])
                                P.act(osb[:, 0:nq], po_[:, 0:nq], AF.Copy, reads=pok, writes=[(osb, A)])
                                P.v("dve", "reciprocal", rsb[:, 0:nq], ssb[:, 0:nq], reads=[(ssb, A)], writes=[(rsb, A)])
                                P.v("dve", "tensor_tensor", a_[:, 0:nq], osb[:, 0:nq], rsb[:, 0:nq], ALU.mult,
                                    reads=[(osb, A), (rsb, A)], writes=[(a_, A)])
                            else:
                                P.v("dve", "reciprocal", rsb[:, 0:nq], psm[:, 0:nq], reads=psmk, writes=[(rsb, A)])
                                P.v("dve", "tensor_tensor", a_[:, 0:nq], po_[:, 0:nq], rsb[:, 0:nq], ALU.mult,
                                    reads=pok + [(rsb, A)], writes=[(a_, A)])
                            P.dma("pool", aoT1_d[h, :, e0:e0 + nq], a_[:, 0:nq], reads=[(a_, A)])
                        while nsteps:
                            nsteps.pop(0)()
                P.barrier()
            if stop_after == "F":
                return finish_build()

            with contextlib.ExitStack() as st:
                stg = [kb.sb(st, f"jstg{i}", [128, D], F32) for i in range(2)]
                wo = kb.sb(st, "wout1", [128, 8, D], BF16)
                for k in range(8):
                    wload(stg, wo[:, k, :], wout1_d[k * 128:(k + 1) * 128, :], D, wo, k)
                WO_R = [(wo, k) for k in range(8)]
                aog = [kb.sb(st, f"jaog{i}", [128, 8, 512], BF16) for i in range(2)]
                xg = [kb.sb(st, f"jxg{i}", [128, 4, D], F32) for i in range(2)]
                tmp = [kb.sb(st, f"jtmp{i}", [128, D], F32) for i in range(2)]
                xo = [kb.sb(st, f"jxo{i}", [128, D], F32) for i in range(2)]
                for bi, (e0, n) in enumerate(eblocks):
                    ntok = n * 128
                    ab, xb = aog[bi % 2], xg[bi % 2]
                    P.dma("sp", ab[:, :, 0:ntok], aoT1_d[:, :, e0:e0 + ntok].rearrange("c p t -> p c t"), writes=[(ab, A)])
                    P.dma("sp", xb[:, 0:n, :], xE_d[e0:e0 + ntok, :].rearrange("(j p) d -> p j d", p=128), writes=[(xb, A)])
                    for j in range(n):
                        pv, pk = bank((j % 2) * 2 + (4 if (j // 2) % 2 else 0), 2)
                        for hf in range(2):
                            for c in range(8):
                                P.mm(pv[:, hf * 512:(hf + 1) * 512], ab[:, c, j * 128:(j + 1) * 128],
                                     wo[:, c, hf * 512:(hf + 1) * 512], start=(c == 0), stop=(c == 7),
                                     reads=[(ab, A)] + WO_R, writes=[pk[hf]])
                        t_, o_ = tmp[j % 2], xo[j % 2]
                        P.v("dve", "tensor_tensor", t_[:], pv, G["gates"][:, 0, 0, :], ALU.mult,
                            reads=pk + [(G["gates"], (0, 0))], writes=[(t_, A)])
                        P.v("pool", "tensor_tensor", o_[:], t_[:], xb[:, j, :], ALU.add, reads=[(t_, A), (xb, A)],
                            writes=[(o_, A)])
                        P.dma("pool", xs3_d[e0 + j * 128:e0 + (j + 1) * 128, :], o_[:], reads=[(o_, A)])
            P.barrier()
            if stop_after == "G":
                return finish_build()

            wins = []
            nw = HALF // 256
            for wi in range(nw):
                e0 = 128 + wi * 256
                wins.append(dict(src=xs3_d[e0:e0 + 256, :], res=xs3_d[e0:e0 + 256, :], mid=xmid_d[e0:e0 + 256, :], which=0,
                                 prev=xs3_d[e0 - 1:e0, :], next=xs3_d[e0 + 256:e0 + 257, :],
                                 pmask=("A" if wi == 0 else None), nmask=("B" if wi == nw - 1 else None),
                                 dst=out_d[wi * 256:(wi + 1) * 256, :]))
            ffn_phase(1, wins, final=True)

        return finish_build()


def _rope_tables(n_tokens, rope_dim, grid_w=64, theta=10000.0):
    rows = n_tokens // grid_w
    row = np.repeat(np.arange(rows, dtype=np.float32), grid_w)
    col = np.tile(np.arange(grid_w, dtype=np.float32), rows)
    n_freq = rope_dim // 4
    freq = (np.float32(theta) ** (-np.arange(n_freq, dtype=np.float32) / np.float32(n_freq))).astype(np.float32)
    ang = np.concatenate([row[:, None] * freq, col[:, None] * freq], axis=-1).astype(np.float32)
    return np.cos(ang).astype(np.float32), np.sin(ang).astype(np.float32)


def _invcnt(T):
    t = np.arange(T)
    out = np.zeros((4, T), np.float32)
    for g, w in enumerate((2, 4, 8, 16)):
        lo = np.clip(t - w // 2, 0, T)
        hi = np.clip(t - w // 2 + w, 0, T)
        out[g] = (np.float32(1.0) / (hi - lo).astype(np.float32)).astype(np.float32)
    return out


_CONST = {}


def _consts():
    if _CONST:
        return _CONST
    cA, sA = _rope_tables(S, 32)
    cC, sC = _rope_tables(S, 128)
    tokA = np.zeros((NTOK, 32), np.float32)
    tokA[:S, :16] = cA
    tokA[:S, 16:] = sA
    tokA[S:, :16] = 1.0
    featc = np.ones((32, NTOK), np.float32)
    feats = np.zeros((32, NTOK), np.float32)
    for i in range(16):
        featc[2 * i, :S] = cA[:, i]
        featc[2 * i + 1, :S] = cA[:, i]
        feats[2 * i, :S] = -sA[:, i]
        feats[2 * i + 1, :S] = sA[:, i]
    tokC = np.zeros((NTOK, 128), np.float32)
    tokC[:S, :64] = cC
    tokC[:S, 64:] = sC
    tokC[S:, :64] = 1.0
    _CONST.update(ropeA_tok=tokA, ropeA_c=featc, ropeA_s=feats, ropeC_k=tokC, cC=cC, sC=sC,
                  invcnt=_invcnt(S), invcnt_ctx=_invcnt(CT),
                  ident=np.eye(128, dtype=np.float32).astype(ml_dtypes.bfloat16))
    return _CONST


def host_inputs(inputs, core, mode="full"):
    K = _consts()
    b, h = core // 2, core % 2
    f = lambda a: np.ascontiguousarray(np.asarray(a, dtype=np.float32))
    m = {}
    m["ident"] = K["ident"]
    cc = np.zeros((128, 8, 2), np.float32)
    cc[:, :, 0] = f(inputs["c"])[b].reshape(8, 128).T
    cc[:, :, 1] = f(inputs["c_ctx"]).reshape(8, 128).T
    m["cc"] = cc
    m["w_mod"] = f(inputs["w_mod"])
    m["b_mod"] = f(inputs["b_mod"])
    m["b_modT"] = np.ascontiguousarray(f(inputs["b_mod"]).reshape(2, 48, 128).transpose(0, 2, 1))
    m["ffn_w_up"] = f(inputs["ffn_w_up"])
    m["ffn_w_down"] = f(inputs["ffn_w_down"])
    m["conv_wT"] = np.ascontiguousarray(f(inputs["ffn_conv_w"]).reshape(2, 3, NFC, 128).transpose(0, 3, 1, 2))
    m["conv_bT"] = np.ascontiguousarray(f(inputs["ffn_conv_b"]).reshape(2, NFC, 128).transpose(0, 2, 1))
    if mode in ("full", "l0"):
        m["x"] = f(inputs["x"])[b]
        m["ctx"] = f(inputs["ctx"])[b]
        m["mix0_w_in"] = f(inputs["mix0_w_in"])[0]
        wuq = f(inputs["mla_w_uq"])[0]
        m["w_uq"] = wuq
        sw = wuq.copy()
        for hh in range(8):
            base = hh * 96 + 64
            sw[:, base:base + 32:2] = wuq[:, base + 1:base + 32:2]
            sw[:, base + 1:base + 32:2] = wuq[:, base:base + 32:2]
        m["w_uq_sw"] = sw
        m["g_q0T"] = np.ascontiguousarray(f(inputs["mla_g_q"])[0].reshape(3, 128).T)
        m["g_kv0T"] = np.ascontiguousarray(f(inputs["mla_g_kv"])[0].reshape(2, 128).T)
        m["w_uk"] = f(inputs["mla_w_uk"])[0]
        m["w_uv"] = f(inputs["mla_w_uv"])[0]
        m["pool_w"] = f(inputs["pool_w"])[0]
        m["pool_sT"] = np.ascontiguousarray(f(inputs["pool_scale"])[0].reshape(4, 128).T)
        m["mix0_w_out"] = f(inputs["mix0_w_out"])[0]
        m["ropeA_tok"] = K["ropeA_tok"]
        m["ropeA_c"] = K["ropeA_c"]
        m["ropeA_s"] = K["ropeA_s"]
        m["invcnt"] = K["invcnt"]
        m["invcnt_ctx"] = K["invcnt_ctx"]
    if mode in ("full", "l1"):
        m["gqa_w_in"] = f(inputs["gqa_w_in"])[0]
        m["gqa_g_q"] = f(inputs["gqa_g_q"])[0]
        m["gqa_g_k"] = f(inputs["gqa_g_k"])[0]
        m["gqa_w_out"] = f(inputs["gqa_w_out"])[0]
        m["g_final"] = f(inputs["g_final"])
        m["ropeC_k"] = K["ropeC_k"]
        rq = np.zeros((EN, 128), np.float32)
        rq[:, :64] = 1.0
        tok = (h * 32) * 128 - 128 + np.arange(EN)
        ok = (tok >= 0) & (tok < S)
        rq[ok, :64] = K["cC"][tok[ok]]
        rq[ok, 64:] = K["sC"][tok[ok]]
        m["ropeC_q"] = rq
        m01 = np.zeros((128, 2), np.float32)
        m01[:, 0] = 1.0 - h
        m01[:, 1] = float(h)
        m["m01"] = m01
        mab = np.zeros((128, 2), np.float32)
        mab[:, 0] = float(h == 1)
        mab[:, 1] = float(h == 0)
        m["mAB"] = mab
    return m


_PROG = {}


def kernel(**inputs):
    if "full" not in _PROG:
        _PROG["full"] = build("full")
    kb = _PROG["full"]
    n = 8
    in_maps = [host_inputs(inputs, c, "full") for c in range(n)]
    res = run_bass_kernel_spmd(kb.nc, in_maps, core_ids=list(range(n)))
    B = 4
    out = np.zeros((B, S, D), np.float32)
    for c in range(n):
        b, h = c // 2, c % 2
        out[b, h * HALF:(h + 1) * HALF] = res.results[c]["out"]
    return out
```

```python
import contextlib
import numpy as np
import ml_dtypes
import concourse.bass as bass
import concourse.mybir as mybir
from concourse.bass_utils import run_bass_kernel_spmd

F32 = mybir.dt.float32
BF16 = mybir.dt.bfloat16
AF = mybir.ActivationFunctionType
ALU = mybir.AluOpType
AX = mybir.AxisListType
ALLK = "__all__"

D = 1024
S = 8192
CT = 256
NTOK = S + CT
NT = NTOK // 128
DFF = 2816
NFC = DFF // 128
EPS = 1e-6
HALF = S // 2
ET = HALF // 128 + 2
EN = ET * 128


class Buf:
    def __init__(self, t, name):
        self.t = t
        self.name = name
        self.st = {}

    def __getitem__(self, idx):
        return self.t[idx]


class Op:
    __slots__ = ("eng", "fn", "deps", "is_dma", "pos", "tok", "waits", "signal", "vc", "pre")

    def __init__(self, eng, fn, is_dma):
        self.eng = eng
        self.fn = fn
        self.deps = []
        self.is_dma = is_dma
        self.pos = -1
        self.tok = None
        self.waits = []
        self.signal = False
        self.vc = None
        self.pre = None


class Prog:
    ENGS = ("pe", "act", "dve", "pool", "sp")
    NDMA = 12

    def __init__(self, nc):
        self.nc = nc
        self.ops = []
        self.streams = {e: [] for e in self.ENGS}
        self.dma_count = {e: 0 for e in self.ENGS}
        self.dma_ops = {e: [] for e in self.ENGS}
        self.pending_dma = []

    def _deps(self, op, reads, writes):
        deps = set()

        def conflicts(buf, key):
            st = buf.st
            if key == ALLK:
                return list(st.values())
            out = []
            if key in st:
                out.append(st[key])
            if ALLK in st:
                out.append(st[ALLK])
            return out

        for (buf, key) in reads:
            for s in conflicts(buf, key):
                if s[0] is not None:
                    deps.add(s[0])
        for (buf, key) in writes:
            for s in conflicts(buf, key):
                if s[0] is not None:
                    deps.add(s[0])
                for r in s[1]:
                    deps.add(r)
        deps.discard(op)
        for (buf, key) in reads:
            s = buf.st.setdefault(key, [None, []])
            s[1].append(op)
        for (buf, key) in writes:
            if key == ALLK:
                buf.st.clear()
            buf.st[key] = [op, []]
        return deps

    def add(self, eng, fn, reads=(), writes=(), is_dma=False):
        op = Op(eng, fn, is_dma)
        deps = self._deps(op, list(reads), list(writes))
        if eng == "pe" and not is_dma:
            deps = {d for d in deps if not (d.eng == "pe" and not d.is_dma)}
        op.deps = sorted(deps, key=lambda o: o.pos)
        op.pos = len(self.ops)
        self.ops.append(op)
        self.streams[eng].append(op)
        if is_dma:
            i = self.dma_count[eng]
            self.dma_count[eng] += 1
            op.tok = (("dma", eng, i % self.NDMA), 16 * (i // self.NDMA + 1))
            if i >= self.NDMA:
                op.pre = self.dma_ops[eng][i - self.NDMA]
            self.dma_ops[eng].append(op)
            self.pending_dma.append(op)
        return op

    def barrier(self):
        lasts = [s[-1] for s in self.streams.values() if s]
        deps = lasts + self.pending_dma
        self.pending_dma = []
        for e in self.ENGS:
            op = self.add(e, lambda en: en.nop())
            op.deps = sorted(set(deps) - {op}, key=lambda o: o.pos)

    def dma(self, q, out, in_, reads=(), writes=()):
        return self.add(q, lambda e: e.dma_start(out=out, in_=in_), reads, writes, is_dma=True)

    def mm(self, out, lhsT, rhs, start, stop, reads=(), writes=()):
        return self.add("pe", lambda e: e.matmul(out, lhsT, rhs, start=start, stop=stop), reads, writes)

    def tr(self, out, in_, ident, reads=(), writes=()):
        return self.add("pe", lambda e: e.transpose(out, in_, ident), reads, writes)

    def act(self, out, in_, func, bias=None, scale=None, accum_out=None, reads=(), writes=()):
        kw = {}
        if bias is not None:
            kw["bias"] = bias
        if scale is not None:
            kw["scale"] = scale
        if accum_out is not None:
            kw["accum_out"] = accum_out
        return self.add("act", lambda e: e.activation(out, in_, func, **kw), reads, writes)

    def v(self, eng, name, *args, reads=(), writes=(), **kw):
        return self.add(eng, lambda e: getattr(e, name)(*args, **kw), reads, writes)

    def lower(self):
        known = {e: {} for e in self.ENGS}
        for op in self.ops:
            kn = known[op.eng]
            deps = list(op.deps)
            if op.pre is not None:
                deps.append(op.pre)
            for d in deps:
                key = d.tok[0] if d.is_dma else ("eng", d.eng)
                val = d.tok[1] if d.is_dma else d.pos
                if kn.get(key, -1) >= val:
                    continue
                op.waits.append(d)
                d.signal = True
                for k2, v2 in d.vc.items():
                    if kn.get(k2, -1) < v2:
                        kn[k2] = v2
                kn[key] = max(kn.get(key, -1), val)
            vc = dict(kn)
            if op.is_dma:
                vc[op.tok[0]] = max(vc.get(op.tok[0], -1), op.tok[1])
            else:
                vc[("eng", op.eng)] = op.pos
            op.vc = vc
        cnt = {e: 0 for e in self.ENGS}
        for op in self.ops:
            if op.is_dma:
                continue
            if op.signal:
                cnt[op.eng] += 1
                op.tok = (("eng", op.eng), cnt[op.eng])
        for op in self.ops:
            op.vc = None

    def emit(self, stack):
        nc = self.nc
        self.lower()
        sems = {}

        def getsem(key):
            if key not in sems:
                sems[key] = stack.enter_context(nc.semaphore("s_" + "_".join(str(x) for x in key)))
            return sems[key]

        for op in self.ops:
            if op.is_dma or op.signal:
                getsem(op.tok[0])
        block = stack.enter_context(nc.Block())

        def run_stream(ename, eobj):
            for op in self.streams[ename]:
                for d in op.waits:
                    eobj.wait_ge(getsem(d.tok[0]), d.tok[1])
                ins = op.fn(eobj)
                if op.is_dma:
                    ins.then_inc(getsem(op.tok[0]), 16)
                elif op.signal:
                    ins.then_inc(getsem(op.tok[0]), 1)

        @block.tensor
        def _(e):
            run_stream("pe", e)

        @block.scalar
        def _(e):
            run_stream("act", e)

        @block.vector
        def _(e):
            run_stream("dve", e)

        @block.gpsimd
        def _(e):
            run_stream("pool", e)

        @block.sync
        def _(e):
            run_stream("sp", e)


class KB:
    def __init__(self, mode, dbg=False):
        self.mode = mode
        self.dbg = dbg
        self.nc = bass.Bass("TRN2", target_bir_lowering=False)
        self.P = Prog(self.nc)
        self.din = {}
        self.dout = {}
        self.rr = 0

    def inp(self, name, shape, dt=F32):
        t = self.nc.dram_tensor(name, list(shape), dt, kind="ExternalInput").ap()
        self.din[name] = t
        return t

    def outp(self, name, shape, dt=F32):
        t = self.nc.dram_tensor(name, list(shape), dt, kind="ExternalOutput").ap()
        self.dout[name] = t
        return t

    def scratch(self, name, shape, dt=F32, external=None):
        if external == "in":
            return self.inp(name, shape, dt)
        if external == "out" or self.dbg:
            return self.outp(name, shape, dt)
        return self.nc.dram_tensor(name, list(shape), dt).ap()

    def sb(self, st, name, shape, dt):
        self.rr += 1
        name = f"{name}_{self.rr}"
        return Buf(st.enter_context(self.nc.sbuf_tensor(name, list(shape), dt)), name)


def build(mode="full", dbg=False, stop_after=None):
    kb = KB(mode, dbg)
    nc, P = kb.nc, kb.P
    do0 = mode in ("full", "l0")
    do1 = mode in ("full", "l1")
    A = ALLK

    ident_d = kb.inp("ident", [128, 128], BF16)
    cc_d = kb.inp("cc", [128, 8, 2])
    wmod_d = kb.inp("w_mod", [2, D, 6 * D])
    bmod_d = kb.inp("b_mod", [2, 6 * D])
    bmodT_d = kb.inp("b_modT", [2, 128, 48])
    wup_d = kb.inp("ffn_w_up", [2, D, 2 * DFF])
    wdn_d = kb.inp("ffn_w_down", [2, DFF, D])
    convw_d = kb.inp("conv_wT", [2, 128, 3, NFC])
    convb_d = kb.inp("conv_bT", [2, 128, NFC])
    if do0:
        x_d = kb.inp("x", [S, D])
        ctx_d = kb.inp("ctx", [CT, D])
        win0_d = kb.inp("mix0_w_in", [D, 1184])
        wuq_d = kb.inp("w_uq", [384, 768])
        wuqs_d = kb.inp("w_uq_sw", [384, 768])
        gq0_d = kb.inp("g_q0T", [128, 3])
        gkv0_d = kb.inp("g_kv0T", [128, 2])
        wuk_d = kb.inp("w_uk", [256, 512])
        wuv_d = kb.inp("w_uv", [256, 512])
        poolw_d = kb.inp("pool_w", [4, 128, 128])
        pools_d = kb.inp("pool_sT", [128, 4])
        wout0_d = kb.inp("mix0_w_out", [D, D])
        ropeA_tok_d = kb.inp("ropeA_tok", [NTOK, 32])
        ropeA_c_d = kb.inp("ropeA_c", [32, NTOK])
        ropeA_s_d = kb.inp("ropeA_s", [32, NTOK])
        invc_d = kb.inp("invcnt", [4, S])
        invcc_d = kb.inp("invcnt_ctx", [4, CT])
    if do1:
        win1_d = kb.inp("gqa_w_in", [D, 1536])
        gq1_d = kb.inp("gqa_g_q", [128])
        gk1_d = kb.inp("gqa_g_k", [128])
        wout1_d = kb.inp("gqa_w_out", [D, D])
        gfin_d = kb.inp("g_final", [D])
        ropeC_k_d = kb.inp("ropeC_k", [NTOK, 128])
        ropeC_q_d = kb.inp("ropeC_q", [EN, 128])
        m01_d = kb.inp("m01", [128, 2])
        mAB_d = kb.inp("mAB", [128, 2])
        out_d = kb.outp("out", [HALF, D])

    if do0:
        cqnT_d = kb.scratch("cqnT", [3, 128, NTOK], BF16)
        pT_d = kb.scratch("pT", [4, 128, S + 16])
        pTc_d = kb.scratch("pTc", [4, 128, CT + 16])
        aoT_d = kb.scratch("aoT", [8, 128, NTOK], BF16)
        xs1_d = kb.scratch("xs1", [NTOK, D])
    ext = "out" if mode == "l0" else ("in" if mode == "l1" else None)
    xs2_d = kb.scratch("xs2", [S + 256, D], external=ext)
    xcs2_d = kb.scratch("xcs2", [CT, D], external=ext)
    xmid_d = kb.scratch("xmid", [NTOK, D])
    if do1:
        xs3_d = kb.scratch("xs3", [EN, D])
        xE_d = kb.scratch("xE", [EN, D])
        aoT1_d = kb.scratch("aoT1", [8, 128, EN], BF16)

    with contextlib.ExitStack() as top:
        def finish_build():
            P.barrier()
            P.emit(top)
            return kb

        ident = kb.sb(top, "ident_sb", [128, 128], BF16)
        ones_f = kb.sb(top, "ones_f", [128, 128], F32)
        ones_b = kb.sb(top, "ones_b", [128, 128], BF16)
        cc = kb.sb(top, "cc_sb", [128, 8, 2], F32)
        sc2 = kb.sb(top, "sc2", [128, 8, 2], F32)
        modv = kb.sb(top, "modv", [128, 6, 8, 2], F32)
        G = {}
        psA = Buf(top.enter_context(nc.psum_tensor("psA", [128, 2048], F32)), "psA")
        psB = Buf(top.enter_context(nc.psum_tensor("psB", [128, 2048], F32)), "psB")

        def bank(i, n=1):
            assert (i % 4) + n <= 4
            b = psA if i < 4 else psB
            o = (i % 4) * 512
            return b[:, o:o + 512 * n], [(b, i + q) for q in range(n)]

        def bank_bf(i, n=1):
            ap, keys = bank(i, n)
            return ap.bitcast(BF16), keys

        P.dma("sp", ident[:], ident_d, writes=[(ident, A)])
        P.dma("sp", cc[:], cc_d, writes=[(cc, A)])
        P.v("pool", "memset", ones_f[:], 1.0, writes=[(ones_f, A)])
        P.v("pool", "memset", ones_b[:], 1.0, writes=[(ones_b, A)])
        P.act(sc2[:], cc[:], AF.Silu, reads=[(cc, A)], writes=[(sc2, A)])

        cast_rr = [0]

        def wload(stg_bufs, dst_ap, src_ap, n, dst_buf, dst_key, scale_ap=None, scale_reads=()):
            i = cast_rr[0]
            cast_rr[0] += 1
            stg = stg_bufs[i % len(stg_bufs)]
            P.dma("sp", stg[:, 0:n], src_ap, writes=[(stg, A)])
            if scale_ap is not None:
                P.v("dve", "tensor_scalar", dst_ap, stg[:, 0:n], scale_ap, None, ALU.mult,
                    reads=[(stg, A)] + list(scale_reads), writes=[(dst_buf, dst_key)])
            else:
                eng = "pool" if i % 2 == 0 else "dve"
                P.v(eng, "tensor_copy", dst_ap, stg[:, 0:n], reads=[(stg, A)], writes=[(dst_buf, dst_key)])

        def mod_phase(l, need_ctx_gates):
            with contextlib.ExitStack() as st:
                wm = [kb.sb(st, f"wm{i}", [128, 8, D], F32) for i in range(2)]
                bmT = kb.sb(st, "bmT", [128, 48], F32)
                bmb = [kb.sb(st, f"bmb{i}", [128, D], F32) for i in range(2)]
                scb = kb.sb(st, "scb", [128, 8, 2, 128], F32)
                P.dma("sp", bmT[:], bmodT_d[l], writes=[(bmT, A)])
                for k in range(8):
                    for w in range(2):
                        P.v("dve", "tensor_copy", scb[:, k, w, :], sc2[:, k, w:w + 1].to_broadcast([128, 128]),
                            reads=[(sc2, A)], writes=[(scb, (k, w))])
                for mi, m in enumerate((0, 1, 3, 4, 2, 5)):
                    wb = wm[mi % 2]
                    P.dma("sp", wb[:], wmod_d[l, :, m * D:(m + 1) * D].rearrange("(k p) n -> p k n", p=128),
                          writes=[(wb, A)])
                    if m in (0, 1, 3, 4):
                        pv, pk = bank(mi % 2)
                        pv3 = pv[:, 0:16].rearrange("p (j w) -> p j w", w=2)
                        for j in range(8):
                            for k in range(8):
                                P.mm(pv3[:, j, :], wb[:, k, j * 128:(j + 1) * 128], sc2[:, k, :], start=(k == 0),
                                     stop=(k == 7), reads=[(wb, A), (sc2, A)], writes=pk)
                        P.v("dve", "tensor_tensor", modv[:, m, :, :], pv3,
                            bmT[:, m * 8:(m + 1) * 8].unsqueeze(2).to_broadcast([128, 8, 2]), ALU.add,
                            reads=pk + [(bmT, A)], writes=[(modv, m)])
                        if m in (1, 4):
                            P.v("dve", "tensor_scalar_add", modv[:, m, :, :], modv[:, m, :, :], 1.0,
                                reads=[(modv, m)], writes=[(modv, m)])
                    else:
                        gi = 0 if m == 2 else 1
                        bb = bmb[gi]
                        P.dma("sp", bb[:], bmod_d[l, m * D:(m + 1) * D].partition_broadcast(128), writes=[(bb, A)])
                        for w in range(2 if need_ctx_gates else 1):
                            pv, pk = bank(2 + 2 * w if w == 0 else 4, 2)
                            for hf in range(2):
                                for k in range(8):
                                    P.mm(pv[:, hf * 512:(hf + 1) * 512], scb[:, k, w, :],
                                         wb[:, k, hf * 512:(hf + 1) * 512], start=(k == 0), stop=(k == 7),
                                         reads=[(scb, (k, w)), (wb, A)], writes=[pk[hf]])
                            P.v("dve", "tensor_tensor", G["gates"][:, w, gi, :], pv, bb[:], ALU.add,
                                reads=pk + [(bb, A)], writes=[(G["gates"], (w, gi))])
            P.barrier()

        def rms_prep(xt, tiles, mean, rstd, xn, junk, dim_scale=1.0 / 32.0):
            nt_ = len(tiles)
            for (j, p_) in tiles:
                P.act(junk[0:p_, :], xt[0:p_, j, :], AF.Square, scale=dim_scale, accum_out=mean[0:p_, j:j + 1],
                      reads=[(xt, j)], writes=[(junk, A), (mean, j)])
            jmax = max(j for j, _ in tiles) + 1
            P.act(rstd[:, 0:jmax], mean[:, 0:jmax], AF.Ln, bias=EPS, reads=[(mean, j) for j, _ in tiles],
                  writes=[(rstd, A)])
            P.act(rstd[:, 0:jmax], rstd[:, 0:jmax], AF.Exp, scale=-0.5, reads=[(rstd, A)], writes=[(rstd, A)])
            for (j, p_) in tiles:
                P.act(xn[0:p_, j, :], xt[0:p_, j, :], AF.Copy, scale=rstd[0:p_, j:j + 1],
                      reads=[(xt, j), (rstd, A)], writes=[(xn, j)])

        def trans_mod(xn, trans, evac, hT, which, mset, pbase, halves=(0, 1), act_main=False):
            m_shift, m_scale = (0, 1) if mset == 0 else (3, 4)
            for half in halves:
                pv, pk = bank_bf(pbase, 2)
                pv3 = pv.rearrange("p (k t) -> p k t", k=4)
                for (j, p0, np_, dc) in trans:
                    for kk in range(4):
                        k = half * 4 + kk
                        P.tr(pv3[:, kk, dc:dc + np_], xn[p0:p0 + np_, j, k * 128:(k + 1) * 128],
                             ident[p0:p0 + np_, p0:p0 + np_], reads=[(xn, j), (ident, A)], writes=pk)
                for kk in range(4):
                    k = half * 4 + kk
                    for (sc_, dc_, nc_) in evac:
                        if act_main and nc_ >= 64:
                            P.act(hT[:, k, dc_:dc_ + nc_], pv3[:, kk, sc_:sc_ + nc_], AF.Identity,
                                  bias=modv[:, m_shift, k, which:which + 1], scale=modv[:, m_scale, k, which:which + 1],
                                  reads=pk + [(modv, m_scale), (modv, m_shift)], writes=[(hT, A)])
                        else:
                            P.v("dve", "tensor_scalar", hT[:, k, dc_:dc_ + nc_], pv3[:, kk, sc_:sc_ + nc_],
                                modv[:, m_scale, k, which:which + 1], modv[:, m_shift, k, which:which + 1], ALU.mult,
                                ALU.add, reads=pk + [(modv, m_scale), (modv, m_shift)], writes=[(hT, A)])

        def ffn_phase(l, windows, final=False):
            for half in range(2):
                ffn_pass(l, windows, final, half)

        def ffn_pass(l, windows, final, half):
            NH = NFC // 2
            HW = NH * 128
            last = (half == 1)
            with contextlib.ExitStack() as st:
                stg = [kb.sb(st, f"fstg{i}", [128, HW], F32) for i in range(2)]
                wup = kb.sb(st, "wup", [128, 8, 2 * HW], BF16)
                wdn = kb.sb(st, "wdn", [128, NH, D], BF16)
                cw = kb.sb(st, "cw", [128, 3, NFC], F32)
                cb = kb.sb(st, "cb", [128, NFC], F32)
                P.dma("sp", cw[:], convw_d[l], writes=[(cw, A)])
                P.dma("sp", cb[:], convb_d[l], writes=[(cb, A)])
                for k in range(8):
                    for gu_ in range(2):
                        c0 = gu_ * DFF + half * HW
                        wload(stg, wup[:, k, gu_ * HW:(gu_ + 1) * HW], wup_d[l, k * 128:(k + 1) * 128, c0:c0 + HW], HW,
                              wup, (k, gu_))
                for c in range(NH):
                    r0 = (half * NH + c) * 128
                    wload(stg, wdn[:, c, :], wdn_d[l, r0:r0 + 128, :], D, wdn, c)
                WUP_R = [(wup, (k, g2)) for k in range(8) for g2 in range(2)]
                WDN_R = [(wdn, c) for c in range(NH)]
                if final:
                    gfb = kb.sb(st, "gfb", [128, D], F32)
                    P.dma("sp", gfb[:], gfin_d.partition_broadcast(128), writes=[(gfb, A)])
                    mAB = kb.sb(st, "mAB_sb", [128, 2], F32)
                    P.dma("sp", mAB[:], mAB_d, writes=[(mAB, A)])
                xw = kb.sb(st, "fxw", [128, 3, D], F32)
                mean = [kb.sb(st, f"fmean{i}", [128, 4], F32) for i in range(2)]
                rstd = [kb.sb(st, f"frstd{i}", [128, 4], F32) for i in range(2)]
                fm2 = kb.sb(st, "fm2", [128, 4], F32)
                fr2 = kb.sb(st, "fr2", [128, 4], F32)
                xn = kb.sb(st, "fxn", [128, 3, D], BF16)
                junk = kb.sb(st, "fjunk", [128, D], BF16)
                hT = [kb.sb(st, f"fhT{i}", [128, 8, 258], BF16) for i in range(2)]
                acc = [kb.sb(st, f"facc{i}", [128, 256], F32) for i in range(3)]
                sil = [kb.sb(st, f"fsil{i}", [128, 256], F32) for i in range(2)]
                gu = kb.sb(st, "fgu", [128, NH, 256], BF16)
                tmp = [kb.sb(st, f"ftmp{i}", [128, D], F32) for i in range(4)]
                xo = [kb.sb(st, f"fxo{i}", [128, D], F32) for i in range(4)]
                xr = [kb.sb(st, f"fxr{i}", [128, D], F32) for i in range(2)]
                P.v("pool", "memset", xw[:], 0.0, writes=[(xw, A)])
                for mb_ in mean + rstd:
                    P.v("pool", "memset", mb_[:], 1.0, writes=[(mb_, A)])

                def prep_a(wi):
                    w = windows[wi]
                    P.dma("sp", xw[:, 0:2, :], w["src"].rearrange("(j p) d -> p j d", p=128),
                          writes=[(xw, 0), (xw, 1)])
                    tiles = [(0, 128), (1, 128)]
                    if w["prev"] is not None:
                        P.dma("sp", xw[0:1, 2, :], w["prev"], writes=[(xw, 2)])
                        P.dma("sp", xw[32:33, 2, :], w["next"], writes=[(xw, 2)])
                        tiles.append((2, 33))
                    rms_prep(xw, tiles, mean[wi % 2], rstd[wi % 2], xn, junk)

                def prep(wi):
                    prep_a(wi)
                    prep_b(wi)

                def prep_b(wi, halves=(0, 1)):
                    w = windows[wi]
                    h_ = hT[wi % 2]
                    trans = [(0, 0, 128, 0), (1, 0, 128, 128)]
                    evac = [(0, 1, 256)]
                    halo = w["prev"] is not None
                    if halo:
                        trans.append((2, 0, 33, 256))
                        evac += [(256, 0, 1), (288, 257, 1)]
                    trans_mod(xn, trans, evac, h_, w["which"], 1, 0, halves=halves, act_main=True)
                    for (mk, col) in ((w["pmask"], 0), (w["nmask"], 257)):
                        for k in [kk_ + 4 * hv for hv in halves for kk_ in range(4)]:
                            if (not halo) or mk == 0:
                                P.v("dve", "memset", h_[:, k, col:col + 1], 0.0, writes=[(h_, A)])
                            elif mk in ("A", "B"):
                                mi = 0 if mk == "A" else 1
                                P.v("dve", "tensor_scalar", h_[:, k, col:col + 1], h_[:, k, col:col + 1],
                                    mAB[:, mi:mi + 1], None, ALU.mult, reads=[(h_, A), (mAB, A)], writes=[(h_, A)])

                def finish(wi):
                    w = windows[wi]
                    which = w["which"]
                    dst = w["mid"] if half == 0 else w["dst"]
                    for j in range(2):
                        r_ = xr[j]
                        t_, o_ = tmp[(2 * wi + j) % 4], xo[(2 * wi + j) % 4]
                        for hf in range(2):
                            pv, pk = bank(hf)
                            for c in range(NH):
                                P.mm(pv, gu[:, c, j * 128:(j + 1) * 128],
                                     wdn[:, c, hf * 512:(hf + 1) * 512], start=(c == 0), stop=(c == NH - 1),
                                     reads=[(gu, c)] + WDN_R, writes=pk)
                            P.v("dve", "tensor_tensor", t_[:, hf * 512:(hf + 1) * 512], pv,
                                G["gates"][:, which, 1, hf * 512:(hf + 1) * 512], ALU.mult,
                                reads=pk + [(G["gates"], (which, 1))], writes=[(t_, hf)])
                        P.v("pool", "tensor_tensor", o_[:], t_[:], r_[:], ALU.add,
                            reads=[(t_, 0), (t_, 1), (r_, A)], writes=[(o_, A)])
                        if not (final and last):
                            P.dma("pool", dst[j * 128:(j + 1) * 128, :], o_[:], reads=[(o_, A)])

                def finish_b(wi):
                    if not (final and last):
                        return
                    w = windows[wi]
                    dst = w["dst"]
                    for j in range(2):
                        t_, o_ = tmp[(2 * wi + j) % 4], xo[(2 * wi + j) % 4]
                        mj = (2 * wi + j) % 4
                        P.act(junk[:], o_[:], AF.Square, scale=1.0 / 32.0, accum_out=fm2[:, mj:mj + 1],
                              reads=[(o_, A)], writes=[(junk, A), (fm2, mj)])
                        P.act(fr2[:, mj:mj + 1], fm2[:, mj:mj + 1], AF.Ln, bias=EPS, reads=[(fm2, mj)], writes=[(fr2, mj)])
                        P.act(fr2[:, mj:mj + 1], fr2[:, mj:mj + 1], AF.Exp, scale=-0.5, reads=[(fr2, mj)],
                              writes=[(fr2, mj)])
                        P.v("dve", "scalar_tensor_tensor", t_[:], o_[:], fr2[:, mj:mj + 1], gfb[:], ALU.mult, ALU.mult,
                            reads=[(o_, A), (fr2, mj), (gfb, A)], writes=[(t_, 0), (t_, 1)])
                        P.dma("pool", dst[j * 128:(j + 1) * 128, :], t_[:], reads=[(t_, 0), (t_, 1)])

                GB = [2, 3, 6]
                UB = [4, 5, 7]

                def st_pe(wi, c):
                    h_ = hT[wi % 2]
                    pg, pgk = bank(GB[c % 3])
                    pu, puk = bank(UB[c % 3])
                    for k in range(8):
                        P.mm(pg[:, 0:258], wup[:, k, c * 128:(c + 1) * 128], h_[:, k, 0:258], start=(k == 0),
                             stop=(k == 7), reads=[(h_, A)] + WUP_R, writes=pgk)
                    for k in range(8):
                        P.mm(pu[:, 0:256], wup[:, k, HW + c * 128:HW + (c + 1) * 128], h_[:, k, 1:257],
                             start=(k == 0), stop=(k == 7), reads=[(h_, A)] + WUP_R, writes=puk)

                def st_id(c):
                    cg = half * NH + c
                    pg, pgk = bank(GB[c % 3])
                    a_ = acc[c % 3]
                    P.act(a_[:], pg[:, 1:257], AF.Identity, bias=cb[:, cg:cg + 1], scale=cw[:, 1, cg:cg + 1],
                          reads=pgk + [(cb, A), (cw, A)], writes=[(a_, A)])

                def st_conv(c, which_tap):
                    cg = half * NH + c
                    pg, pgk = bank(GB[c % 3])
                    a_ = acc[c % 3]
                    if which_tap == 0:
                        P.v("dve", "scalar_tensor_tensor", a_[:], pg[:, 0:256], cw[:, 0, cg:cg + 1], a_[:], ALU.mult,
                            ALU.add, reads=pgk + [(cw, A), (a_, A)], writes=[(a_, A)])
                    else:
                        P.v("dve", "scalar_tensor_tensor", a_[:], pg[:, 2:258], cw[:, 2, cg:cg + 1], a_[:], ALU.mult,
                            ALU.add, reads=pgk + [(cw, A), (a_, A)], writes=[(a_, A)])

                def st_silu(c):
                    a_, s_ = acc[c % 3], sil[c % 2]
                    P.act(s_[:], a_[:], AF.Silu, reads=[(a_, A)], writes=[(s_, A)])

                def st_mult(c):
                    s_ = sil[c % 2]
                    pu, puk = bank(UB[c % 3])
                    P.v("dve", "tensor_tensor", gu[:, c, :], s_[:], pu[:, 0:256], ALU.mult,
                        reads=[(s_, A)] + puk, writes=[(gu, c)])

                import os as _os3
                _PA = int(_os3.environ.get("FFN_PA", "3"))
                _PB = int(_os3.environ.get("FFN_PB", "6"))
                _PB2 = int(_os3.environ.get("FFN_PB2", "9"))
                prep(0)
                for wi, w in enumerate(windows):
                    base = w["res"] if half == 0 else w["mid"]
                    for j in range(2):
                        P.dma("sp", xr[j][:], base[j * 128:(j + 1) * 128, :], writes=[(xr[j], A)])
                    for c in range(NH + 1):
                        if c < NH:
                            st_pe(wi, c)
                            st_id(c)
                            st_conv(c, 0)
                        if c >= 1:
                            st_silu(c - 1)
                            st_mult(c - 1)
                        if c < NH:
                            st_conv(c, 1)
                        if c == 2 and wi >= 1:
                            finish_b(wi - 1)
                        if c == _PA and wi + 1 < len(windows):
                            prep_a(wi + 1)
                        if c == _PB and wi + 1 < len(windows):
                            prep_b(wi + 1, halves=(0,))
                        if c == _PB2 and wi + 1 < len(windows):
                            prep_b(wi + 1, halves=(1,))
                    finish(wi)
                finish_b(len(windows) - 1)
            P.barrier()

        if do0:
          with contextlib.ExitStack() as lay0:
            G["gates"] = kb.sb(lay0, "gates0", [128, 2, 2, D], F32)
            mod_phase(0, True)
            if stop_after == "mod":
                return finish_build()

            with contextlib.ExitStack() as l0:
                ckvnT = kb.sb(l0, "ckvnT", [128, 2, NTOK], BF16)
                KTb = [kb.sb(l0, f"KT{i}", [96, NTOK], BF16) for i in range(2)]

                def src_rows(t0, n):
                    if t0 < S:
                        return x_d[t0:t0 + n, :]
                    return ctx_d[t0 - S:t0 - S + n, :]

                groups = [(g * 512, 4, 0) for g in range(16)] + [(S, 2, 1)]

                with contextlib.ExitStack() as st:
                    zero_f = kb.sb(st, "zero_f", [128, 1024], F32)
                    P.v("pool", "memset", zero_f[:], 0.0, writes=[(zero_f, A)])
                    for g_ in range(4):
                        P.dma("sp", pT_d[g_, :, 0:8], zero_f[:, 0:8], reads=[(zero_f, A)])
                        P.dma("sp", pT_d[g_, :, S + 8:S + 16], zero_f[:, 0:8], reads=[(zero_f, A)])
                        P.dma("sp", pTc_d[g_, :, 0:8], zero_f[:, 0:8], reads=[(zero_f, A)])
                        P.dma("sp", pTc_d[g_, :, CT + 8:CT + 16], zero_f[:, 0:8], reads=[(zero_f, A)])
                    P.dma("sp", xs2_d[0:128, :], zero_f[:], reads=[(zero_f, A)])
                    P.dma("sp", xs2_d[128 + S:256 + S, :], zero_f[:], reads=[(zero_f, A)])
                    stg = [kb.sb(st, f"astg{i}", [128, 1184], F32) for i in range(2)]
                    win = kb.sb(st, "win0", [128, 8, 1184], BF16)
                    for k in range(8):
                        wload(stg, win[:, k, :], win0_d[k * 128:(k + 1) * 128, :], 1184, win, k)
                    WIN_R = [(win, k) for k in range(8)]
                    ropeA = kb.sb(st, "ropeA", [128, NT, 32], F32)
                    for t8 in range(0, NT, 8):
                        n8 = min(8, NT - t8)
                        P.dma("sp", ropeA[:, t8:t8 + n8, :],
                              ropeA_tok_d[t8 * 128:(t8 + n8) * 128, :].rearrange("(t p) c -> p t c", p=128),
                              writes=[(ropeA, A)])
                    xg = [kb.sb(st, f"axg{i}", [128, 4, D], F32) for i in range(2)]
                    mean = [kb.sb(st, f"amean{i}", [128, 4], F32) for i in range(2)]
                    rstd = [kb.sb(st, f"arstd{i}", [128, 4], F32) for i in range(2)]
                    xn = kb.sb(st, "axn", [128, 4, D], BF16)
                    junk = kb.sb(st, "ajunk", [128, D], BF16)
                    hT = [kb.sb(st, f"ahT{i}", [128, 8, 512], BF16) for i in range(2)]
                    m2 = kb.sb(st, "am2", [128, 4, 2], F32)
                    r2 = kb.sb(st, "ar2", [128, 4, 2], F32)
                    cq_tok = kb.sb(st, "acq", [128, 4, 384], BF16)
                    kv_tok = kb.sb(st, "akv", [128, 4, 288], BF16)
                    rtmp = kb.sb(st, "artmp", [128, 8, 16], F32)
                    cqT = [kb.sb(st, f"acqT{i}", [128, 3, 512], BF16) for i in range(2)]
                    pst = [kb.sb(st, f"apst{i}", [128, 4, 512], F32) for i in range(1)]

                    def a_prep(gi):
                        t0, n, which = groups[gi]
                        xb = xg[gi % 2]
                        P.dma("sp", xb[:, 0:n, :], src_rows(t0, n * 128).rearrange("(j p) d -> p j d", p=128),
                              writes=[(xb, j) for j in range(n)])
                        rms_prep(xb, [(j, 128) for j in range(n)], mean[gi % 2], rstd[gi % 2], xn, junk)
                        trans_mod(xn, [(j, 0, 128, j * 128) for j in range(n)], [(0, 0, n * 128)], hT[gi % 2], which, 0, 0)

                    import os as _os
                    _ng = int(_os.environ.get("A_NG", "99"))
                    _parts = int(_os.environ.get("A_PARTS", "15"))
                    a_prep(0)
                    for gi, (t0, n, which) in enumerate(groups):
                        if gi >= _ng:
                            break
                        h_ = hT[gi % 2]
                        ntok = n * 128
                        for j in range(n if (_parts & 1) else 0):
                            tt = t0 // 128 + j
                            pq, pqk = bank(2 if j % 2 == 0 else 4)
                            pk_, pkk = bank(3 if j % 2 == 0 else 5)
                            for k in range(8):
                                P.mm(pq[:, 0:384], h_[:, k, j * 128:(j + 1) * 128], win[:, k, 0:384], start=(k == 0),
                                     stop=(k == 7), reads=[(h_, A)] + WIN_R, writes=pqk)
                            for k in range(8):
                                P.mm(pk_[:, 0:288], h_[:, k, j * 128:(j + 1) * 128], win[:, k, 384:672],
                                     start=(k == 0), stop=(k == 7), reads=[(h_, A)] + WIN_R, writes=pkk)
                            if not (int(_os.environ.get("A_SUB", "3")) & 1):
                                continue
                            P.act(junk[:, 0:384], pq[:, 0:384], AF.Square, scale=384.0 ** -0.5,
                                  accum_out=m2[:, j, 0:1], reads=pqk, writes=[(junk, A), (m2, j)])
                            P.act(junk[:, 0:256], pk_[:, 0:256], AF.Square, scale=1.0 / 16.0,
                                  accum_out=m2[:, j, 1:2], reads=pkk, writes=[(junk, A), (m2, j)])
                            P.act(r2[:, j, :], m2[:, j, :], AF.Ln, bias=EPS, reads=[(m2, j)], writes=[(r2, j)])
                            P.act(r2[:, j, :], r2[:, j, :], AF.Exp, scale=-0.5, reads=[(r2, j)], writes=[(r2, j)])
                            P.act(cq_tok[:, j, :], pq[:, 0:384], AF.Copy, scale=r2[:, j, 0:1],
                                  reads=pqk + [(r2, j)], writes=[(cq_tok, j)])
                            P.act(kv_tok[:, j, 0:256], pk_[:, 0:256], AF.Copy, scale=r2[:, j, 1:2],
                                  reads=pkk + [(r2, j)], writes=[(kv_tok, j)])
                            if not (int(_os.environ.get("A_SUB", "3")) & 2):
                                continue
                            kr32 = rtmp[:, 4:6, :].rearrange("p a i -> p (a i)")
                            P.act(kr32, pk_[:, 256:288], AF.Copy, reads=pkk, writes=[(rtmp, 4)])
                            kr = kr32.rearrange("p (i two) -> p i two", two=2)
                            cs, sn = ropeA[:, tt, 0:16], ropeA[:, tt, 16:32]
                            R_ = [(rtmp, 4), (ropeA, A)]
                            P.v("dve", "tensor_tensor", rtmp[:, 0, :], kr[:, :, 0], cs, ALU.mult, reads=R_, writes=[(rtmp, 0)])
                            P.v("dve", "tensor_tensor", rtmp[:, 1, :], kr[:, :, 1], sn, ALU.mult, reads=R_, writes=[(rtmp, 1)])
                            P.v("dve", "tensor_tensor", rtmp[:, 2, :], kr[:, :, 0], sn, ALU.mult, reads=R_, writes=[(rtmp, 2)])
                            P.v("dve", "tensor_tensor", rtmp[:, 3, :], kr[:, :, 1], cs, ALU.mult, reads=R_, writes=[(rtmp, 3)])
                            ro = rtmp[:, 6:8, :].rearrange("p a i -> p (a i)")
                            ro2 = ro.rearrange("p (i two) -> p i two", two=2)
                            P.v("dve", "tensor_tensor", ro2[:, :, 0], rtmp[:, 0, :], rtmp[:, 1, :], ALU.subtract,
                                reads=[(rtmp, 0), (rtmp, 1)], writes=[(rtmp, 6)])
                            P.v("dve", "tensor_tensor", ro2[:, :, 1], rtmp[:, 2, :], rtmp[:, 3, :], ALU.add,
                                reads=[(rtmp, 2), (rtmp, 3), (rtmp, 6)], writes=[(rtmp, 6)])
                            P.v("dve", "tensor_copy", kv_tok[:, j, 256:288], ro, reads=[(rtmp, 6)], writes=[(kv_tok, j)])
                        pv, pk6 = bank_bf(6, 2)
                        pv3 = pv[:, 0:1536].rearrange("p (c t) -> p c t", c=3)
                        for j in range(n if (_parts & 2) else 0):
                            for c in range(3):
                                P.tr(pv3[:, c, j * 128:(j + 1) * 128], cq_tok[:, j, c * 128:(c + 1) * 128], ident[:],
                                     reads=[(cq_tok, j), (ident, A)], writes=pk6)
                        cqb = cqT[gi % 2]
                        for c in range(3 if (_parts & 2) else 0):
                            P.v("dve", "tensor_copy", cqb[:, c, 0:ntok], pv3[:, c, 0:ntok], reads=pk6, writes=[(cqb, A)])
                        if _parts & 2:
                            P.dma("pool", cqnT_d[:, :, t0:t0 + ntok].rearrange("c p t -> p c t"), cqb[:, :, 0:ntok],
                                  reads=[(cqb, A)])
                        for j in range(n if (_parts & 4) else 0):
                            for c in range(2):
                                P.tr(pv3[:, c, j * 128:(j + 1) * 128], kv_tok[:, j, c * 128:(c + 1) * 128], ident[:],
                                     reads=[(kv_tok, j), (ident, A)], writes=pk6)
                            P.tr(pv3[0:96, 2, j * 128:(j + 1) * 128], kv_tok[:, j, 192:288], ident[:],
                                 reads=[(kv_tok, j), (ident, A)], writes=pk6)
                        for c in range(2 if (_parts & 4) else 0):
                            P.v("dve", "tensor_copy", ckvnT[:, c, t0:t0 + ntok], pv3[:, c, 0:ntok], reads=pk6,
                                writes=[(ckvnT, gi)])
                        if _parts & 4:
                            for KT_ in KTb:
                                P.v("dve", "tensor_copy", KT_[64:96, t0:t0 + ntok], pv3[64:96, 2, 0:ntok], reads=pk6,
                                    writes=[(KT_, ("r", gi))])
                        pb = pst[0]
                        for gq in range(4 if (_parts & 8) else 0):
                            pp, ppk = bank(4 + gq % 2)
                            for k in range(8):
                                P.mm(pp[:, 0:ntok], win[:, k, 672 + gq * 128:672 + (gq + 1) * 128], h_[:, k, 0:ntok],
                                     start=(k == 0), stop=(k == 7), reads=[(h_, A)] + WIN_R, writes=ppk)
                            P.act(pb[:, gq, 0:ntok], pp[:, 0:ntok], AF.Copy, reads=ppk, writes=[(pb, gq)])
                        if not (_parts & 8):
                            pass
                        elif which == 0:
                            P.dma("pool", pT_d[:, :, 8 + t0:8 + t0 + ntok].rearrange("g p t -> p g t"), pb[:, :, 0:ntok],
                                  reads=[(pb, g_) for g_ in range(4)])
                        else:
                            P.dma("pool", pTc_d[:, :, 8:8 + ntok].rearrange("g p t -> p g t"), pb[:, :, 0:ntok],
                                  reads=[(pb, g_) for g_ in range(4)])
                        if gi + 1 < len(groups) and gi + 1 < _ng:
                            a_prep(gi + 1)
                P.barrier()
                if stop_after == "A":
                    return finish_build()

                with contextlib.ExitStack() as st:
                    stg = [kb.sb(st, f"bstg{i}", [128, 128], F32) for i in range(2)]
                    pwb = kb.sb(st, "poolw", [128, 4, 128], BF16)
                    for g_ in range(4):
                        wload(stg, pwb[:, g_, :], poolw_d[g_], 128, pwb, g_)
                    psc = kb.sb(st, "pools", [128, 4], F32)
                    P.dma("sp", psc[:], pools_d, writes=[(psc, A)])
                    pw = [kb.sb(st, f"bpw{i}", [128, 4, 528], F32) for i in range(2)]
                    inv = [kb.sb(st, f"binv{i}", [128, 4, 512], F32) for i in range(2)]
                    t1 = [kb.sb(st, f"bt1_{i}", [128, 528], F32) for i in range(4)]
                    t2 = [kb.sb(st, f"bt2_{i}", [128, 528], F32) for i in range(4)]
                    dT = [kb.sb(st, f"bdT{i}", [128, 4, 512], BF16) for i in range(2)]
                    po = [kb.sb(st, f"bpo{i}", [128, 4, 512], BF16) for i in range(2)]
                    blocks = [(g * 512, 512, 0) for g in range(16)] + [(S, 256, 1)]
                    for bi, (t0, ntok, which) in enumerate(blocks):
                        p_, iv = pw[bi % 2], inv[bi % 2]
                        if which == 0:
                            P.dma("sp", p_[:, :, 0:ntok + 16], pT_d[:, :, t0:t0 + ntok + 16].rearrange("g p t -> p g t"),
                                  writes=[(p_, A)])
                            for g_ in range(4):
                                P.dma("sp", iv[:, g_, 0:ntok], invc_d[g_, t0:t0 + ntok].partition_broadcast(128),
                                      writes=[(iv, g_)])
                        else:
                            P.dma("sp", p_[:, :, 0:ntok + 16], pTc_d[:, :, 0:ntok + 16].rearrange("g p t -> p g t"),
                                  writes=[(p_, A)])
                            for g_ in range(4):
                                P.dma("sp", iv[:, g_, 0:ntok], invcc_d[g_, 0:ntok].partition_broadcast(128),
                                      writes=[(iv, g_)])
                        d_, o_ = dT[bi % 2], po[bi % 2]
                        for g_ in range(4):
                            w_ = 2 << g_
                            eng = "dve" if g_ % 2 == 1 else "pool"
                            cur, curlen, curbuf = p_[:, g_, 0:ntok + 16], ntok + 16, None
                            a_, b_ = t1[g_], t2[g_]
                            step = 1
                            while step < w_:
                                nl = curlen - step
                                dst = a_
                                P.v(eng, "tensor_tensor", dst[:, 0:nl], cur[:, 0:nl], cur[:, step:step + nl], ALU.add,
                                    reads=[(p_, A)] + ([(curbuf, A)] if curbuf is not None else []), writes=[(dst, A)])
                                cur, curlen, curbuf = dst[:, 0:nl], nl, dst
                                a_, b_ = b_, a_
                                step *= 2
                            o0 = 8 - w_ // 2
                            dst = a_
                            P.v("dve", "tensor_tensor", dst[:, 0:ntok], cur[:, o0:o0 + ntok], iv[:, g_, 0:ntok], ALU.mult,
                                reads=[(curbuf, A), (iv, g_)], writes=[(dst, A)])
                            P.v("dve", "tensor_tensor", d_[:, g_, 0:ntok], dst[:, 0:ntok], p_[:, g_, 8:8 + ntok],
                                ALU.subtract, reads=[(dst, A), (p_, A)], writes=[(d_, g_)])
                            pp, ppk = bank(g_)
                            P.mm(pp[:, 0:ntok], pwb[:, g_, :], d_[:, g_, 0:ntok], start=True, stop=True,
                                 reads=[(pwb, g_), (d_, g_)], writes=ppk)
                            P.act(o_[:, g_, 0:ntok], pp[:, 0:ntok], AF.Copy, scale=psc[:, g_:g_ + 1],
                                  reads=ppk + [(psc, A)], writes=[(o_, g_)])
                        P.dma("pool", aoT_d[4:8, :, t0:t0 + ntok].rearrange("g p t -> p g t"), o_[:, :, 0:ntok],
                              reads=[(o_, g_) for g_ in range(4)])
                P.barrier()
                if stop_after == "B":
                    return finish_build()

                with contextlib.ExitStack() as st:
                    stg = [kb.sb(st, f"cstg{i}", [128, 768], F32) for i in range(2)]
                    gq0 = kb.sb(st, "gq0", [128, 3], F32)
                    gkv0 = kb.sb(st, "gkv0", [128, 2], F32)
                    P.dma("sp", gq0[:], gq0_d, writes=[(gq0, A)])
                    P.dma("sp", gkv0[:], gkv0_d, writes=[(gkv0, A)])
                    wuq = kb.sb(st, "wuq", [128, 3, 768], BF16)
                    wuqs = kb.sb(st, "wuqs", [128, 3, 768], BF16)
                    wuk = kb.sb(st, "wuk", [128, 2, 512], BF16)
                    wuv = kb.sb(st, "wuv", [128, 2, 512], BF16)
                    for c in range(3):
                        wload(stg, wuq[:, c, :], wuq_d[c * 128:(c + 1) * 128, :], 768, wuq, A,
                              scale_ap=gq0[:, c:c + 1], scale_reads=[(gq0, A)])
                        wload(stg, wuqs[:, c, :], wuqs_d[c * 128:(c + 1) * 128, :], 768, wuqs, A,
                              scale_ap=gq0[:, c:c + 1], scale_reads=[(gq0, A)])
                    for c in range(2):
                        wload(stg, wuk[:, c, :], wuk_d[c * 128:(c + 1) * 128, :], 512, wuk, A,
                              scale_ap=gkv0[:, c:c + 1], scale_reads=[(gkv0, A)])
                        wload(stg, wuv[:, c, :], wuv_d[c * 128:(c + 1) * 128, :], 512, wuv, A,
                              scale_ap=gkv0[:, c:c + 1], scale_reads=[(gkv0, A)])
                    Vb = [kb.sb(st, f"V0_{i}", [128, NT, 128], BF16) for i in range(2)]
                    cqb = [kb.sb(st, f"ccq{i}", [128, 3, 512], BF16) for i in range(2)]
                    ctab = [kb.sb(st, f"cct{i}", [96, 2, 512], F32) for i in range(2)]
                    QT = [kb.sb(st, f"cQT{i}", [96, 512], BF16) for i in range(2)]
                    qt1 = kb.sb(st, "cqt1", [96, 512], F32)
                    qt2 = kb.sb(st, "cqt2", [96, 512], F32)
                    PT = [kb.sb(st, f"cPT{i}", [128, 512], BF16) for i in range(4)]
                    rsb = kb.sb(st, "crsb", [128, 512], F32)
                    bcs = kb.sb(st, "cbcs", [128, 512], F32)
                    ao = [kb.sb(st, f"cao{i}", [128, 512], BF16) for i in range(2)]
                    qblocks = [(g * 512, 512, list(range(NT))) for g in range(16)] + [(S, 256, [64, 65])]
                    SC = 96.0 ** -0.5
                    chunks = [(g * 512, 512) for g in range(16)] + [(S, 256)]

                    def build_steps(h):
                        KTh, Vh = KTb[h % 2], Vb[h % 2]
                        even = (h % 2 == 0)
                        voff = 0 if even else 64
                        steps = []

                        def k_chunk(ci):
                            c0, cn = chunks[ci]
                            pp, ppk = bank(6 + ci % 2)
                            for kc in range(2):
                                P.mm(pp[0:64, 0:cn], wuk[:, kc, h * 64:(h + 1) * 64], ckvnT[:, kc, c0:c0 + cn],
                                     start=(kc == 0), stop=(kc == 1), reads=[(wuk, A), (ckvnT, ci)], writes=ppk)
                            P.v("dve", "tensor_copy", KTh[0:64, c0:c0 + cn], pp[0:64, 0:cn], reads=ppk,
                                writes=[(KTh, ("n", ci))])

                        def v_init():
                            if even:
                                P.v("pool", "memset", Vh[:, :, 64:65], 1.0, writes=[(Vh, A)])
                            else:
                                P.v("pool", "memset", Vh[:, :, 0:64], 0.0, writes=[(Vh, A)])
                                P.v("pool", "memset", Vh[:, :, 0:1], 1.0, writes=[(Vh, A)])

                        def v_batch(b0):
                            nb = min(8, NT - b0)
                            pp, ppk = bank(7)
                            pp3 = pp.rearrange("p (i d) -> p i d", d=64)
                            for i in range(nb):
                                t = b0 + i
                                for kc in range(2):
                                    P.mm(pp3[:, i, :], ckvnT[:, kc, t * 128:(t + 1) * 128], wuv[:, kc, h * 64:(h + 1) * 64],
                                         start=(kc == 0), stop=(kc == 1), reads=[(ckvnT, t // 4), (wuv, A)], writes=ppk)
                            P.v("dve", "tensor_copy", Vh[:, b0:b0 + nb, voff:voff + 64], pp3[:, 0:nb, :], reads=ppk,
                                writes=[(Vh, b0 // 8)])

                        steps.append(v_init)
                        for ci in range(len(chunks)):
                            steps.append(lambda ci=ci: k_chunk(ci))
                        for b0 in range(0, NT, 8):
                            steps.append(lambda b0=b0: v_batch(b0))
                        return steps

                    for st_ in build_steps(0):
                        st_()
                    for h in range(8):
                        even = (h % 2 == 0)
                        KTh, V = KTb[h % 2], Vb[h % 2]
                        nxt = build_steps(h + 1) if h + 1 < 8 else []
                        M = 65 if even else 128
                        srow = 64 if even else 0
                        r0 = 0 if even else 64

                        def prep_q(qi):
                            t0, nq, _ = qblocks[qi]
                            cb_, tb_, q_ = cqb[qi % 2], ctab[qi % 2], QT[qi % 2]
                            P.dma("sp", cb_[:, :, 0:nq], cqnT_d[:, :, t0:t0 + nq].rearrange("c p t -> p c t"),
                                  writes=[(cb_, A)])
                            P.dma("sp", tb_[64:96, 0, 0:nq], ropeA_c_d[:, t0:t0 + nq], writes=[(tb_, 0)])
                            P.dma("sp", tb_[64:96, 1, 0:nq], ropeA_s_d[:, t0:t0 + nq], writes=[(tb_, 1)])
                            p1, p1k = bank(6)
                            p2, p2k = bank(7)
                            for kc in range(3):
                                P.mm(p1[0:96, 0:nq], wuq[:, kc, h * 96:(h + 1) * 96], cb_[:, kc, 0:nq], start=(kc == 0),
                                     stop=(kc == 2), reads=[(wuq, A), (cb_, A)], writes=p1k)
                            for kc in range(3):
                                P.mm(p2[0:96, 0:nq], wuqs[:, kc, h * 96:(h + 1) * 96], cb_[:, kc, 0:nq], start=(kc == 0),
                                     stop=(kc == 2), reads=[(wuqs, A), (cb_, A)], writes=p2k)
                            P.v("dve", "tensor_copy", q_[0:64, 0:nq], p1[0:64, 0:nq], reads=p1k, writes=[(q_, "n")])
                            P.v("dve", "tensor_tensor", qt1[64:96, 0:nq], p1[64:96, 0:nq], tb_[64:96, 0, 0:nq], ALU.mult,
                                reads=p1k + [(tb_, 0)], writes=[(qt1, A)])
                            P.v("dve", "tensor_tensor", qt2[64:96, 0:nq], p2[64:96, 0:nq], tb_[64:96, 1, 0:nq], ALU.mult,
                                reads=p2k + [(tb_, 1)], writes=[(qt2, A)])
                            P.v("dve", "tensor_tensor", q_[64:96, 0:nq], qt1[64:96, 0:nq], qt2[64:96, 0:nq], ALU.add,
                                reads=[(qt1, A), (qt2, A)], writes=[(q_, "r")])

                        def finalize(qi):
                            t0, nq, _ = qblocks[qi]
                            po_, pok = bank(4 + qi % 2)
                            P.v("dve", "reciprocal", rsb[srow:srow + 1, 0:nq], po_[srow:srow + 1, 0:nq], reads=pok,
                                writes=[(rsb, A)])
                            pb_, pbk = bank(6)
                            P.mm(pb_[:, 0:nq], ones_f[srow:srow + 1, :], rsb[srow:srow + 1, 0:nq], start=True, stop=True,
                                 reads=[(ones_f, A), (rsb, A)], writes=pbk)
                            P.act(bcs[r0:r0 + 64, 0:nq], pb_[r0:r0 + 64, 0:nq], AF.Copy, reads=pbk, writes=[(bcs, A)])
                            a_ = ao[qi % 2]
                            P.v("dve", "tensor_tensor", a_[r0:r0 + 64, 0:nq], po_[r0:r0 + 64, 0:nq], bcs[r0:r0 + 64, 0:nq],
                                ALU.mult, reads=pok + [(bcs, A)], writes=[(a_, A)])
                            P.dma("pool", aoT_d[h // 2, r0:r0 + 64, t0:t0 + nq], a_[r0:r0 + 64, 0:nq], reads=[(a_, A)])

                        prep_q(0)
                        pending = None
                        for qi, (t0, nq, kts) in enumerate(qblocks):
                            q_ = QT[qi % 2]
                            po_, pok = bank(4 + qi % 2)
                            nk = len(kts)
                            npair = (nk + 1) // 2

                            SBK = [0, 1, 2, 3]

                            def qk(jj):
                                kt = kts[jj]
                                ps_, psk = bank(SBK[jj % 4])
                                ci = kt // 4
                                P.mm(ps_[:, 0:nq], KTh[0:96, kt * 128:(kt + 1) * 128], q_[0:96, 0:nq], start=True, stop=True,
                                     reads=[(KTh, ("n", ci)), (KTh, ("r", ci)), (q_, "n"), (q_, "r")], writes=psk)

                            def ex_pv(jj):
                                kt = kts[jj]
                                ps_, psk = bank(SBK[jj % 4])
                                pt_ = PT[jj % 4]
                                P.act(pt_[:, 0:nq], ps_[:, 0:nq], AF.Exp, scale=SC, reads=psk, writes=[(pt_, A)])
                                P.mm(po_[0:M, 0:nq], V[:, kt, 0:M], pt_[:, 0:nq], start=(jj == 0), stop=(jj == nk - 1),
                                     reads=[(V, kt // 8), (pt_, A)], writes=pok)

                            for jj in range(min(3, nk)):
                                qk(jj)
                            for jj in range(nk):
                                if jj + 3 < nk:
                                    qk(jj + 3)
                                ex_pv(jj)
                                if jj == 3 and pending is not None:
                                    finalize(pending)
                                    pending = None
                                if jj == min(8, nk - 1) and qi + 1 < len(qblocks):
                                    prep_q(qi + 1)
                                if jj in (20, 40) and nxt:
                                    nxt.pop(0)()
                            if pending is not None:
                                finalize(pending)
                            pending = qi
                        finalize(pending)
                        while nxt:
                            nxt.pop(0)()
                P.barrier()
            if stop_after == "C":
                return finish_build()

            with contextlib.ExitStack() as st:
                stg = [kb.sb(st, f"dstg{i}", [128, D], F32) for i in range(2)]
                wo = kb.sb(st, "wout0", [128, 8, D], BF16)
                for k in range(8):
                    wload(stg, wo[:, k, :], wout0_d[k * 128:(k + 1) * 128, :], D, wo, k)
                WO_R = [(wo, k) for k in range(8)]
                aog = [kb.sb(st, f"daog{i}", [128, 8, 512], BF16) for i in range(2)]
                xg = [kb.sb(st, f"dxg{i}", [128, 4, D], F32) for i in range(2)]
                tmp = [kb.sb(st, f"dtmp{i}", [128, D], F32) for i in range(2)]
                xo = [kb.sb(st, f"dxo{i}", [128, D], F32) for i in range(2)]
                for gi, (t0, n, which) in enumerate(groups):
                    ntok = n * 128
                    ab, xb = aog[gi % 2], xg[gi % 2]
                    P.dma("sp", ab[:, :, 0:ntok], aoT_d[:, :, t0:t0 + ntok].rearrange("c p t -> p c t"), writes=[(ab, A)])
                    P.dma("sp", xb[:, 0:n, :], src_rows(t0, ntok).rearrange("(j p) d -> p j d", p=128), writes=[(xb, A)])
                    for j in range(n):
                        pv, pk = bank((j % 2) * 2 + (4 if (j // 2) % 2 else 0), 2)
                        for hf in range(2):
                            for c in range(8):
                                P.mm(pv[:, hf * 512:(hf + 1) * 512], ab[:, c, j * 128:(j + 1) * 128],
                                     wo[:, c, hf * 512:(hf + 1) * 512], start=(c == 0), stop=(c == 7),
                                     reads=[(ab, A)] + WO_R, writes=[pk[hf]])
                        t_, o_ = tmp[j % 2], xo[j % 2]
                        P.v("dve", "tensor_tensor", t_[:], pv, G["gates"][:, which, 0, :], ALU.mult,
                            reads=pk + [(G["gates"], (which, 0))], writes=[(t_, A)])
                        P.v("pool", "tensor_tensor", o_[:], t_[:], xb[:, j, :], ALU.add, reads=[(t_, A), (xb, A)],
                            writes=[(o_, A)])
                        P.dma("pool", xs1_d[t0 + j * 128:t0 + (j + 1) * 128, :], o_[:], reads=[(o_, A)])
            P.barrier()

            if stop_after == "D":
                return finish_build()
            wins = []
            for wi in range(S // 256):
                s0 = wi * 256
                wins.append(dict(src=xs1_d[s0:s0 + 256, :], res=xs1_d[s0:s0 + 256, :], mid=xmid_d[s0:s0 + 256, :],
                                 which=0,
                                 prev=xs1_d[max(s0 - 1, 0):max(s0 - 1, 0) + 1, :],
                                 next=xs1_d[min(s0 + 256, S - 1):min(s0 + 256, S - 1) + 1, :],
                                 pmask=(0 if wi == 0 else None), nmask=(0 if wi == S // 256 - 1 else None),
                                 dst=xs2_d[128 + s0:128 + s0 + 256, :]))
            wins.append(dict(src=xs1_d[S:S + 256, :], res=xs1_d[S:S + 256, :], mid=xmid_d[S:S + 256, :], which=1,
                             prev=None, next=None, pmask=0, nmask=0, dst=xcs2_d))
            ffn_phase(0, wins, final=False)

        if do1:
          with contextlib.ExitStack() as lay1:
            G["gates"] = kb.sb(lay1, "gates1", [128, 1, 2, D], F32)
            mod_phase(1, False)
            eblocks = [(g * 512, 4) for g in range(8)] + [(4096, 2)]
            with contextlib.ExitStack() as l1:
                KT1 = kb.sb(l1, "KT1", [128, 2, NTOK], BF16)
                V1 = kb.sb(l1, "V1", [128, NT, 256], BF16)
                gqb = kb.sb(l1, "gqb", [128, 128], F32)
                gkb = kb.sb(l1, "gkb", [128, 128], F32)
                P.dma("sp", gqb[:], gq1_d.partition_broadcast(128), writes=[(gqb, A)])
                P.dma("sp", gkb[:], gk1_d.partition_broadcast(128), writes=[(gkb, A)])
                xn = kb.sb(l1, "gxn", [128, 4, D], BF16)
                junk = kb.sb(l1, "gjunk", [128, D], BF16)
                hT = [kb.sb(l1, f"ghT{i}", [128, 8, 512], BF16) for i in range(2)]
                mean = [kb.sb(l1, f"gmean{i}", [128, 4], F32) for i in range(2)]
                rstd = [kb.sb(l1, f"grstd{i}", [128, 4], F32) for i in range(2)]
                hs_l = [kb.sb(l1, f"ghs{i}", [128, 8], F32) for i in range(2)]
                hr_l = [kb.sb(l1, f"ghr{i}", [128, 8], F32) for i in range(2)]
                qn_l = [kb.sb(l1, f"gqn{i}", [128, D], F32) for i in range(2)]
                rt_l = [[kb.sb(l1, f"grt{i}_{q}", [128, 8, 64], F32) for i in range(2)] for q in range(2)]
                hnr_cnt = [0]

                def head_norm_rope(src_ap, src_keys, nt, nhpt, gain, tab3, tab_reads, dst_ap, dst_w):
                    nh = nt * nhpt
                    w_ = nh * 128
                    par_ = hnr_cnt[0] % 2
                    hnr_cnt[0] += 1
                    qn, hs, hr, rt = qn_l[par_], hs_l[par_], hr_l[par_], rt_l[par_]
                    q3 = qn[:, 0:w_].rearrange("p (h d) -> p h d", d=128)
                    P.act(qn[:, 0:w_], src_ap, AF.Square, scale=128.0 ** -0.5, reads=src_keys, writes=[(qn, A)])
                    P.v("dve", "tensor_reduce", hs[:, 0:nh], q3, AX.X, ALU.add, reads=[(qn, A)], writes=[(hs, A)])
                    P.act(hr[:, 0:nh], hs[:, 0:nh], AF.Ln, bias=EPS, reads=[(hs, A)], writes=[(hr, A)])
                    P.act(hr[:, 0:nh], hr[:, 0:nh], AF.Exp, scale=-0.5, reads=[(hr, A)], writes=[(hr, A)])
                    for hh in range(nh):
                        P.act(qn[:, hh * 128:(hh + 1) * 128], src_ap[:, hh * 128:(hh + 1) * 128], AF.Copy,
                              scale=hr[:, hh:hh + 1], reads=src_keys + [(hr, A)], writes=[(qn, A)])
                    P.v("dve", "tensor_tensor", q3, q3, gain[:].unsqueeze(1).to_broadcast([128, nh, 128]), ALU.mult,
                        reads=[(qn, A), (gain, A)], writes=[(qn, A)])
                    q5 = qn[:, 0:w_].rearrange("p (t h i two) -> p t h i two", t=nt, h=nhpt, two=2)
                    cs = tab3[:, :, 0:64].unsqueeze(2).to_broadcast([128, nt, nhpt, 64])
                    sn = tab3[:, :, 64:128].unsqueeze(2).to_broadcast([128, nt, nhpt, 64])
                    R_ = [(qn, A)] + list(tab_reads)
                    t0_ = rt[0][:, 0:nh, :].rearrange("p (t h) i -> p t h i", t=nt)
                    t1_ = rt[1][:, 0:nh, :].rearrange("p (t h) i -> p t h i", t=nt)
                    x0, x1 = q5[:, :, :, :, 0], q5[:, :, :, :, 1]
                    P.v("dve", "tensor_tensor", t0_, x0, cs, ALU.mult, reads=R_, writes=[(rt[0], A)])
                    P.v("dve", "tensor_tensor", t1_, x1, sn, ALU.mult, reads=R_, writes=[(rt[1], A)])
                    P.v("dve", "tensor_tensor", t0_, t0_, t1_, ALU.subtract, reads=[(rt[0], A), (rt[1], A)],
                        writes=[(rt[0], A)])
                    P.v("dve", "tensor_tensor", t1_, x0, sn, ALU.mult, reads=R_, writes=[(rt[1], A)])
                    P.v("dve", "tensor_tensor", x1, x1, cs, ALU.mult, reads=R_, writes=[(qn, A)])
                    P.v("dve", "tensor_tensor", x1, x1, t1_, ALU.add, reads=[(qn, A), (rt[1], A)], writes=[(qn, A)])
                    P.v("dve", "tensor_copy", x0, t0_, reads=[(rt[0], A), (qn, A)], writes=[(qn, A)])
                    P.v("dve", "tensor_copy", dst_ap, qn[:, 0:w_], reads=[(qn, A)], writes=dst_w)

                with contextlib.ExitStack() as st:
                    stg = [kb.sb(st, f"gstg{i}", [128, 512], F32) for i in range(2)]
                    win = kb.sb(st, "win1kv", [128, 8, 512], BF16)
                    for k in range(8):
                        wload(stg, win[:, k, :], win1_d[k * 128:(k + 1) * 128, 1024:1536], 512, win, k)
                    WIN_R = [(win, k) for k in range(8)]
                    xg = [kb.sb(st, f"exg{i}", [128, 4, D], F32) for i in range(2)]
                    rk = [kb.sb(st, f"erk{i}", [128, 4, 128], F32) for i in range(2)]
                    ktok = kb.sb(st, "ektok", [128, 4, 256], BF16)
                    kraw = kb.sb(st, "ekraw", [128, D], F32)
                    groups1 = [(g * 512, 4, 0) for g in range(16)] + [(S, 2, 1)]

                    def e_prep(gi):
                        t0, n, which = groups1[gi]
                        xb = xg[gi % 2]
                        src = xs2_d[128 + t0:128 + t0 + n * 128, :] if which == 0 else xcs2_d
                        P.dma("sp", xb[:, 0:n, :], src.rearrange("(j p) d -> p j d", p=128),
                              writes=[(xb, j) for j in range(n)])
                        P.dma("sp", rk[gi % 2][:, 0:n, :],
                              ropeC_k_d[t0:t0 + n * 128, :].rearrange("(j p) c -> p j c", p=128), writes=[(rk[gi % 2], A)])
                        rms_prep(xb, [(j, 128) for j in range(n)], mean[gi % 2], rstd[gi % 2], xn, junk)
                        trans_mod(xn, [(j, 0, 128, j * 128) for j in range(n)], [(0, 0, n * 128)], hT[gi % 2], which, 0, 0)

                    e_prep(0)
                    for gi, (t0, n, which) in enumerate(groups1):
                        h_ = hT[gi % 2]
                        ntok = n * 128
                        for j in range(n):
                            tt = t0 // 128 + j
                            pp, ppk = bank(2 + j % 2)
                            for k in range(8):
                                P.mm(pp[:, 0:512], h_[:, k, j * 128:(j + 1) * 128], win[:, k, :], start=(k == 0),
                                     stop=(k == 7), reads=[(h_, A)] + WIN_R, writes=ppk)
                            P.act(kraw[:, j * 256:(j + 1) * 256], pp[:, 0:256], AF.Copy, reads=ppk, writes=[(kraw, j)])
                            P.act(V1[:, tt, :], pp[:, 256:512], AF.Copy, reads=ppk, writes=[(V1, tt)])
                        head_norm_rope(kraw[:, 0:n * 256], [(kraw, j) for j in range(n)], n, 2, gkb, rk[gi % 2][:, 0:n, :],
                                       [(rk[gi % 2], A)], ktok[:, 0:n, :].rearrange("p j c -> p (j c)"),
                                       [(ktok, j) for j in range(n)])
                        pv, pk6 = bank_bf(6, 1)
                        pv3 = pv.rearrange("p (c t) -> p c t", c=2)
                        for j in range(n):
                            for c in range(2):
                                P.tr(pv3[:, c, j * 128:(j + 1) * 128], ktok[:, j, c * 128:(c + 1) * 128], ident[:],
                                     reads=[(ktok, j), (ident, A)], writes=pk6)
                        for c in range(2):
                            P.v("dve", "tensor_copy", KT1[:, c, t0:t0 + ntok], pv3[:, c, 0:ntok], reads=pk6,
                                writes=[(KT1, gi)])
                        if gi + 1 < len(groups1):
                            e_prep(gi + 1)
                P.barrier()
                if stop_after == "E":
                    return finish_build()

                with contextlib.ExitStack() as st:
                    win = kb.sb(st, "win1q", [128, 8, D], BF16)
                    for k in range(8):
                        wload(qn_l, win[:, k, :], win1_d[k * 128:(k + 1) * 128, 0:1024], D, win, k)
                    WIN_R = [(win, k) for k in range(8)]
                    m01 = kb.sb(st, "m01_sb", [128, 2], F32)
                    P.dma("sp", m01[:], m01_d, writes=[(m01, A)])
                    xa = kb.sb(st, "hxa", [128, 4, D], F32)
                    xbb = kb.sb(st, "hxb", [128, D], F32)
                    rq = [kb.sb(st, f"hrq{i}", [128, 4, 128], F32) for i in range(2)]
                    QT1 = [kb.sb(st, f"hQT{i}", [128, 8, 512], BF16) for i in range(2)]
                    qb16 = [kb.sb(st, f"hqb16_{i}", [128, D], BF16) for i in range(2)]
                    PT = [kb.sb(st, f"hPT{i}", [128, 512], BF16) for i in range(4)]
                    rsb = kb.sb(st, "hrsb", [128, 512], F32)
                    ssb = kb.sb(st, "hssb", [128, 512], F32)
                    osb = kb.sb(st, "hosb", [128, 512], F32)
                    qraw = kb.sb(st, "hqraw", [128, D], F32)
                    ao = [kb.sb(st, f"hao{i}", [128, 512], BF16) for i in range(2)]
                    SC1 = 128.0 ** -0.5
                    SB_ = [0, 1, 7]

                    def f_prep_steps(bi):
                        e0, n = eblocks[bi]
                        h_ = hT[bi % 2]
                        q_ = QT1[bi % 2]
                        steps = []

                        def s0():
                            P.dma("sp", xa[:, 0:n, :], xs2_d[e0:e0 + n * 128, :].rearrange("(j p) d -> p j d", p=128),
                                  writes=[(xa, j) for j in range(n)])
                            P.dma("sp", rq[bi % 2][:, 0:n, :],
                                  ropeC_q_d[e0:e0 + n * 128, :].rearrange("(j p) c -> p j c", p=128), writes=[(rq[bi % 2], A)])
                            for j in range(n):
                                P.dma("sp", xbb[:], xs2_d[HALF + e0 + j * 128:HALF + e0 + (j + 1) * 128, :], writes=[(xbb, A)])
                                P.v("dve", "tensor_scalar", xa[:, j, :], xa[:, j, :], m01[:, 0:1], None, ALU.mult,
                                    reads=[(xa, j), (m01, A)], writes=[(xa, j)])
                                P.v("dve", "scalar_tensor_tensor", xa[:, j, :], xbb[:], m01[:, 1:2], xa[:, j, :], ALU.mult,
                                    ALU.add, reads=[(xbb, A), (xa, j), (m01, A)], writes=[(xa, j)])
                            P.dma("pool", xE_d[e0:e0 + n * 128, :].rearrange("(j p) d -> p j d", p=128), xa[:, 0:n, :],
                                  reads=[(xa, j) for j in range(n)])
                            rms_prep(xa, [(j, 128) for j in range(n)], mean[bi % 2], rstd[bi % 2], xn, junk)

                        def s1():
                            trans_mod(xn, [(j, 0, 128, j * 128) for j in range(n)], [(0, 0, n * 128)], h_, 0, 0, 0)

                        def tile_a(j):
                            pp, ppk = bank(2, 2)
                            for hf in range(2):
                                for k in range(8):
                                    P.mm(pp[:, hf * 512:(hf + 1) * 512], h_[:, k, j * 128:(j + 1) * 128],
                                         win[:, k, hf * 512:(hf + 1) * 512], start=(k == 0), stop=(k == 7),
                                         reads=[(h_, A)] + WIN_R, writes=[ppk[hf]])
                            P.act(qraw[:], pp, AF.Copy, reads=ppk, writes=[(qraw, A)])
                            head_norm_rope(qraw[:], [(qraw, A)], 1, 8, gqb, rq[bi % 2][:, j:j + 1, :], [(rq[bi % 2], A)],
                                           qb16[j % 2][:], [(qb16[j % 2], A)])

                        def tile_b(j):
                            pv, pkq = bank_bf(6, 1)
                            pv3 = pv.rearrange("p (h t) -> p h t", h=8)
                            for hh in range(8):
                                P.tr(pv3[:, hh, :], qb16[j % 2][:, hh * 128:(hh + 1) * 128], ident[:],
                                     reads=[(qb16[j % 2], A), (ident, A)], writes=pkq)
                            P.v("dve", "tensor_copy", q_[:, :, j * 128:(j + 1) * 128], pv3, reads=pkq, writes=[(q_, A)])

                        steps.append(s0)
                        steps.append(s1)
                        for j in range(n):
                            steps.append(lambda j=j: tile_a(j))
                            steps.append(lambda j=j: tile_b(j))
                        return steps

                    import os as _os4
                    _L1_STAGE = int(_os4.environ.get("L1_STAGE", "0"))
                    _L1_EVAC = int(_os4.environ.get("L1_EVAC", "1"))
                    for st_ in f_prep_steps(0):
                        st_()
                    for bi, (e0, n) in enumerate(eblocks):
                        nq = n * 128
                        q_ = QT1[bi % 2]
                        nsteps = f_prep_steps(bi + 1) if bi + 1 < len(eblocks) else []
                        for h in range(8):
                            kvh = h // 4
                            po_, pok = bank(4)
                            psm, psmk = bank(5)
                            seen_acc = set()

                            def qk(jj):
                                ps_, psk = bank(SB_[jj % 3])
                                P.mm(ps_[:, 0:nq], KT1[:, kvh, jj * 128:(jj + 1) * 128], q_[:, h, 0:nq], start=True,
                                     stop=True, reads=[(KT1, jj // 4), (q_, A)], writes=psk)

                            def ex_pv(jj):
                                ps_, psk = bank(SB_[jj % 3])
                                pt_ = PT[jj % 4]
                                P.act(pt_[:, 0:nq], ps_[:, 0:nq], AF.Exp, scale=SC1, reads=psk, writes=[(pt_, A)])
                                P.mm(po_[:, 0:nq], V1[:, jj, kvh * 128:(kvh + 1) * 128], pt_[:, 0:nq], start=(jj == 0),
                                     stop=(jj == NT - 1), reads=[(V1, jj), (pt_, A)], writes=pok)
                                P.mm(psm[:, 0:nq], ones_b[:], pt_[:, 0:nq], start=(jj == 0), stop=(jj == NT - 1),
                                     reads=[(ones_b, A), (pt_, A)], writes=psmk)

                            qk(0)
                            qk(1)
                            for jj in range(NT):
                                if jj + 2 < NT:
                                    qk(jj + 2)
                                ex_pv(jj)
                                if _L1_STAGE:
                                    if h == 3 and jj == 8 and nsteps:
                                        nsteps.pop(0)()
                                        nsteps.pop(0)()
                                    elif h >= 4 and jj in (8, 40) and nsteps:
                                        nsteps.pop(0)()
                                elif h == 3 and jj == 8:
                                    while nsteps:
                                        nsteps.pop(0)()
                            a_ = ao[h % 2]
                            if _L1_EVAC:
                                P.act(ssb[:, 0:nq], psm[:, 0:nq], AF.Copy, reads=psmk, writes=[(ssb, A)])
                                P.act(osb[:, 0:nq], po_[:, 0:nq], AF.Copy, reads=pok, writes=[(osb, A)])
                                P.v("dve", "reciprocal", rsb[:, 0:nq], ssb[:, 0:nq], reads=[(ssb, A)], writes=[(rsb, A)])
                                P.v("dve", "tensor_tensor", a_[:, 0:nq], osb[:, 0:nq], rsb[:, 0:nq], ALU.mult,
                                    reads=[(osb, A), (rsb, A)], writes=[(a_, A)])
                            else:
                                P.v("dve", "reciprocal", rsb[:, 0:nq], psm[:, 0:nq], reads=psmk, writes=[(rsb, A)])
                                P.v("dve", "tensor_tensor", a_[:, 0:nq], po_[:, 0:nq], rsb[:, 0:nq], ALU.mult,
                                    reads=pok + [(rsb, A)], writes=[(a_, A)])
                            P.dma("pool", aoT1_d[h, :, e0:e0 + nq], a_[:, 0:nq], reads=[(a_, A)])
                        while nsteps:
                            nsteps.pop(0)()
                P.barrier()
            if stop_after == "F":
                return finish_build()

            with contextlib.ExitStack() as st:
                stg = [kb.sb(st, f"jstg{i}", [128, D], F32) for i in range(2)]
                wo = kb.sb(st, "wout1", [128, 8, D], BF16)
                for k in range(8):
                    wload(stg, wo[:, k, :], wout1_d[k * 128:(k + 1) * 128, :], D, wo, k)
                WO_R = [(wo, k) for k in range(8)]
                aog = [kb.sb(st, f"jaog{i}", [128, 8, 512], BF16) for i in range(2)]
                xg = [kb.sb(st, f"jxg{i}", [128, 4, D], F32) for i in range(2)]
                tmp = [kb.sb(st, f"jtmp{i}", [128, D], F32) for i in range(2)]
                xo = [kb.sb(st, f"jxo{i}", [128, D], F32) for i in range(2)]
                for bi, (e0, n) in enumerate(eblocks):
                    ntok = n * 128
                    ab, xb = aog[bi % 2], xg[bi % 2]
                    P.dma("sp", ab[:, :, 0:ntok], aoT1_d[:, :, e0:e0 + ntok].rearrange("c p t -> p c t"), writes=[(ab, A)])
                    P.dma("sp", xb[:, 0:n, :], xE_d[e0:e0 + ntok, :].rearrange("(j p) d -> p j d", p=128), writes=[(xb, A)])
                    for j in range(n):
                        pv, pk = bank((j % 2) * 2 + (4 if (j // 2) % 2 else 0), 2)
                        for hf in range(2):
                            for c in range(8):
                                P.mm(pv[:, hf * 512:(hf + 1) * 512], ab[:, c, j * 128:(j + 1) * 128],
                                     wo[:, c, hf * 512:(hf + 1) * 512], start=(c == 0), stop=(c == 7),
                                     reads=[(ab, A)] + WO_R, writes=[pk[hf]])
                        t_, o_ = tmp[j % 2], xo[j % 2]
                        P.v("dve", "tensor_tensor", t_[:], pv, G["gates"][:, 0, 0, :], ALU.mult,
                            reads=pk + [(G["gates"], (0, 0))], writes=[(t_, A)])
                        P.v("pool", "tensor_tensor", o_[:], t_[:], xb[:, j, :], ALU.add, reads=[(t_, A), (xb, A)],
                            writes=[(o_, A)])
                        P.dma("pool", xs3_d[e0 + j * 128:e0 + (j + 1) * 128, :], o_[:], reads=[(o_, A)])
            P.barrier()
            if stop_after == "G":
                return finish_build()

            wins = []
            nw = HALF // 256
            for wi in range(nw):
                e0 = 128 + wi * 256
                wins.append(dict(src=xs3_d[e0:e0 + 256, :], res=xs3_d[e0:e0 + 256, :], mid=xmid_d[e0:e0 + 256, :], which=0,
                                 prev=xs3_d[e0 - 1:e0, :], next=xs3_d[e0 + 256:e0 + 257, :],
                                 pmask=("A" if wi == 0 else None), nmask=("B" if wi == nw - 1 else None),
                                 dst=out_d[wi * 256:(wi + 1) * 256, :]))
            ffn_phase(1, wins, final=True)

        return finish_build()


def _rope_tables(n_tokens, rope_dim, grid_w=64, theta=10000.0):
    rows = n_tokens // grid_w
    row = np.repeat(np.arange(rows, dtype=np.float32), grid_w)
    col = np.tile(np.arange(grid_w, dtype=np.float32), rows)
    n_freq = rope_dim // 4
    freq = (np.float32(theta) ** (-np.arange(n_freq, dtype=np.float32) / np.float32(n_freq))).astype(np.float32)
    ang = np.concatenate([row[:, None] * freq, col[:, None] * freq], axis=-1).astype(np.float32)
    return np.cos(ang).astype(np.float32), np.sin(ang).astype(np.float32)


def _invcnt(T):
    t = np.arange(T)
    out = np.zeros((4, T), np.float32)
    for g, w in enumerate((2, 4, 8, 16)):
        lo = np.clip(t - w // 2, 0, T)
        hi = np.clip(t - w // 2 + w, 0, T)
        out[g] = (np.float32(1.0) / (hi - lo).astype(np.float32)).astype(np.float32)
    return out


_CONST = {}


def _consts():
    if _CONST:
        return _CONST
    cA, sA = _rope_tables(S, 32)
    cC, sC = _rope_tables(S, 128)
    tokA = np.zeros((NTOK, 32), np.float32)
    tokA[:S, :16] = cA
    tokA[:S, 16:] = sA
    tokA[S:, :16] = 1.0
    featc = np.ones((32, NTOK), np.float32)
    feats = np.zeros((32, NTOK), np.float32)
    for i in range(16):
        featc[2 * i, :S] = cA[:, i]
        featc[2 * i + 1, :S] = cA[:, i]
        feats[2 * i, :S] = -sA[:, i]
        feats[2 * i + 1, :S] = sA[:, i]
    tokC = np.zeros((NTOK, 128), np.float32)
    tokC[:S, :64] = cC
    tokC[:S, 64:] = sC
    tokC[S:, :64] = 1.0
    _CONST.update(ropeA_tok=tokA, ropeA_c=featc, ropeA_s=feats, ropeC_k=tokC, cC=cC, sC=sC,
                  invcnt=_invcnt(S), invcnt_ctx=_invcnt(CT),
                  ident=np.eye(128, dtype=np.float32).astype(ml_dtypes.bfloat16))
    return _CONST


def host_inputs(inputs, core, mode="full"):
    K = _consts()
    b, h = core // 2, core % 2
    f = lambda a: np.ascontiguousarray(np.asarray(a, dtype=np.float32))
    m = {}
    m["ident"] = K["ident"]
    cc = np.zeros((128, 8, 2), np.float32)
    cc[:, :, 0] = f(inputs["c"])[b].reshape(8, 128).T
    cc[:, :, 1] = f(inputs["c_ctx"]).reshape(8, 128).T
    m["cc"] = cc
    m["w_mod"] = f(inputs["w_mod"])
    m["b_mod"] = f(inputs["b_mod"])
    m["b_modT"] = np.ascontiguousarray(f(inputs["b_mod"]).reshape(2, 48, 128).transpose(0, 2, 1))
    m["ffn_w_up"] = f(inputs["ffn_w_up"])
    m["ffn_w_down"] = f(inputs["ffn_w_down"])
    m["conv_wT"] = np.ascontiguousarray(f(inputs["ffn_conv_w"]).reshape(2, 3, NFC, 128).transpose(0, 3, 1, 2))
    m["conv_bT"] = np.ascontiguousarray(f(inputs["ffn_conv_b"]).reshape(2, NFC, 128).transpose(0, 2, 1))
    if mode in ("full", "l0"):
        m["x"] = f(inputs["x"])[b]
        m["ctx"] = f(inputs["ctx"])[b]
        m["mix0_w_in"] = f(inputs["mix0_w_in"])[0]
        wuq = f(inputs["mla_w_uq"])[0]
        m["w_uq"] = wuq
        sw = wuq.copy()
        for hh in range(8):
            base = hh * 96 + 64
            sw[:, base:base + 32:2] = wuq[:, base + 1:base + 32:2]
            sw[:, base + 1:base + 32:2] = wuq[:, base:base + 32:2]
        m["w_uq_sw"] = sw
        m["g_q0T"] = np.ascontiguousarray(f(inputs["mla_g_q"])[0].reshape(3, 128).T)
        m["g_kv0T"] = np.ascontiguousarray(f(inputs["mla_g_kv"])[0].reshape(2, 128).T)
        m["w_uk"] = f(inputs["mla_w_uk"])[0]
        m["w_uv"] = f(inputs["mla_w_uv"])[0]
        m["pool_w"] = f(inputs["pool_w"])[0]
        m["pool_sT"] = np.ascontiguousarray(f(inputs["pool_scale"])[0].reshape(4, 128).T)
        m["mix0_w_out"] = f(inputs["mix0_w_out"])[0]
        m["ropeA_tok"] = K["ropeA_tok"]
        m["ropeA_c"] = K["ropeA_c"]
        m["ropeA_s"] = K["ropeA_s"]
        m["invcnt"] = K["invcnt"]
        m["invcnt_ctx"] = K["invcnt_ctx"]
    if mode in ("full", "l1"):
        m["gqa_w_in"] = f(inputs["gqa_w_in"])[0]
        m["gqa_g_q"] = f(inputs["gqa_g_q"])[0]
        m["gqa_g_k"] = f(inputs["gqa_g_k"])[0]
        m["gqa_w_out"] = f(inputs["gqa_w_out"])[0]
        m["g_final"] = f(inputs["g_final"])
        m["ropeC_k"] = K["ropeC_k"]
        rq = np.zeros((EN, 128), np.float32)
        rq[:, :64] = 1.0
        tok = (h * 32) * 128 - 128 + np.arange(EN)
        ok = (tok >= 0) & (tok < S)
        rq[ok, :64] = K["cC"][tok[ok]]
        rq[ok, 64:] = K["sC"][tok[ok]]
        m["ropeC_q"] = rq
        m01 = np.zeros((128, 2), np.float32)
        m01[:, 0] = 1.0 - h
        m01[:, 1] = float(h)
        m["m01"] = m01
        mab = np.zeros((128, 2), np.float32)
        mab[:, 0] = float(h == 1)
        mab[:, 1] = float(h == 0)
        m["mAB"] = mab
    return m


_PROG = {}


def kernel(**inputs):
    if "full" not in _PROG:
        _PROG["full"] = build("full")
    kb = _PROG["full"]
    n = 8
    in_maps = [host_inputs(inputs, c, "full") for c in range(n)]
    res = run_bass_kernel_spmd(kb.nc, in_maps, core_ids=list(range(n)))
    B = 4
    out = np.zeros((B, S, D), np.float32)
    for c in range(n):
        b, h = c // 2, c % 2
        out[b, h * HALF:(h + 1) * HALF] = res.results[c]["out"]
    return out
```
